# Optimizing a Trainium2 kernel written in Bass

```python
import jax, jax.numpy as jnp
from jax import lax
import numpy as np

D_MODEL = 2048
BATCH = 4
SEQ = 4096
DEPTH = 2

GRID_W = 64
CTX_LEN = 256
N_MIXERS = 2
EPS = 1e-6
RWKV_HEAD = 64
RWKV_HEADS = D_MODEL // RWKV_HEAD
RWKV_LORA = max(32, int(round(1.8 * D_MODEL ** 0.5 / 32)) * 32)
RWKV_GN_EPS = 64e-5
N_SHIFT_TARGETS = 6
RET_HEADS = 8
RET_QK_HEAD = D_MODEL // RET_HEADS
RET_V_DIM = 2 * D_MODEL
RET_V_HEAD = RET_V_DIM // RET_HEADS
RET_IN_DIM = 2 * D_MODEL + 2 * RET_V_DIM
RET_CHUNK = 64
ROPE_BASE = 10000.0
N_RWKV_LAYERS = (DEPTH + N_MIXERS - 1) // N_MIXERS
N_RET_LAYERS = DEPTH // N_MIXERS

kernel_name = "hybrid_rwkv7_retention_dit_trunk"

F32 = jnp.float32


def rmsnorm(x, g):
    xf = x.astype(F32)
    y = xf * lax.rsqrt(jnp.mean(xf * xf, axis=-1, keepdims=True) + EPS)
    return (y * g.astype(F32)).astype(x.dtype)


def adaln(cvec, w, b):
    m = jax.nn.silu(cvec) @ w + b
    return jnp.split(m[:, None, :], 3, axis=-1)


def qshift_grid(x):
    B, L, D = x.shape
    rows = L // GRID_W
    g = x.reshape(B, rows, GRID_W, D)
    q = D // 4
    left = jnp.pad(g[:, :, :-1, :q], ((0, 0), (0, 0), (1, 0), (0, 0)))
    right = jnp.pad(g[:, :, 1:, q:2 * q], ((0, 0), (0, 0), (0, 1), (0, 0)))
    up = jnp.pad(g[:, :-1, :, 2 * q:3 * q], ((0, 0), (1, 0), (0, 0), (0, 0)))
    down = jnp.pad(g[:, 1:, :, 3 * q:], ((0, 0), (0, 1), (0, 0), (0, 0)))
    return jnp.concatenate([left, right, up, down], axis=-1).reshape(B, L, D)


def shift_seq(x):
    h = x.shape[-1] // 2
    prev = jnp.pad(x[:, :-1, :h], ((0, 0), (1, 0), (0, 0)))
    nxt = jnp.pad(x[:, 1:, h:], ((0, 0), (0, 1), (0, 0)))
    return jnp.concatenate([prev, nxt], axis=-1)


def rope_2d(x):
    L, d = x.shape[1], x.shape[-1]
    t = jnp.arange(L)
    row = (t // GRID_W).astype(F32)
    col = (t % GRID_W).astype(F32)
    nf = d // 4
    inv = ROPE_BASE ** (-jnp.arange(nf, dtype=F32) / nf)
    ang = jnp.concatenate([row[:, None] * inv, col[:, None] * inv], axis=-1)
    cos = jnp.cos(ang)[None, :, None, :]
    sin = jnp.sin(ang)[None, :, None, :]
    xf = x.astype(F32)
    x1, x2 = xf[..., :d // 2], xf[..., d // 2:]
    return jnp.concatenate([x1 * cos - x2 * sin, x1 * sin + x2 * cos], axis=-1).astype(x.dtype)


def rwkv_heads(t):
    return t.reshape(t.shape[:-1] + (RWKV_HEADS, RWKV_HEAD)).astype(F32)


def rwkv7_project(h, h_shift, mix, w_in, w0, w1, w2, a0, a1, a2, k_k, k_a):
    xm = h[:, :, None, :] + (h_shift - h)[:, :, None, :] * mix
    rkvg = jnp.einsum('bljd,jde->blje', xm[:, :, :4], w_in)
    r, k, v, g = rkvg[:, :, 0], rkvg[:, :, 1], rkvg[:, :, 2], rkvg[:, :, 3]
    xw, xa = xm[:, :, 4], xm[:, :, 5]
    ww = w0[:, None, None, :] + jnp.einsum('nblr,nre->nble', jnp.tanh(jnp.einsum('bld,ndr->nblr', xw, w1)), w2)
    decay = jnp.exp(-jnp.exp(-jax.nn.softplus(-ww.astype(F32)) - 0.5))
    a = jax.nn.sigmoid((a0[:, None, None, :] + jnp.einsum('nblr,nre->nble', jnp.einsum('bld,ndr->nblr', xa, a1), a2)).astype(F32))
    kk = rwkv_heads(k * k_k)
    kk = kk / jnp.maximum(jnp.sqrt(jnp.sum(kk * kk, axis=-1, keepdims=True)), 1e-12)
    a_h = rwkv_heads(a)
    k_dirs = rwkv_heads(k)[None] * (1.0 + (a_h - 1.0) * k_a.reshape(RWKV_HEADS, RWKV_HEAD).astype(F32))
    return rwkv_heads(r), k_dirs, rwkv_heads(v), kk, rwkv_heads(decay), a_h, g


def wkv7_scan(S0, r, decay, k, v, kk, a, reverse):
    def step(S, inp):
        r_t, w_t, k_t, v_t, kk_t, a_t = inp
        sa = jnp.einsum('bhvk,bhk->bhv', S, kk_t)
        S = S * w_t[:, :, None, :] - sa[..., None] * (kk_t * a_t)[:, :, None, :] + v_t[..., None] * k_t[:, :, None, :]
        return S, jnp.einsum('bhvk,bhk->bhv', S, r_t)
    xs = tuple(jnp.swapaxes(t, 0, 1) for t in (r, decay, k, v, kk, a))
    S, o = lax.scan(step, S0, xs, reverse=reverse)
    return S, jnp.swapaxes(o, 0, 1)


def rwkv7_output(o, r, k_dirs, v, g, r_k, ln_g, ln_b, w_out, dtype):
    B, L = o.shape[:2]
    mu = jnp.mean(o, axis=-1, keepdims=True)
    var = jnp.mean(jnp.square(o - mu), axis=-1, keepdims=True)
    on = ((o - mu) * lax.rsqrt(var + RWKV_GN_EPS)).reshape(B, L, D_MODEL) * ln_g + ln_b
    bonus = (jnp.sum(r * k_dirs.sum(0) * r_k.astype(F32), axis=-1, keepdims=True) * v).reshape(B, L, D_MODEL)
    y = (on + bonus) * jax.nn.silu(g.astype(F32))
    return y.astype(dtype) @ w_out


def rwkv7_mixer(h_lat, h_ctx, need_ctx, mix, w_in, w0, w1, w2, a0, a1, a2, k_k, k_a, r_k, ln_g, ln_b, w_out):
    prm = (mix, w_in, w0, w1, w2, a0, a1, a2, k_k, k_a)
    r_l, k_l, v_l, kk_l, w_l, a_l, g_l = rwkv7_project(h_lat, qshift_grid(h_lat), *prm)
    r_c, k_c, v_c, kk_c, w_c, a_c, g_c = rwkv7_project(h_ctx, shift_seq(h_ctx), *prm)
    S0 = jnp.zeros((h_lat.shape[0], RWKV_HEADS, RWKV_HEAD, RWKV_HEAD), F32)
    o_lat, o_ctx = 0.0, 0.0
    for d in range(2):
        S_c, oc = wkv7_scan(S0, r_c, w_c[d], k_c[d], v_c, kk_c, a_c[d], d == 1)
        _, ol = wkv7_scan(S_c, r_l, w_l[d], k_l[d], v_l, kk_l, a_l[d], d == 1)
        o_lat = o_lat + ol
        o_ctx = o_ctx + oc
    y_lat = rwkv7_output(o_lat, r_l, k_l, v_l, g_l, r_k, ln_g, ln_b, w_out, h_lat.dtype)
    y_ctx = rwkv7_output(o_ctx, r_c, k_c, v_c, g_c, r_k, ln_g, ln_b, w_out, h_ctx.dtype) if need_ctx else None
    return y_lat, y_ctx


def retention_project(h, w_in, rotate):
    B, L, _ = h.shape
    qkvg = h @ w_in
    q = qkvg[..., :D_MODEL].reshape(B, L, RET_HEADS, RET_QK_HEAD)
    k = qkvg[..., D_MODEL:2 * D_MODEL].reshape(B, L, RET_HEADS, RET_QK_HEAD)
    v = qkvg[..., 2 * D_MODEL:2 * D_MODEL + RET_V_DIM].reshape(B, L, RET_HEADS, RET_V_HEAD)
    g = qkvg[..., 2 * D_MODEL + RET_V_DIM:]
    if rotate:
        q, k = rope_2d(q), rope_2d(k)
    return q.astype(F32), k.astype(F32) * (RET_QK_HEAD ** -0.5), v.astype(F32), g


def retention_scan(R0, q, k, v, lg, reverse):
    B, L, H, dk = q.shape
    dv = v.shape[-1]
    C = RET_CHUNK
    nC = L // C
    p = jnp.arange(C, dtype=F32)
    if reverse:
        p = C - 1 - p
    diff = p[:, None] - p[None, :]
    mask = (diff > 0) if reverse else (diff >= 0)
    Dm = jnp.where(mask[None], jnp.exp(jnp.where(mask, diff, 0.0)[None] * lg[:, None, None]), 0.0)
    dq = jnp.exp((p + 1)[:, None] * lg[None, :])[None, :, :, None]
    dkey = jnp.exp((C - 1 - p)[:, None] * lg[None, :])[None, :, :, None]
    dchunk = jnp.exp(C * lg)[None, :, None, None]

    def step(R, inp):
        qc, kc, vc = inp
        s = jnp.einsum('bnhd,bmhd->bhnm', qc, kc) * Dm[None]
        o = jnp.einsum('bhnm,bmhe->bnhe', s, vc) + jnp.einsum('bnhd,bhde->bnhe', qc, R) * dq
        R = R * dchunk + jnp.einsum('bmhd,bmhe->bhde', kc * dkey, vc)
        return R, o

    xs = tuple(jnp.swapaxes(t.reshape(B, nC, C, H, t.shape[-1]), 0, 1) for t in (q, k, v))
    R, o = lax.scan(step, R0, xs, reverse=reverse)
    return R, jnp.swapaxes(o, 0, 1).reshape(B, L, H, dv)


def retention_output(o, g, gn_g, w_out, dtype):
    B, L = o.shape[:2]
    on = o * lax.rsqrt(jnp.mean(o * o, axis=-1, keepdims=True) + EPS)
    y = on.reshape(B, L, RET_V_DIM) * gn_g * jax.nn.silu(g.astype(F32))
    return y.astype(dtype) @ w_out


def retention_mixer(h_lat, h_ctx, need_ctx, w_in, decay_logit, gn_g, w_out):
    q_l, k_l, v_l, g_l = retention_project(h_lat, w_in, True)
    q_c, k_c, v_c, g_c = retention_project(h_ctx, w_in, False)
    lg = jax.nn.log_sigmoid(decay_logit.astype(F32))
    R0 = jnp.zeros((h_lat.shape[0], RET_HEADS, RET_QK_HEAD, RET_V_HEAD), F32)
    o_lat, o_ctx = 0.0, 0.0
    for d in range(2):
        R_c, oc = retention_scan(R0, q_c, k_c, v_c, lg[d], d == 1)
        _, ol = retention_scan(R_c, q_l, k_l, v_l, lg[d], d == 1)
        o_lat = o_lat + ol
        o_ctx = o_ctx + oc
    y_lat = retention_output(o_lat, g_l, gn_g, w_out, h_lat.dtype)
    y_ctx = retention_output(o_ctx, g_c, gn_g, w_out, h_ctx.dtype) if need_ctx else None
    return y_lat, y_ctx


def setup_inputs(seed: int = 0) -> dict:
    key = jax.random.key(seed)
    ks = iter(jax.random.split(key, 40))

    def nrm(shape, std):
        return jax.random.normal(next(ks), shape, F32) * std

    D = D_MODEL
    NR, NT = N_RWKV_LAYERS, N_RET_LAYERS
    w0_base = jnp.linspace(-6.5, -1.5, D, dtype=F32)
    ret_logit = jnp.log(2.0 ** (5.0 + jnp.arange(RET_HEADS, dtype=F32)) - 1.0)
    return {
        "x": nrm((BATCH, SEQ, D), 1.0),
        "c": nrm((BATCH, D), 1.0),
        "ctx": nrm((BATCH, CTX_LEN, D), 1.0),
        "c_ctx": nrm((D,), 1.0),
        "ada_w": nrm((DEPTH, D, 3 * D), 0.5 * D ** -0.5),
        "ada_b": nrm((DEPTH, 3 * D), 0.02),
        "norm_g": 1.0 + nrm((DEPTH, D), 0.02),
        "rk_mix": jax.random.uniform(next(ks), (NR, N_SHIFT_TARGETS, D), F32),
        "rk_w_in": nrm((NR, 4, D, D), D ** -0.5),
        "rk_w0": w0_base + nrm((NR, 2, D), 0.1),
        "rk_w1": nrm((NR, 2, D, RWKV_LORA), D ** -0.5),
        "rk_w2": nrm((NR, 2, RWKV_LORA, D), 0.1 * RWKV_LORA ** -0.5),
        "rk_a0": nrm((NR, 2, D), 0.1),
        "rk_a1": nrm((NR, 2, D, RWKV_LORA), D ** -0.5),
        "rk_a2": nrm((NR, 2, RWKV_LORA, D), 0.1 * RWKV_LORA ** -0.5),
        "rk_k_k": 0.85 + nrm((NR, D), 0.02),
        "rk_k_a": 1.0 + nrm((NR, D), 0.02),
        "rk_r_k": nrm((NR, RWKV_HEADS, RWKV_HEAD), 0.1),
        "rk_ln_g": 1.0 + nrm((NR, D), 0.02),
        "rk_ln_b": nrm((NR, D), 0.02),
        "rk_w_out": nrm((NR, D, D), D ** -0.5),
        "rt_w_in": nrm((NT, D, RET_IN_DIM), D ** -0.5),
        "rt_decay_logit": ret_logit + nrm((NT, 2, RET_HEADS), 0.05),
        "rt_gn_g": 1.0 + nrm((NT, RET_V_DIM), 0.02),
        "rt_w_out": nrm((NT, RET_V_DIM, D), RET_V_DIM ** -0.5),
        "final_g": 1.0 + nrm((D,), 0.02),
    }


def reference(x, c, ctx, c_ctx, ada_w, ada_b, norm_g, rk_mix, rk_w_in, rk_w0, rk_w1, rk_w2, rk_a0, rk_a1, rk_a2,
              rk_k_k, rk_k_a, rk_r_k, rk_ln_g, rk_ln_b, rk_w_out, rt_w_in, rt_decay_logit, rt_gn_g, rt_w_out, final_g):
    for i in range(DEPTH):
        last = i == DEPTH - 1
        shift, scale, gate = adaln(c, ada_w[i], ada_b[i])
        s_c, sc_c, g_c = adaln(c_ctx[None], ada_w[i], ada_b[i])
        h_lat = rmsnorm(x, norm_g[i]) * (1.0 + scale) + shift
        h_ctx = rmsnorm(ctx, norm_g[i]) * (1.0 + sc_c) + s_c
        j = i // N_MIXERS
        if i % N_MIXERS == 0:
            y_lat, y_ctx = rwkv7_mixer(h_lat, h_ctx, not last, rk_mix[j], rk_w_in[j], rk_w0[j], rk_w1[j], rk_w2[j],
                                       rk_a0[j], rk_a1[j], rk_a2[j], rk_k_k[j], rk_k_a[j], rk_r_k[j],
                                       rk_ln_g[j], rk_ln_b[j], rk_w_out[j])
        else:
            y_lat, y_ctx = retention_mixer(h_lat, h_ctx, not last, rt_w_in[j], rt_decay_logit[j], rt_gn_g[j], rt_w_out[j])
        x = x + gate * y_lat
        if not last:
            ctx = ctx + g_c * y_ctx
    return rmsnorm(x, final_g)
```

```python
import math
from contextlib import ExitStack
import numpy as np
import concourse.bass as bass
import concourse.mybir as mybir
from concourse.bass_utils import run_bass_kernel_spmd

F32 = mybir.dt.float32
BF16 = mybir.dt.bfloat16
ALU = mybir.AluOpType
AF = mybir.ActivationFunctionType
AX = mybir.AxisListType

ENGS = ["pe", "act", "dve", "pool", "sp"]
SIG_LIM = 30000


class Buf:
    __slots__ = ("name", "t", "w", "r", "excl")

    def __init__(self, name, t, excl=False):
        self.name = name
        self.t = t
        self.w = {}
        self.r = {}
        self.excl = excl

    def __getitem__(self, idx):
        return self.t[idx]


class Op:
    __slots__ = ("eng", "fn", "deps", "signal", "sig_no", "dma", "sem_i", "val")

    def __init__(self, eng, fn, dma):
        self.eng = eng
        self.fn = fn
        self.deps = []
        self.signal = False
        self.sig_no = 0
        self.dma = dma
        self.sem_i = 0
        self.val = 0


class Prog:
    def __init__(self, nc):
        self.nc = nc
        self.ops = {e: [] for e in ENGS}
        self.n_dma = {e: 0 for e in ENGS}
        self.n_dma_sems = {"sp": 40, "pool": 4, "act": 8, "pe": 1, "dve": 1}
        self.out_dmas = []
        self.dma_since = {e: [] for e in ENGS}
        self.bar_bufs = None

    def add(self, eng, fn, reads=(), writes=(), dma=False, is_output=False):
        op = Op(eng, fn, dma)
        key = op if dma else eng
        deps = {}
        for b in reads:
            for k, w in b.w.items():
                deps[id(w)] = w
            if b.excl:
                for k, r in b.r.items():
                    if k != eng:
                        deps[id(r)] = r
        for b in writes:
            for k, w in b.w.items():
                if dma or k != eng:
                    deps[id(w)] = w
            for k, r in b.r.items():
                if dma or k != eng:
                    deps[id(r)] = r
        for b in reads:
            b.r[key] = op
        for b in writes:
            b.w = {key: op}
            b.r = {}
        op.deps = list(deps.values())
        for d in op.deps:
            d.signal = True
        if dma:
            ns = self.n_dma_sems[eng]
            i = self.n_dma[eng]
            self.n_dma[eng] += 1
            op.sem_i = i % ns
            op.val = 16 * (i // ns + 1)
            op.signal = True
            self.dma_since[eng].append(op)
            if len(self.dma_since[eng]) > ns:
                self.dma_since[eng] = self.dma_since[eng][-ns:]
            if is_output:
                self.out_dmas.append(op)
        self.ops[eng].append(op)
        return op

    def dma(self, q, out_ap, in_ap, reads=(), writes=(), is_output=False, slow=False):
        if slow:
            return self.add(q, lambda e: e.dma_start(out=out_ap, in_=in_ap, allow_slow_non_contiguous=True), reads, writes,
                            dma=True, is_output=is_output)
        return self.add(q, lambda e: e.dma_start(out=out_ap, in_=in_ap), reads, writes, dma=True,
                        is_output=is_output)

    def barrier(self):
        bb = self.bar_bufs
        firsts = []
        for e in ENGS:
            if e == "sp":
                op = self.add("sp", lambda q: q.dma_start(out=bb["sp"][0:1, 0:4], in_=bb["spsrc"][0:1, 0:4]),
                              writes=[bb["sp"]], dma=True)
                for q in ENGS:
                    for d in self.dma_since[q]:
                        if d is not op:
                            op.deps.append(d)
                            d.signal = True
            elif e == "pe":
                op = self.add("pe", lambda t: t.matmul(bb["pe"][0:1, 0:1], lhsT=bb["pesrc"][0:1, 0:1],
                                                        rhs=bb["pesrc"][0:1, 0:1], start=True, stop=True),
                              writes=[bb["pe"]])
            else:
                b = bb[e]
                if e == "act":
                    op = self.add(e, (lambda b: (lambda g: g.memzero(b[0:1, 0:4])))(b), writes=[b])
                else:
                    op = self.add(e, (lambda b: (lambda g: g.memset(b[0:1, 0:4], 0.0)))(b), writes=[b])
            firsts.append(op)
        for e in ENGS:
            if e == "sp":
                op = self.add("sp", lambda q: q.dma_start(out=bb["sp2"][0:1, 0:4], in_=bb["spsrc"][0:1, 0:4]),
                              writes=[bb["sp2"]], dma=True)
            elif e == "pe":
                op = self.add("pe", lambda t: t.matmul(bb["pe"][0:1, 1:2], lhsT=bb["pesrc"][0:1, 0:1],
                                                        rhs=bb["pesrc"][0:1, 0:1], start=True, stop=True),
                              writes=[])
            else:
                b = bb[e + "2"]
                if e == "act":
                    op = self.add(e, (lambda b: (lambda g: g.memzero(b[0:1, 0:4])))(b), writes=[b])
                else:
                    op = self.add(e, (lambda b: (lambda g: g.memset(b[0:1, 0:4], 0.0)))(b), writes=[b])
            for f in firsts:
                if f.eng != e:
                    op.deps.append(f)
                    f.signal = True
        for q in ENGS:
            self.dma_since[q] = []

    def emit(self, E):
        nc = self.nc
        nsig = {}
        for e in ENGS:
            n = 0
            for op in self.ops[e]:
                if op.signal and not op.dma:
                    n += 1
                    op.sig_no = n
            nsig[e] = n
        esem = {}
        for e in ENGS:
            k = max(1, (nsig[e] + SIG_LIM - 1) // SIG_LIM)
            esem[e] = [E(nc.semaphore(f"s_{e}_{j}")) for j in range(k)]
        dsem = {}
        for e in ENGS:
            if self.n_dma[e] > 0:
                dsem[e] = [E(nc.semaphore(f"d_{e}_{j}")) for j in range(self.n_dma_sems[e])]
        block = E(nc.Block())
        engobj = {"pe": "tensor", "act": "scalar", "dve": "vector", "pool": "gpsimd", "sp": "sync"}

        def dep_wait(op):
            if op.dma:
                return dsem[op.eng][op.sem_i], op.val, ("d", op.eng, op.sem_i)
            ep = (op.sig_no - 1) // SIG_LIM
            return esem[op.eng][ep], op.sig_no - ep * SIG_LIM, ("e", op.eng, ep)

        out_dmas = self.out_dmas

        def make_body(e):
            def body(eng):
                waited = {}
                for op in self.ops[e]:
                    for d in op.deps:
                        sem, val, key = dep_wait(d)
                        if waited.get(key, 0) >= val:
                            continue
                        waited[key] = val
                        eng.wait_ge(sem, val)
                    if op.dma and op.val > 16:
                        key = ("d", e, op.sem_i)
                        if waited.get(key, 0) < op.val - 16:
                            waited[key] = op.val - 16
                            eng.wait_ge(dsem[e][op.sem_i], op.val - 16)
                    inst = op.fn(eng)
                    if op.dma:
                        inst.then_inc(dsem[e][op.sem_i], 16)
                    elif op.signal:
                        ep = (op.sig_no - 1) // SIG_LIM
                        inst.then_inc(esem[e][ep], 1)
                if e == "sp":
                    for d in out_dmas:
                        sem, val, key = dep_wait(d)
                        if waited.get(key, 0) >= val:
                            continue
                        waited[key] = val
                        eng.wait_ge(sem, val)
            return body

        for e in ENGS:
            if self.ops[e] or e == "sp":
                getattr(block, engobj[e])(make_body(e))


D = 2048
KT = 16
NCTX = 256
NLAT = 4096
T = NCTX + NLAT
NTILE = T // 128
EPS = 1e-6
GN_EPS = 64e-5
LORA = 96
C0 = 64
C1 = 128
EXPM05 = math.exp(-0.5)

DEBUG_OUT = set()
NCORES = 4
import os as _os
STOP_AFTER = _os.environ.get('KSTOP') or None


def build():
    nc = bass.Bass("TRN2", target_bir_lowering=False)

    def din(name, shape):
        return nc.dram_tensor(name, list(shape), F32, kind="ExternalInput").ap()

    def scratch(name, shape, dt=F32):
        kind = "ExternalOutput" if name in DEBUG_OUT else "Internal"
        return nc.dram_tensor(name, list(shape), dt, kind=kind).ap()

    xin = din("xin", [T, D])
    cvec = din("cvec", [2, D])
    ada_w = din("ada_w", [2, D, 3 * D])
    ada_b = din("ada_b", [2, 3 * D])
    norm_g = din("norm_g", [2, D])
    rk_mix = din("rk_mix", [6, D])
    rk_w_in = din("rk_w_in", [4, D, D])
    rk_w0 = din("rk_w0", [2, D])
    rk_w1 = din("rk_w1", [2, D, LORA])
    rk_w2 = din("rk_w2", [2, LORA, D])
    rk_a0 = din("rk_a0", [2, D])
    rk_a1 = din("rk_a1", [2, D, LORA])
    rk_a2 = din("rk_a2", [2, LORA, D])
    rk_k_k = din("rk_k_k", [D])
    rk_k_a = din("rk_k_a", [D])
    rk_r_k = din("rk_r_k", [D])
    rk_ln_g = din("rk_ln_g", [D])
    rk_ln_b = din("rk_ln_b", [D])
    rk_w_out = din("rk_w_out", [D, D])
    rt_w_in = din("rt_w_in", [D, 6 * D])
    rt_dl = din("rt_dl", [16])
    rt_gn_g = din("rt_gn_g", [2 * D])
    rt_w_out = din("rt_w_out", [2 * D, D])
    final_g = din("final_g", [D])
    ropec = din("ropec", [128, NLAT])
    ropes = din("ropes", [128, NLAT])
    yout = nc.dram_tensor("yout", [NLAT, D], F32, kind="ExternalOutput").ap()

    Wb_r = scratch("Wb_r", [16, 128, KT, 128], BF16)
    Wb_k = scratch("Wb_k", [16, 128, KT, 128], BF16)
    Wb_v = scratch("Wb_v", [8, 128, KT, 256], BF16)
    Wb_g = scratch("Wb_g", [8, 128, KT, 256], BF16)
    Wb_out0 = scratch("Wb_out0", [D, D], BF16)
    Wb_qk = scratch("Wb_qk", [32, 128, KT, 128], BF16)
    Wb_vg = scratch("Wb_vg", [16, 128, KT, 512], BF16)
    Wb_o1 = scratch("Wb_o1", [8, 128, 32, 256], BF16)
    ADA = scratch("ADA", [2, 2, 3, D])
    HT0 = scratch("HT0", [KT, 128, T])
    HT1 = scratch("HT1", [KT, 128, T], BF16)
    RT = scratch("RT", [KT, 128, T])
    KKT = scratch("KKT", [KT, 128, T])
    KDT = [scratch(f"KDT{d}", [KT, 128, T]) for d in range(2)]
    BT = [scratch(f"BT{d}", [KT, 128, T]) for d in range(2)]
    LW = [scratch(f"LW{d}", [T, D]) for d in range(2)]
    V0 = scratch("V0", [T, D])
    SG0 = scratch("SG0", [T, D])
    BON = scratch("BON", [T, 32])
    O0 = [scratch(f"O0_{d}", [T, D]) for d in range(2)]
    X1 = scratch("X1", [T, D])
    QT = [scratch(f"QT{d}", [KT, 128, T], BF16) for d in range(2)]
    KTT = [scratch(f"KTT{d}", [KT, 128, T], BF16) for d in range(2)]
    V1 = scratch("V1", [T, 2 * D], BF16)
    SG1 = scratch("SG1", [T, 2 * D])
    O1 = [scratch(f"O1_{d}", [T, 2 * D]) for d in range(2)]

    with ExitStack() as es:
        E = es.enter_context
        P = Prog(nc)
        cnt = [0]

        def sb(name, shape, dt=F32, stack=None):
            cnt[0] += 1
            return Buf(name, (stack or E)(nc.sbuf_tensor(f"{name}_{cnt[0]}", list(shape), dt)))

        P.bar_bufs = {k: sb("bar_" + k, [1, 8]) for k in ["sp", "sp2", "spsrc", "act", "act2", "dve", "dve2", "pool", "pool2"]}
        P.bar_bufs["pesrc"] = sb("bar_pesrc", [1, 8])
        PS = [Buf(f"ps{i}", E(nc.psum_tensor(f"ps{i}", [128, 512], F32)), excl=True) for i in range(7)]
        P.bar_bufs["pe"] = Buf("ps_bar", E(nc.psum_tensor("ps_bar", [128, 512], F32)))
        P.add("pool", lambda e: e.memset(P.bar_bufs["pesrc"][:], 0.0), writes=[P.bar_bufs["pesrc"]])
        P.add("pool", lambda e: e.memset(P.bar_bufs["spsrc"][:], 0.0), writes=[P.bar_bufs["spsrc"]])
        psi = [0]

        psn = [7]

        def ps():
            psi[0] = (psi[0] + 1) % psn[0]
            return PS[psi[0]]

        def tt(eng, out, in0, in1, op, R, W):
            P.add(eng, lambda e: e.tensor_tensor(out=out, in0=in0, in1=in1, op=op), reads=R, writes=W)

        def ts(eng, out, in0, s1, s2, op0, op1, R, W):
            if s2 is None:
                P.add(eng, lambda e: e.tensor_scalar(out=out, in0=in0, scalar1=s1, scalar2=None, op0=op0), reads=R, writes=W)
            else:
                P.add(eng, lambda e: e.tensor_scalar(out=out, in0=in0, scalar1=s1, scalar2=s2, op0=op0, op1=op1), reads=R, writes=W)

        def stt(eng, out, in0, s, in1, op0, op1, R, W):
            P.add(eng, lambda e: e.scalar_tensor_tensor(out=out, in0=in0, scalar=s, in1=in1, op0=op0, op1=op1), reads=R, writes=W)

        def act(out, in_, func, R, W, bias=None, scale=None, accum=None):
            kw = {}
            if bias is not None:
                kw["bias"] = bias
            if scale is not None:
                kw["scale"] = scale
            if accum is not None:
                kw["accum_out"] = accum
            P.add("act", lambda e: e.activation(out=out, in_=in_, func=func, **kw), reads=R, writes=W)

        def cp(eng, out, in_, R, W):
            if eng == "act":
                P.add("act", lambda e: e.copy(out=out, in_=in_), reads=R, writes=W)
            else:
                P.add(eng, lambda e: e.tensor_copy(out=out, in_=in_), reads=R, writes=W)

        def mm(out, lhsT, rhs, start, stop, R, W):
            P.add("pe", lambda e: e.matmul(out, lhsT=lhsT, rhs=rhs, start=start, stop=stop), reads=R, writes=W)

        def tr(out, in_, ident, R, W):
            P.add("pe", lambda e: e.transpose(out, in_, ident), reads=R, writes=W)

        def memset(eng, ap, val, W):
            P.add(eng, lambda e: e.memset(ap, val), writes=W)

        def red(eng, out, in_, R, W):
            P.add(eng, lambda e: e.tensor_reduce(out=out, in_=in_, axis=AX.X, op=ALU.add), reads=R, writes=W)

        def recip(out, in_, R, W):
            P.add("dve", lambda e: e.reciprocal(out=out, in_=in_), reads=R, writes=W)

        def ftv(ap1d):
            return ap1d.rearrange("(k p) -> p k", p=128)

        ident = sb("ident", [128, 128])
        identb = sb("identb", [128, 128], BF16)
        P.add("pool", lambda e: e.memset(ident[:], 1.0), writes=[ident])
        P.add("pool", lambda e: e.affine_select(out=ident[:], in_=ident[:], pattern=[[-1, 128]], compare_op=ALU.is_equal,
                                                fill=0.0, base=0, channel_multiplier=1), reads=[ident], writes=[ident])
        cp("dve", identb[:], ident[:], [ident], [identb])
        triU = sb("triU", [128, 128]); triUs = sb("triUs", [128, 128]); triL = sb("triL", [128, 128]); triLs = sb("triLs", [128, 128])

        def mk_tri(tb, cmp_, sg):
            P.add("pool", lambda e: e.memset(tb[:], 1.0), writes=[tb])
            P.add("pool", lambda e: e.affine_select(out=tb[:], in_=tb[:], pattern=[[sg, 128]], compare_op=cmp_,
                                                    fill=0.0, base=0, channel_multiplier=-sg), reads=[tb], writes=[tb])
        mk_tri(triU, ALU.is_ge, 1)
        mk_tri(triUs, ALU.is_gt, 1)
        mk_tri(triL, ALU.is_ge, -1)
        mk_tri(triLs, ALU.is_gt, -1)
        blk = sb("blk", [128, 128])
        P.add("pool", lambda e: e.memset(blk[:], 0.0), writes=[blk])
        P.add("pool", lambda e: e.memset(blk[0:64, 0:64], 1.0), writes=[blk])
        P.add("pool", lambda e: e.memset(blk[64:128, 64:128], 1.0), writes=[blk])
        blkb = sb("blkb", [128, 128], BF16)
        cp("dve", blkb[:], blk[:], [blk], [blkb])
        triUsb = sb("triUsb", [128, 128], BF16)
        cp("dve", triUsb[:], triUs[:], [triUs], [triUsb])
        tribI = []; tribS = []
        for d_, (si, ss) in enumerate(((triU, triUs), (triL, triLs))):
            bi = sb(f"tribI{d_}", [64, 64], BF16); bs = sb(f"tribS{d_}", [64, 64], BF16)
            cp("dve", bi[:], si[0:64, 0:64], [si], [bi]); cp("dve", bs[:], ss[0:64, 0:64], [ss], [bs])
            tribI.append(bi); tribS.append(bs)
        ind2 = sb("ind2", [128, 2])
        P.add("pool", lambda e: e.memset(ind2[:], 0.0), writes=[ind2])
        P.add("pool", lambda e: e.memset(ind2[0:64, 0:1], 1.0), writes=[ind2])
        P.add("pool", lambda e: e.memset(ind2[64:128, 1:2], 1.0), writes=[ind2])
        mS4 = [sb(f"mS4_{d}", [128, 4, 128]) for d in range(2)]
        mI4 = [sb(f"mI4_{d}", [128, 4, 128]) for d in range(2)]
        for d in range(2):
            srcS = triUs if d == 0 else triLs
            srcI = triU if d == 0 else triL
            for r_ in range(4):
                tt("pool", mS4[d][:, r_, :], srcS[:], blk[:], ALU.mult, [srcS, blk], [mS4[d]])
                tt("pool", mI4[d][:, r_, :], srcI[:], blk[:], ALU.mult, [srcI, blk], [mI4[d]])

        def pv(w2d, c0, e):
            return w2d[:, c0:c0 + e].rearrange("(k p) e -> p k e", p=128)
        for t_ in range(16):
            P.dma("pool", Wb_r[t_], pv(rk_w_in[0], t_ * 128, 128))
            P.dma("pool", Wb_k[t_], pv(rk_w_in[1], t_ * 128, 128))
        for t_ in range(8):
            P.dma("pool", Wb_v[t_], pv(rk_w_in[2], t_ * 256, 256))
            P.dma("pool", Wb_g[t_], pv(rk_w_in[3], t_ * 256, 256))
        for r_ in range(4):
            P.dma("pool", Wb_out0[r_ * 512:(r_ + 1) * 512, :], rk_w_out[r_ * 512:(r_ + 1) * 512, :])
        cast1 = []
        for t_ in range(32):
            cast1.append((Wb_qk[t_], pv(rt_w_in, t_ * 128, 128)))
        for t_ in range(16):
            cast1.append((Wb_vg[t_], pv(rt_w_in, 2 * D + t_ * 512, 512)))
        for t_ in range(8):
            cast1.append((Wb_o1[t_], pv(rt_w_out, t_ * 256, 256)))
        cast1_it = iter(cast1)

        def issue_cast1(n=1):
            for _ in range(n):
                nx = next(cast1_it, None)
                if nx is not None:
                    P.dma("pool", nx[0], nx[1])

        def adaln(layer):
            with ExitStack() as ph:
                cs = sb("cs", [128, KT, 2], stack=ph.enter_context)
                cst = sb("cst", [128, 2, KT], stack=ph.enter_context)
                for r_ in range(2):
                    P.dma("sp", cst[:, r_, :], ftv(cvec[r_, :]), writes=[cst], slow=True)
                act(cst[:], cst[:], AF.Silu, [cst], [cst])
                cp("dve", cs[:].rearrange("p k r -> p r k"), cst[:], [cst], [cs])
                wts = [sb(f"adaw{i}", [128, KT, 512], stack=ph.enter_context) for i in range(2)]
                bia = [sb(f"adab{i}", [2, 512], stack=ph.enter_context) for i in range(2)]
                ng = sb("ng", [2, 512], stack=ph.enter_context)
                res = [sb(f"adar{i}", [2, 512], stack=ph.enter_context) for i in range(2)]
                for cg in range(12):
                    w_ = wts[cg % 2]; b_ = bia[cg % 2]; r_ = res[cg % 2]
                    P.dma("sp", w_[:], ada_w[layer, :, cg * 512:(cg + 1) * 512].rearrange("(k p) e -> p k e", p=128), writes=[w_])
                    P.dma("sp", b_[:], ada_b[layer, cg * 512:(cg + 1) * 512].partition_broadcast(2), writes=[b_])
                    pp = ps()
                    for kt in range(KT):
                        mm(pp[0:2, :], cs[:, kt, :], w_[:, kt, :], kt == 0, kt == KT - 1, [cs, w_], [pp])
                    tt("dve", r_[:], pp[0:2, :], b_[:], ALU.add, [pp, b_], [r_])
                    which = cg // 4
                    if which == 1:
                        c4 = cg % 4
                        P.dma("sp", ng[:], norm_g[layer, c4 * 512:(c4 + 1) * 512].partition_broadcast(2), writes=[ng])
                        stt("dve", r_[:], r_[:], 1.0, ng[:], ALU.add, ALU.mult, [r_, ng], [r_])
                    c4 = cg % 4
                    for row in range(2):
                        P.dma("sp", ADA[layer, row, which, c4 * 512:(c4 + 1) * 512].unsqueeze(0), r_[row:row + 1, :], reads=[r_])
            P.barrier()

        def phase_h(layer, src, HTdst, hdt):
            with ExitStack() as ph:
                S = ph.enter_context
                TA = sb("TA", [128, D], stack=S); TB = sb("TB", [128, D], stack=S)
                xs = [sb(f"hx{i}", [128, D], stack=S) for i in range(2)]
                hs = [sb(f"hh{i}", [128, D], stack=S) for i in range(2)]
                hts = [sb(f"hT{i}", [128, KT, 128], hdt, stack=S) for i in range(2)]
                junk = sb("junk", [128, D], stack=S)
                st = [sb(f"hst{i}", [128, 4], stack=S) for i in range(2)]
                for i in range(NTILE):
                    row = 1 if i < 2 else 0
                    if i == 0 or i == 2:
                        P.dma("sp", TA[:], ADA[layer, row, 1, :].partition_broadcast(128), writes=[TA])
                        P.dma("sp", TB[:], ADA[layer, row, 0, :].partition_broadcast(128), writes=[TB])
                    x_ = xs[i % 2]; h_ = hs[i % 2]; hT = hts[i % 2]; s_ = st[i % 2]
                    P.dma("sp", x_[:], src[i * 128:(i + 1) * 128, :], writes=[x_])
                    act(junk[:], x_[:], AF.Square, [x_], [junk, s_], accum=s_[:, 0:1])
                    ts("dve", s_[:, 1:2], s_[:, 0:1], 1.0 / D, EPS, ALU.mult, ALU.add, [s_], [s_])
                    act(s_[:, 2:3], s_[:, 1:2], AF.Sqrt, [s_], [s_])
                    recip(s_[:, 3:4], s_[:, 2:3], [s_], [s_])
                    stt("dve", h_[:], x_[:], s_[:, 3:4], TA[:], ALU.mult, ALU.mult, [x_, s_, TA], [h_])
                    tt("pool", h_[:], h_[:], TB[:], ALU.add, [h_, TB], [h_])
                    for g in range(4):
                        pp = ps()
                        for q in range(4):
                            kt = g * 4 + q
                            tr(pp[:, q * 128:(q + 1) * 128], h_[:, kt * 128:(kt + 1) * 128], ident[:], [h_, ident], [pp])
                        cp("act" if g % 2 == 0 else "dve", hT[:, g * 4:(g + 1) * 4, :],
                           pp[:].rearrange("p (q t) -> p q t", q=4), [pp], [hT])
                    P.dma("sp", HTdst[:, :, i * 128:(i + 1) * 128].rearrange("k p t -> p k t"), hT[:], reads=[hT])
            P.barrier()

        def phase_p0():
            with ExitStack() as ph:
                S = ph.enter_context
                W = 256
                HW = sb("HW", [128, KT, W + 128], stack=S)
                dT = sb("dT", [128, KT, W], stack=S)
                xm = {j: sb(f"xm{j}", [128, KT, W], BF16, stack=S) for j in ("r", "k", "t0")}
                xm["t1"] = xm["t0"]
                mixT = sb("mixT", [128, 6, KT], stack=S)
                for j in range(6):
                    P.dma("sp", mixT[:, j, :], ftv(rk_mix[j, :]), writes=[mixT], slow=True)
                prm = sb("prm", [128, 8, KT], stack=S)
                for i_, src_ in enumerate([rk_k_k, rk_k_a, rk_k_a, rk_r_k, rk_a0[0, :], rk_a0[1, :]]):
                    P.dma("sp", prm[:, i_, :], ftv(src_), writes=[prm], slow=True)
                ts("dve", prm[:, 2, :], prm[:, 2, :], -1.0, 1.0, ALU.mult, ALU.add, [prm], [prm])
                w1b = [sb(f"w1b{d}", [128, KT, LORA], BF16, stack=S) for d in range(2)]
                a1b = [sb(f"a1b{d}", [128, KT, LORA], BF16, stack=S) for d in range(2)]
                w2b = [sb(f"w2b{d}", [LORA, D], BF16, stack=S) for d in range(2)]
                a2b = [sb(f"a2b{d}", [LORA, D], BF16, stack=S) for d in range(2)]
                W0t = [sb(f"W0t{d}", [128, D], stack=S) for d in range(2)]
                for d in range(2):
                    P.dma("pool", w1b[d][:], rk_w1[d].rearrange("(k p) r -> p k r", p=128), writes=[w1b[d]])
                    P.dma("pool", a1b[d][:], rk_a1[d].rearrange("(k p) r -> p k r", p=128), writes=[a1b[d]])
                    P.dma("pool", w2b[d][:], rk_w2[d], writes=[w2b[d]])
                    P.dma("pool", a2b[d][:], rk_a2[d], writes=[a2b[d]])
                    P.dma("sp", W0t[d][:], rk_w0[d, :].partition_broadcast(128), writes=[W0t[d]])
                thT = [sb(f"thT{d}", [LORA, W], BF16, stack=S) for d in range(2)]
                xa1 = [sb(f"xa1{d}", [LORA, W], BF16, stack=S) for d in range(2)]
                big = [sb(f"big{i}", [128, D], stack=S) for i in range(2)]
                wpc = [sb(f"wpc{i}", [128, KT, 256], BF16, stack=S) for i in range(2)]
                wrk = [sb(f"wrk{i}", [128, KT, 128], BF16, stack=S) for i in range(4)]
                fm = {n: [sb(f"fm_{n}{i}", [128, W], stack=S) for i in range(1)] * 2 for n in
                      ("r", "k", "a0", "a1", "kkr", "sq", "rn", "kk", "t1", "kd0", "kd1", "b0", "b1", "ks", "rk")}
                bon = [sb(f"bon{i}", [128, 32], stack=S) for i in range(2)]
                sqb = sb("sqb", [128, W], BF16, stack=S)
                rkb = sb("rkb", [128, W], BF16, stack=S)
                IND = sb("IND", [128, KT, 32], BF16, stack=S)
                memset("pool", IND[:], 0.0, [IND])
                for et_ in range(KT):
                    memset("pool", IND[0:64, et_, 2 * et_:2 * et_ + 1], 1.0, [IND])
                    memset("pool", IND[64:128, et_, 2 * et_ + 1:2 * et_ + 2], 1.0, [IND])
                psb = [PS[5], PS[6]]
                psn[0] = 5
                nwin = T // W
                import os
                nwin_run = int(os.environ.get('P0_NWIN', nwin))
                p0step = int(os.environ.get('P0_STEP', 9))
                p0sub = int(os.environ.get('P0_SUB', 9))
                big_i = [0]; wpc_i = [0]; wrk_i = [0]
                for w in range(nwin_run):
                    t0 = w * W
                    isctx = (w == 0)
                    if p0step < 1:
                        continue
                    lo = max(t0 - 64, 0 if isctx else NCTX)
                    hi = min(t0 + W + 64, NCTX if isctx else T)
                    if w in (0, 1, nwin - 1):
                        memset("pool", HW[:], 0.0, [HW])
                    P.dma("sp", HW[:, :, lo - (t0 - 64):hi - (t0 - 64)], HT0[:, :, lo:hi].rearrange("k p t -> p k t"), writes=[HW])
                    ctr = HW[:, :, 64:64 + W]
                    if isctx:
                        groups = [(0, 8, -1), (8, 16, 1)]
                    else:
                        groups = [(0, 4, -1), (4, 8, 1), (8, 12, -64), (12, 16, 64)]
                    for gi, (k0, k1, s_) in enumerate(groups):
                        tt("dve" if gi % 2 == 0 else "pool", dT[:, k0:k1, :], HW[:, k0:k1, 64 + s_:64 + s_ + W], HW[:, k0:k1, 64:64 + W],
                           ALU.subtract, [HW], [dT])
                    if not isctx:
                        d4 = dT[:].rearrange("p k (r c) -> p k r c", c=64)
                        h4 = ctr.rearrange("p k (r c) -> p k r c", c=64)
                        ts("dve", d4[:, 0:4, :, 0], h4[:, 0:4, :, 0], -1.0, None, ALU.mult, None, [HW, dT], [dT])
                        ts("dve", d4[:, 4:8, :, 63], h4[:, 4:8, :, 63], -1.0, None, ALU.mult, None, [HW, dT], [dT])

                    def mkxm(j, dst, engs=("dve",)):
                        for kt in range(KT):
                            stt(engs[kt % len(engs)], dst[:, kt, :], dT[:, kt, :], mixT[:, j, kt:kt + 1], ctr[:, kt, :],
                                ALU.mult, ALU.add, [dT, mixT, HW], [dst])

                    if p0step < 2:
                        continue
                    mkxm(4, xm["t0"])
                    for d in range(2):
                        pp = ps()
                        for kt in range(KT):
                            mm(pp[0:LORA, 0:W], w1b[d][:, kt, :], xm["t0"][:, kt, :], kt == 0, kt == KT - 1, [w1b[d], xm["t0"]], [pp])
                        act(thT[d][:], pp[0:LORA, 0:W], AF.Tanh, [pp], [thT[d]])
                    for d in range(2):
                        for tt_ in range(2):
                            bg = big[big_i[0] % 2]; big_i[0] += 1
                            for cg in range(4):
                                pp = ps()
                                mm(pp[:], thT[d][:, tt_ * 128:(tt_ + 1) * 128], w2b[d][:, cg * 512:(cg + 1) * 512], True, True, [thT[d], w2b[d]], [pp])
                                tt("dve", bg[:, cg * 512:(cg + 1) * 512], pp[:], W0t[d][:, cg * 512:(cg + 1) * 512], ALU.add, [pp, W0t[d]], [bg])
                            act(bg[:], bg[:], AF.Sigmoid, [bg], [bg])
                            ts("pool", bg[:], bg[:], -EXPM05, None, ALU.mult, None, [bg], [bg])
                            P.dma("sp", LW[d][t0 + tt_ * 128:t0 + (tt_ + 1) * 128, :], bg[:], reads=[bg])
                    if p0step < 3:
                        continue
                    mkxm(5, xm["t1"])
                    for d in range(2):
                        pp = ps()
                        for kt in range(KT):
                            mm(pp[0:LORA, 0:W], a1b[d][:, kt, :], xm["t1"][:, kt, :], kt == 0, kt == KT - 1, [a1b[d], xm["t1"]], [pp])
                        cp("act", xa1[d][:], pp[0:LORA, 0:W], [pp], [xa1[d]])
                    if p0step < 4:
                        continue
                    for j, dst_dram, key in ((2, V0, "t0"), (3, SG0, "t1")):
                        mkxm(j, xm[key])
                        bgs = [big[0], big[1]]
                        for cg in range(8):
                            wp = wpc[wpc_i[0] % 2]; wpc_i[0] += 1
                            P.dma("sp", wp[:], (Wb_v if j == 2 else Wb_g)[cg], writes=[wp])
                            for tt_ in range(2):
                                pp = ps()
                                for kt in range(KT):
                                    mm(pp[:, 0:256], xm[key][:, kt, tt_ * 128:(tt_ + 1) * 128], wp[:, kt, :], kt == 0, kt == KT - 1, [xm[key], wp], [pp])
                                if j == 2:
                                    cp("act", bgs[tt_][:, cg * 256:(cg + 1) * 256], pp[:, 0:256], [pp], [bgs[tt_]])
                                else:
                                    act(bgs[tt_][:, cg * 256:(cg + 1) * 256], pp[:, 0:256], AF.Silu, [pp], [bgs[tt_]])
                        for tt_ in range(2):
                            P.dma("sp", dst_dram[t0 + tt_ * 128:t0 + (tt_ + 1) * 128, :], bgs[tt_][:], reads=[bgs[tt_]])
                    if p0step < 5:
                        continue
                    mkxm(0, xm["r"])
                    mkxm(1, xm["k"])
                    for et in range(KT):
                        i2 = et % 2
                        wr = wrk[wrk_i[0] % 4]; wk = wrk[(wrk_i[0] + 1) % 4]; wrk_i[0] += 2
                        P.dma("sp", wr[:], Wb_r[et], writes=[wr])
                        P.dma("sp", wk[:], Wb_k[et], writes=[wk])
                        p1 = ps(); p2 = ps()
                        psr = p1[:, 0:W]; psk = p1[:, W:2 * W]
                        for kt in range(KT):
                            mm(psr, wr[:, kt, :], xm["r"][:, kt, :], kt == 0, kt == KT - 1, [wr, xm["r"]], [p1])
                        for kt in range(KT):
                            mm(psk, wk[:, kt, :], xm["k"][:, kt, :], kt == 0, kt == KT - 1, [wk, xm["k"]], [p1])
                        for d in range(2):
                            mm(p2[:, d * W:(d + 1) * W], a2b[d][:, et * 128:(et + 1) * 128], xa1[d][:], True, True, [a2b[d], xa1[d]], [p2])
                        f = {n: fm[n][i2] for n in fm}
                        cp("act", f["r"][:], psr, [p1], [f["r"]])
                        cp("act", f["k"][:], psk, [p1], [f["k"]])
                        for d in range(2):
                            act(f[f"a{d}"][:], p2[:, d * W:(d + 1) * W], AF.Sigmoid, [p2, prm], [f[f"a{d}"]], bias=prm[:, 4 + d, et:et + 1])
                        if p0sub < 2:
                            continue
                        ts("dve", f["kkr"][:], psk, prm[:, 0, et:et + 1], None, ALU.mult, None, [p1, prm], [f["kkr"]])
                        act(sqb[:], psk, AF.Square, [p1, prm], [sqb], scale=prm[:, 0, et:et + 1])
                        p3 = ps()
                        mm(p3[:, 0:W], blkb[:], sqb[:], True, True, [blkb, sqb], [p3])
                        act(f["rn"][:], p3[:, 0:W], AF.Sqrt, [p3], [f["rn"]])
                        ts("dve", f["rn"][:], f["rn"][:], 1e-12, None, ALU.max, None, [f["rn"]], [f["rn"]])
                        recip(f["rn"][:], f["rn"][:], [f["rn"]], [f["rn"]])
                        tt("dve", f["kk"][:], f["kkr"][:], f["rn"][:], ALU.mult, [f["kkr"], f["rn"]], [f["kk"]])
                        if p0sub < 3:
                            continue
                        P.dma("sp", RT[et, :, t0:t0 + W], f["r"][:], reads=[f["r"]])
                        P.dma("sp", KKT[et, :, t0:t0 + W], f["kk"][:], reads=[f["kk"]])
                        for d in range(2):
                            ts("dve", f["t1"][:], f[f"a{d}"][:], prm[:, 1, et:et + 1], prm[:, 2, et:et + 1], ALU.mult, ALU.add,
                               [f[f"a{d}"], prm], [f["t1"]])
                            tt("pool", f[f"kd{d}"][:], f["k"][:], f["t1"][:], ALU.mult, [f["k"], f["t1"]], [f[f"kd{d}"]])
                            tt("pool", f[f"b{d}"][:], f["kk"][:], f[f"a{d}"][:], ALU.mult, [f["kk"], f[f"a{d}"]], [f[f"b{d}"]])
                            P.dma("sp", KDT[d][et, :, t0:t0 + W], f[f"kd{d}"][:], reads=[f[f"kd{d}"]])
                            P.dma("sp", BT[d][et, :, t0:t0 + W], f[f"b{d}"][:], reads=[f[f"b{d}"]])
                        if p0sub < 4:
                            continue
                        tt("dve", f["ks"][:], f["kd0"][:], f["kd1"][:], ALU.add, [f["kd0"], f["kd1"]], [f["ks"]])
                        stt("dve", rkb[:], f["r"][:], prm[:, 3, et:et + 1], f["ks"][:], ALU.mult, ALU.mult, [f["r"], prm, f["ks"]], [rkb])
                        if p0step < 6:
                            continue
                        for tt_ in range(2):
                            mm(psb[tt_][:, 0:32], rkb[:, tt_ * 128:(tt_ + 1) * 128], IND[:, et, :], et == 0, et == KT - 1, [rkb, IND], [psb[tt_]])
                    if p0step < 6:
                        continue
                    for tt_ in range(2):
                        cp("act", bon[tt_][:], psb[tt_][:, 0:32], [psb[tt_]], [bon[tt_]])
                        P.dma("sp", BON[t0 + tt_ * 128:t0 + (tt_ + 1) * 128, :], bon[tt_][:], reads=[bon[tt_]])
                psn[0] = 7
            P.barrier()

        def phase_s0(d, NG=4):
            with ExitStack() as ph:
                S = ph.enter_context
                SDT = BF16
                NP = 16
                GP = NP // NG
                NB = GP // 4
                ld = {n: [sb(f"s_{n}{i}", [128, NP, C0], stack=S) for i in range(2)] for n in ("r", "kd", "kk", "b", "v")}
                lwc = [sb(f"s_lw{i}", [C0, D], stack=S) for i in range(2)]
                lwh = sb("s_lwh", [C0, D], BF16, stack=S); lwl = sb("s_lwl", [C0, D], BF16, stack=S)

                def gb(name, shape, dt=F32):
                    return [sb(f"{name}{g}", shape, dt, stack=S) for g in range(NG)]
                vcb = gb("s_vcb", [128, GP, C0], SDT)
                eP = gb("s_eP", [128, GP, C0]); ePx = gb("s_ePx", [128, GP, C0]); eN = gb("s_eN", [128, GP, C0])
                ex = {n: gb(f"s_ex{n}", [128, GP, 128], SDT) for n in ("A", "B", "K", "R")}
                for n in ex:
                    for g in range(NG):
                        memset("pool", ex[n][g][:], 0.0, [ex[n][g]])
                Xs = [gb(f"s_X{i}", [128, GP, 128], SDT) for i in range(2)]
                Ls = [gb(f"s_L{i}", [128, GP, 128], SDT) for i in range(2)]
                Mak = gb("s_Mak", [128, GP, 128], SDT); Mrb = gb("s_Mrb", [128, GP, 128], SDT); Mrk = gb("s_Mrk", [128, GP, 128], SDT)
                BTe = gb("s_BTe", [128, GP, 128], SDT); KTe = gb("s_KTe", [128, GP, 128], SDT)
                ST32 = gb("s_ST32", [128, GP, C0]); STb = gb("s_STb", [128, GP, C0], SDT)
                Y32 = gb("s_Y32", [128, GP, C0]); Yb = gb("s_Yb", [128, GP, C0], SDT)
                oc = [gb(f"s_oc{i}", [128, GP, C0]) for i in range(2)]
                for g in range(NG):
                    memset("pool", ST32[g][:], 0.0, [ST32[g]])
                    memset("pool", STb[g][:], 0.0, [STb[g]])
                nch = T // C0
                order = list(range(nch)) if d == 0 else [3, 2, 1, 0] + list(range(nch - 1, 3, -1))
                tl = C0 - 1 if d == 0 else 0
                GW = GP * C0

                def fview(ap, t0):
                    return ap[:, :, t0:t0 + C0].rearrange("k p t -> p k t")

                def pview(ap, t0, hh, g):
                    return ap[t0:t0 + C0, g * GP * 128:(g + 1) * GP * 128].rearrange("s (pr h v) -> h s pr v", h=2, v=64)[hh]

                def p3(p_):
                    return p_[:, 0:GW].rearrange("p (a t) -> p a t", t=64)

                def p4(p_):
                    return p_[:].rearrange("p (q t) -> p q t", q=4)

                def body(g, L_, t0, b2):
                    gs = slice(g * GP, (g + 1) * GP)
                    pA = ps(); pB = ps()
                    for pp, trib in ((pA, tribI[d]), (pB, tribS[d])):
                        for q in range(GP):
                            pr = g * GP + q
                            o_ = pp[:, q * 64:(q + 1) * 64]
                            mm(o_, lwh[:, pr * 128:(pr + 1) * 128], trib[:], True, False, [lwh, trib], [pp])
                            mm(o_, lwl[:, pr * 128:(pr + 1) * 128], trib[:], False, True, [lwl, trib], [pp])
                    act(eP[g][:], p3(pA), AF.Exp, [pA], [eP[g]])
                    act(eN[g][:], p3(pA), AF.Exp, [pA], [eN[g]], scale=-1.0)
                    act(ePx[g][:], p3(pB), AF.Exp, [pB], [ePx[g]])
                    yield
                    for hh in range(2):
                        psl = slice(hh * 64, (hh + 1) * 64)
                        csl = slice(hh * 64, (hh + 1) * 64)
                        stt("dve", ex["A"][g][psl, :, csl], L_["kk"][psl, gs, :], -1.0, ePx[g][psl, :, :], ALU.mult, ALU.mult, [L_["kk"], ePx[g]], [ex["A"][g]])
                        tt("pool", ex["B"][g][psl, :, csl], L_["b"][psl, gs, :], eN[g][psl, :, :], ALU.mult, [L_["b"], eN[g]], [ex["B"][g]])
                        tt("dve", ex["K"][g][psl, :, csl], L_["kd"][psl, gs, :], eN[g][psl, :, :], ALU.mult, [L_["kd"], eN[g]], [ex["K"][g]])
                        tt("pool", ex["R"][g][psl, :, csl], L_["r"][psl, gs, :], eP[g][psl, :, :], ALU.mult, [L_["r"], eP[g]], [ex["R"][g]])
                    cp("act", vcb[g][:], L_["v"][:, gs, :], [L_["v"]], [vcb[g]])
                    yield
                    X = Xs[0][g]; Lm = Ls[0][g]
                    specs = [(X, "B", "A", mS4[d]), (Lm, "A", "B", mS4[1 - d]), (Mak[g], "K", "A", mS4[d]), (Mrb[g], "B", "R", mI4[d]), (Mrk[g], "K", "R", mI4[d])]
                    for dst, l_, r_, msk in specs:
                        for sbk in range(NB):
                            pp = ps()
                            for q in range(4):
                                pr = sbk * 4 + q
                                mm(pp[:, q * 128:(q + 1) * 128], ex[l_][g][:, pr, :], ex[r_][g][:, pr, :], True, True, [ex[l_][g], ex[r_][g]], [pp])
                            tt("dve", dst[:, sbk * 4:(sbk + 1) * 4, :], p4(pp), msk[:], ALU.mult, [pp, msk], [dst])
                    yield
                    pY = ps()
                    for q in range(GP):
                        o_ = pY[:, q * 64:(q + 1) * 64]
                        mm(o_, ex["A"][g][:, q, :], STb[g][:, q, :], True, False, [ex["A"][g], STb[g]], [pY])
                        mm(o_, Mak[g][:, q, :], vcb[g][:, q, :], False, True, [Mak[g], vcb[g]], [pY])
                    cp("act", Y32[g][:], p3(pY), [pY], [Y32[g]])
                    cp("dve", Yb[g][:], p3(pY), [pY], [Yb[g]])
                    yield
                    cur = 0
                    for lev in range(6):
                        Xc = Xs[cur][g]; Lc = Ls[cur][g]
                        pY = ps()
                        for q in range(GP):
                            mm(pY[:, q * 64:(q + 1) * 64], Xc[:, q, :], Yb[g][:, q, :], True, True, [Xc, Yb[g]], [pY])
                        if lev < 5:
                            Xn = Xs[1 - cur][g]; Ln = Ls[1 - cur][g]
                            pxs = []
                            for sbk in range(NB):
                                pp = ps()
                                for q in range(4):
                                    pr = sbk * 4 + q
                                    mm(pp[:, q * 128:(q + 1) * 128], Lc[:, pr, :], Xc[:, pr, :], True, True, [Lc, Xc], [pp])
                                pxs.append(pp)
                            pls = []
                            if lev < 4:
                                for sbk in range(NB):
                                    pp = ps()
                                    for q in range(4):
                                        pr = sbk * 4 + q
                                        mm(pp[:, q * 128:(q + 1) * 128], Xc[:, pr, :], Lc[:, pr, :], True, True, [Lc, Xc], [pp])
                                    pls.append(pp)
                        tt("dve", Y32[g][:], Y32[g][:], p3(pY), ALU.add, [Y32[g], pY], [Y32[g]])
                        cp("pool", Yb[g][:], Y32[g][:], [Y32[g]], [Yb[g]])
                        if lev < 5:
                            for sbk, pp in enumerate(pxs):
                                cp("act", Xn[:, sbk * 4:(sbk + 1) * 4, :], p4(pp), [pp], [Xn])
                            for sbk, pp in enumerate(pls):
                                cp("act" if sbk % 2 else "dve", Ln[:, sbk * 4:(sbk + 1) * 4, :], p4(pp), [pp], [Ln])
                            cur = 1 - cur
                        yield
                    o_sb = oc[b2][g]
                    pO = ps()
                    for q in range(GP):
                        o_ = pO[:, q * 64:(q + 1) * 64]
                        mm(o_, ex["R"][g][:, q, :], STb[g][:, q, :], True, False, [ex["R"][g], STb[g]], [pO])
                        mm(o_, Mrb[g][:, q, :], Yb[g][:, q, :], False, False, [Mrb[g], Yb[g]], [pO])
                        mm(o_, Mrk[g][:, q, :], vcb[g][:, q, :], False, True, [Mrk[g], vcb[g]], [pO])
                    pts = []
                    for src_, dst in ((ex["B"][g], BTe[g]), (ex["K"][g], KTe[g])):
                        for sbk in range(NB):
                            pp = ps()
                            for q in range(4):
                                pr = sbk * 4 + q
                                mm(pp[:, q * 128:(q + 1) * 128], src_[:, pr, :], identb[:], True, True, [src_, identb], [pp])
                            pts.append((pp, dst, sbk))
                    cp("act", o_sb[:], p3(pO), [pO], [o_sb])
                    for hh in range(2):
                        P.dma("sp", pview(O0[d], t0, hh, g), o_sb[hh * 64:(hh + 1) * 64, :, :], reads=[o_sb])
                    for i_, (pp, dst, sbk) in enumerate(pts):
                        cp("act" if i_ % 2 else "dve", dst[:, sbk * 4:(sbk + 1) * 4, :], p4(pp), [pp], [dst])
                    yield
                    pS = ps()
                    for q in range(GP):
                        o_ = pS[:, q * 64:(q + 1) * 64]
                        mm(o_, BTe[g][:, q, :], Yb[g][:, q, :], True, False, [BTe[g], Yb[g]], [pS])
                        mm(o_, KTe[g][:, q, :], vcb[g][:, q, :], False, True, [KTe[g], vcb[g]], [pS])
                    tt("dve", ST32[g][:], ST32[g][:], p3(pS), ALU.add, [ST32[g], pS], [ST32[g]])
                    tt("pool", ST32[g][:], ST32[g][:], eP[g][:, :, tl:tl + 1].to_broadcast([128, GP, C0]), ALU.mult, [ST32[g], eP[g]], [ST32[g]])
                    cp("act", STb[g][:], ST32[g][:], [ST32[g]], [STb[g]])
                    yield

                for ci, c in enumerate(order):
                    t0 = c * C0
                    b2 = ci % 2
                    L_ = {n: ld[n][b2] for n in ld}
                    lw_ = lwc[b2]
                    P.dma("sp", L_["r"][:], fview(RT, t0), writes=[L_["r"]])
                    P.dma("sp", L_["kd"][:], fview(KDT[d], t0), writes=[L_["kd"]])
                    P.dma("sp", L_["kk"][:], fview(KKT, t0), writes=[L_["kk"]])
                    P.dma("sp", L_["b"][:], fview(BT[d], t0), writes=[L_["b"]])
                    P.dma("sp", lw_[:], LW[d][t0:t0 + C0, :], writes=[lw_])
                    for hh in range(2):
                        P.dma("sp", L_["v"][hh * 64:(hh + 1) * 64, :, :],
                              V0[t0:t0 + C0, :].rearrange("s (pr h v) -> h s pr v", h=2, v=64)[hh], writes=[L_["v"]])
                    cp("act", lwh[:], lw_[:], [lw_], [lwh])
                    tt("dve", lwl[:], lw_[:], lwh[:], ALU.subtract, [lw_, lwh], [lwl])
                    issue_cast1(1)
                    gens = [body(g, L_, t0, b2) for g in range(NG)]
                    while gens:
                        for gen in list(gens):
                            try:
                                next(gen)
                            except StopIteration:
                                gens.remove(gen)
            P.barrier()

        def phase_o0():
            with ExitStack() as ph:
                S = ph.enter_context
                WoB = sb("WoB", [128, KT, D], BF16, stack=S)
                P.dma("sp", WoB[:], Wb_out0.rearrange("(k p) e -> p k e", p=128), writes=[WoB])
                LNG = sb("LNG", [128, D], stack=S); LNB = sb("LNB", [128, D], stack=S); TG = sb("TG", [128, D], stack=S)
                P.dma("sp", LNG[:], rk_ln_g.partition_broadcast(128), writes=[LNG])
                P.dma("sp", LNB[:], rk_ln_b.partition_broadcast(128), writes=[LNB])
                bufs = {n: [sb(f"o_{n}{i}", [128, D], stack=S) for i in range(1)] * 2 for n in ("of", "ob", "v", "sg", "x")}
                sq = sb("o_sq", [128, D], stack=S)
                ybf = sb("o_ybf", [128, D], BF16, stack=S)
                bo = [sb(f"o_bon{i}", [128, 32], stack=S) for i in range(2)]
                st = [sb(f"o_st{i}", [128, 4, 32], stack=S) for i in range(2)]
                yT = [sb(f"o_yT{i}", [128, KT, 128], BF16, stack=S) for i in range(2)]
                for i in range(NTILE):
                    row = 1 if i < 2 else 0
                    if i == 0 or i == 2:
                        P.dma("sp", TG[:], ADA[0, row, 2, :].partition_broadcast(128), writes=[TG])
                    b2 = i % 2
                    B_ = {n: bufs[n][b2] for n in bufs}
                    rs = slice(i * 128, (i + 1) * 128)
                    P.dma("sp", B_["of"][:], O0[0][rs, :], writes=[B_["of"]])
                    P.dma("sp", B_["ob"][:], O0[1][rs, :], writes=[B_["ob"]])
                    P.dma("sp", B_["v"][:], V0[rs, :], writes=[B_["v"]])
                    P.dma("sp", B_["sg"][:], SG0[rs, :], writes=[B_["sg"]])
                    P.dma("sp", B_["x"][:], xin[rs, :], writes=[B_["x"]])
                    P.dma("sp", bo[b2][:], BON[rs, :], writes=[bo[b2]])
                    o = B_["of"]; s_ = st[b2]
                    o3 = o[:].rearrange("p (h v) -> p h v", v=64)
                    tt("dve", o[:], o[:], B_["ob"][:], ALU.add, [o, B_["ob"]], [o])
                    red("dve", s_[:, 0, :], o3, [o], [s_])
                    ts("dve", s_[:, 0, :], s_[:, 0, :], -1.0 / 64, None, ALU.mult, None, [s_], [s_])
                    tt("dve", o3, o3, s_[:, 0, :].unsqueeze(2).to_broadcast([128, 32, 64]), ALU.add, [o, s_], [o])
                    tt("pool", sq[:], o[:], o[:], ALU.mult, [o], [sq])
                    red("dve", s_[:, 1, :], sq[:].rearrange("p (h v) -> p h v", v=64), [sq], [s_])
                    ts("dve", s_[:, 1, :], s_[:, 1, :], 1.0 / 64, GN_EPS, ALU.mult, ALU.add, [s_], [s_])
                    act(s_[:, 2, :], s_[:, 1, :], AF.Sqrt, [s_], [s_])
                    recip(s_[:, 3, :], s_[:, 2, :], [s_], [s_])
                    tt("dve", o3, o3, s_[:, 3, :].unsqueeze(2).to_broadcast([128, 32, 64]), ALU.mult, [o, s_], [o])
                    tt("pool", o[:], o[:], LNG[:], ALU.mult, [o, LNG], [o])
                    tt("pool", o[:], o[:], LNB[:], ALU.add, [o, LNB], [o])
                    v_ = B_["v"]
                    v3 = v_[:].rearrange("p (h v) -> p h v", v=64)
                    tt("dve", v3, v3, bo[b2][:].unsqueeze(2).to_broadcast([128, 32, 64]), ALU.mult, [v_, bo[b2]], [v_])
                    tt("pool", o[:], o[:], v_[:], ALU.add, [o, v_], [o])
                    tt("dve", o[:], o[:], B_["sg"][:], ALU.mult, [o, B_["sg"]], [o])
                    yt = yT[b2]
                    cp("act", ybf[:], o[:], [o], [ybf])
                    for g in range(4):
                        pp = ps()
                        for q in range(4):
                            kt = g * 4 + q
                            mm(pp[:, q * 128:(q + 1) * 128], ybf[:, kt * 128:(kt + 1) * 128], identb[:], True, True, [ybf, identb], [pp])
                        cp("act", yt[:, g * 4:(g + 1) * 4, :], pp[:].rearrange("p (q t) -> p q t", q=4), [pp], [yt])
                    x_ = B_["x"]
                    for cg in range(4):
                        pp = ps()
                        for kt in range(KT):
                            mm(pp[:], yt[:, kt, :], WoB[:, kt, cg * 512:(cg + 1) * 512], kt == 0, kt == KT - 1, [yt, WoB], [pp])
                        cs_ = slice(cg * 512, (cg + 1) * 512)
                        tt("dve", sq[:, cs_], pp[:], TG[:, cs_], ALU.mult, [pp, TG], [sq])
                        tt("pool", x_[:, cs_], x_[:, cs_], sq[:, cs_], ALU.add, [x_, sq], [x_])
                    P.dma("sp", X1[rs, :], x_[:], reads=[x_])
            P.barrier()

        def phase_p1():
            with ExitStack() as ph:
                S = ph.enter_context
                W = 256
                hT = [sb(f"p1_hT{i}", [128, KT, W], BF16, stack=S) for i in range(2)]
                dl = sb("p1_dl", [128, 16], stack=S)
                P.dma("sp", dl[:], rt_dl.partition_broadcast(128), writes=[dl])
                lg = sb("p1_lg", [128, 6, 16], stack=S)
                act(lg[:, 0, :], dl[:], AF.Exp, [dl], [lg], scale=-1.0)
                act(lg[:, 0, :], lg[:, 0, :], AF.Ln, [lg], [lg], bias=1.0)
                ts("dve", lg[:, 1, :], lg[:, 0, :], 1.0, None, ALU.mult, None, [lg], [lg])
                ts("dve", lg[:, 0, :], lg[:, 1, :], -1.0, None, ALU.mult, None, [lg], [lg])
                ts("dve", lg[:, 2, :], lg[:, 0, :], float(C1), None, ALU.mult, None, [lg], [lg])
                ts("dve", lg[:, 3, :], lg[:, 1, :], math.log(1.0 / 16), None, ALU.add, None, [lg], [lg])
                ts("dve", lg[:, 4, :], lg[:, 1, :], float(C1), math.log(1.0 / 16), ALU.mult, ALU.add, [lg], [lg])
                pos = sb("p1_pos", [128, W], stack=S)
                ones_ = sb("p1_ones", [128, 128], BF16, stack=S)
                memset("pool", ones_[:], 1.0, [ones_])
                pp_ = ps()
                mm(pp_[:, 0:128], ones_[:], triUsb[:], True, True, [ones_, triUsb], [pp_])
                cp("dve", pos[:, 0:128], pp_[:, 0:128], [pp_], [pos])
                cp("dve", pos[:, 128:256], pp_[:, 0:128], [pp_], [pos])
                DQ = [[sb(f"DQ{d}{h}", [128, W], stack=S) for h in range(8)] for d in range(2)]
                DK = [[sb(f"DK{d}{h}", [128, W], stack=S) for h in range(8)] for d in range(2)]
                for h in range(8):
                    c0 = h; c1 = 8 + h
                    act(DQ[0][h][:], pos[:], AF.Exp, [pos, lg], [DQ[0][h]], scale=lg[:, 0, c0:c0 + 1], bias=lg[:, 0, c0:c0 + 1])
                    act(DK[0][h][:], pos[:], AF.Exp, [pos, lg], [DK[0][h]], scale=lg[:, 1, c0:c0 + 1], bias=lg[:, 3, c0:c0 + 1])
                    act(DQ[1][h][:], pos[:], AF.Exp, [pos, lg], [DQ[1][h]], scale=lg[:, 1, c1:c1 + 1], bias=lg[:, 2, c1:c1 + 1])
                    act(DK[1][h][:], pos[:], AF.Exp, [pos, lg], [DK[1][h]], scale=lg[:, 0, c1:c1 + 1], bias=lg[:, 4, c1:c1 + 1])
                cs_t = sb("p1_cos", [128, W], stack=S); sn_t = sb("p1_sin", [128, W], stack=S)
                wrk = [sb(f"p1_wrk{i}", [128, KT, 128], BF16, stack=S) for i in range(4)]
                wpc = [sb(f"p1_wpc{i}", [128, KT, 512], BF16, stack=S) for i in range(2)]
                xx = [[sb(f"p1_x{i}{j}", [128, W], stack=S) for j in range(2)] for i in range(2)]
                tmp = [sb(f"p1_t{i}", [128, W], stack=S) for i in range(4)]
                yy = [sb(f"p1_y{i}", [128, W], stack=S) for i in range(2)]
                ob = [sb(f"p1_ob{i}", [128, W], BF16, stack=S) for i in range(4)]
                vst = [sb(f"p1_vst{i}", [128, 512], BF16, stack=S) for i in range(2)]
                gst = [sb(f"p1_gst{i}", [128, 512], stack=S) for i in range(2)]
                nwin = T // W
                cnt_ = [0]
                for w in range(nwin):
                    t0 = w * W
                    isctx = (w == 0)
                    h_ = hT[w % 2]
                    P.dma("sp", h_[:], HT1[:, :, t0:t0 + W].rearrange("k p t -> p k t"), writes=[h_])
                    if not isctx:
                        P.dma("sp", cs_t[:], ropec[:, t0 - NCTX:t0 - NCTX + W], writes=[cs_t])
                        P.dma("sp", sn_t[:], ropes[:, t0 - NCTX:t0 - NCTX + W], writes=[sn_t])
                    for qk in range(2):
                        dsts = QT if qk == 0 else KTT
                        tabs = DQ if qk == 0 else DK
                        for h in range(8):
                            i2 = cnt_[0] % 2; cnt_[0] += 1
                            for half in range(2):
                                et = h * 2 + half
                                col0 = qk * D + h * 256 + half * 128
                                wr = wrk[(cnt_[0] * 2 + half) % 4]
                                P.dma("sp", wr[:], Wb_qk[qk * 16 + h * 2 + half], writes=[wr])
                                pp = ps()
                                for kt in range(KT):
                                    mm(pp[:, 0:W], wr[:, kt, :], h_[:, kt, :], kt == 0, kt == KT - 1, [wr, h_], [pp])
                                cp("act", xx[i2][half][:], pp[:, 0:W], [pp], [xx[i2][half]])
                            x1 = xx[i2][0]; x2 = xx[i2][1]
                            if isctx:
                                y1, y2 = x1, x2
                            else:
                                y1, y2 = yy[0], yy[1]
                                tt("dve", tmp[0][:], x1[:], cs_t[:], ALU.mult, [x1, cs_t], [tmp[0]])
                                tt("pool", tmp[1][:], x2[:], sn_t[:], ALU.mult, [x2, sn_t], [tmp[1]])
                                tt("dve", y1[:], tmp[0][:], tmp[1][:], ALU.subtract, [tmp[0], tmp[1]], [y1])
                                tt("pool", tmp[2][:], x1[:], sn_t[:], ALU.mult, [x1, sn_t], [tmp[2]])
                                tt("dve", tmp[3][:], x2[:], cs_t[:], ALU.mult, [x2, cs_t], [tmp[3]])
                                tt("pool", y2[:], tmp[2][:], tmp[3][:], ALU.add, [tmp[2], tmp[3]], [y2])
                            for d in range(2):
                                for half, y_ in ((0, y1), (1, y2)):
                                    o_ = ob[d * 2 + half]
                                    tt("dve" if half == 0 else "pool", o_[:], y_[:], tabs[d][h][:], ALU.mult, [y_, tabs[d][h]], [o_])
                                    P.dma("sp", dsts[d][h * 2 + half, :, t0:t0 + W], o_[:], reads=[o_])
                    for vg in range(2):
                        for cg in range(8):
                            wp = wpc[cg % 2]
                            col0 = 2 * D + vg * 2 * D + cg * 512
                            P.dma("sp", wp[:], Wb_vg[vg * 8 + cg], writes=[wp])
                            for tt_ in range(2):
                                pp = ps()
                                for kt in range(KT):
                                    mm(pp[:], h_[:, kt, tt_ * 128:(tt_ + 1) * 128], wp[:, kt, :], kt == 0, kt == KT - 1, [h_, wp], [pp])
                                rs = slice(t0 + tt_ * 128, t0 + (tt_ + 1) * 128)
                                if vg == 0:
                                    cp("act", vst[tt_][:], pp[:], [pp], [vst[tt_]])
                                    P.dma("sp", V1[rs, cg * 512:(cg + 1) * 512], vst[tt_][:], reads=[vst[tt_]])
                                else:
                                    act(gst[tt_][:], pp[:], AF.Silu, [pp], [gst[tt_]])
                                    P.dma("sp", SG1[rs, cg * 512:(cg + 1) * 512], gst[tt_][:], reads=[gst[tt_]])
            P.barrier()

        def phase_s1(d):
            with ExitStack() as ph:
                S = ph.enter_context
                dl = sb("s1_dl", [128, 16], stack=S)
                P.dma("sp", dl[:], rt_dl.partition_broadcast(128), writes=[dl])
                gc = sb("s1_gc", [128, 16], stack=S)
                act(gc[:], dl[:], AF.Exp, [dl], [gc], scale=-1.0)
                act(gc[:], gc[:], AF.Ln, [gc], [gc], bias=1.0)
                act(gc[:], gc[:], AF.Exp, [gc], [gc], scale=-float(C1))
                qt = [sb(f"s1_q{i}", [128, KT, C1], BF16, stack=S) for i in range(2)]
                kt_ = [sb(f"s1_k{i}", [128, KT, C1], BF16, stack=S) for i in range(2)]
                vc = [sb(f"s1_v{i}", [128, 2 * D], BF16, stack=S) for i in range(2)]
                R32 = [sb(f"s1_R32_{h}", [128, 2, 512], stack=S) for h in range(8)]
                Rb = [sb(f"s1_Rb_{h}", [128, 2, 512], BF16, stack=S) for h in range(8)]
                for h_ in range(8):
                    memset("pool", R32[h_][:], 0.0, [R32[h_]]); memset("pool", Rb[h_][:], 0.0, [Rb[h_]])
                Sb = [sb(f"s1_S{i}", [128, 128], BF16, stack=S) for i in range(8)]
                ktok = [sb(f"s1_kt{i}", [128, 256], BF16, stack=S) for i in range(8)]
                ost = [sb(f"s1_o{i}", [128, 512], stack=S) for i in range(8)]
                msk = sb("s1_msk", [128, 128], stack=S)
                cp("dve", msk[:], (triU if d == 0 else triLs)[:], [triU, triLs], [msk])
                nch = T // C1
                order = list(range(nch)) if d == 0 else [1, 0] + list(range(nch - 1, 1, -1))

                def hbody(h, q_, k_, v_, t0):
                    pS = ps()
                    for half in range(2):
                        mm(pS[:, 0:128], k_[:, h * 2 + half, :], q_[:, h * 2 + half, :], half == 0, half == 1, [k_, q_], [pS])
                    tt("dve", Sb[h][:], pS[:, 0:128], msk[:], ALU.mult, [pS, msk], [Sb[h]])
                    pT = ps()
                    for half in range(2):
                        mm(pT[:, half * 128:(half + 1) * 128], k_[:, h * 2 + half, :], identb[:], True, True, [k_, identb], [pT])
                    cp("act", ktok[h][:], pT[:, 0:256], [pT], [ktok[h]])
                    yield
                    pO = ps()
                    mm(pO[:], Sb[h][:], v_[:, h * 512:(h + 1) * 512], True, False, [Sb[h], v_], [pO])
                    for half in range(2):
                        mm(pO[:], q_[:, h * 2 + half, :], Rb[h][:, half, :], False, half == 1, [q_, Rb[h]], [pO])
                    cp("act", ost[h][:], pO[:], [pO], [ost[h]])
                    P.dma("sp", O1[d][t0:t0 + C1, h * 512:(h + 1) * 512], ost[h][:], reads=[ost[h]])
                    yield
                    for half in range(2):
                        pR = ps()
                        mm(pR[:], ktok[h][:, half * 128:(half + 1) * 128], v_[:, h * 512:(h + 1) * 512], True, True, [ktok[h], v_], [pR])
                        tt("dve", R32[h][:, half, :], R32[h][:, half, :], pR[:], ALU.add, [R32[h], pR], [R32[h]])
                        ts("pool", R32[h][:, half, :], R32[h][:, half, :], gc[:, d * 8 + h:d * 8 + h + 1], None, ALU.mult, None, [R32[h], gc], [R32[h]])
                        cp("act", Rb[h][:, half, :], R32[h][:, half, :], [R32[h]], [Rb[h]])
                    yield

                for ci, c in enumerate(order):
                    t0 = c * C1
                    b2 = ci % 2
                    q_ = qt[b2]; k_ = kt_[b2]; v_ = vc[b2]
                    P.dma("sp", q_[:], QT[d][:, :, t0:t0 + C1].rearrange("k p t -> p k t"), writes=[q_])
                    P.dma("sp", k_[:], KTT[d][:, :, t0:t0 + C1].rearrange("k p t -> p k t"), writes=[k_])
                    P.dma("sp", v_[:], V1[t0:t0 + C1, :], writes=[v_])
                    gens = [hbody(h, q_, k_, v_, t0) for h in range(8)]
                    while gens:
                        for gen in list(gens):
                            try:
                                next(gen)
                            except StopIteration:
                                gens.remove(gen)
            P.barrier()

        def phase_o1():
            with ExitStack() as ph:
                S = ph.enter_context
                GNG = sb("o1_gng", [128, 2 * D], stack=S); TG = sb("o1_TG", [128, D], stack=S); FG = sb("o1_FG", [128, D], stack=S)
                P.dma("sp", GNG[:], rt_gn_g.partition_broadcast(128), writes=[GNG])
                P.dma("sp", TG[:], ADA[1, 0, 2, :].partition_broadcast(128), writes=[TG])
                P.dma("sp", FG[:], final_g.partition_broadcast(128), writes=[FG])
                of = [sb(f"o1_of{i}", [128, 2 * D], stack=S) for i in range(1)] * 2
                ob = sb("o1_ob", [128, 2 * D], stack=S)
                sg = sb("o1_sg", [128, 2 * D], stack=S)
                x_b = [sb(f"o1_x{i}", [128, D], stack=S) for i in range(2)]
                st = [sb(f"o1_st{i}", [128, 4, 8], stack=S) for i in range(2)]
                yT = sb("o1_yT", [128, 32, 256], BF16, stack=S)
                wp = [sb(f"o1_wp{i}", [128, 32, 256], BF16, stack=S) for i in range(2)]
                junk = sb("o1_junk", [128, D], stack=S)
                ybf1 = sb("o1_ybf", [128, 2 * D], BF16, stack=S)
                wi = [0]
                for pi in range((NTILE - 2) // 2):
                    tiles = [2 + 2 * pi, 3 + 2 * pi]
                    for j, i in enumerate(tiles):
                        rs = slice(i * 128, (i + 1) * 128)
                        o = of[j]; x_ = x_b[j]; s_ = st[j]
                        P.dma("sp", o[:], O1[0][rs, :], writes=[o])
                        P.dma("sp", ob[:], O1[1][rs, :], writes=[ob])
                        P.dma("sp", sg[:], SG1[rs, :], writes=[sg])
                        P.dma("sp", x_[:], X1[rs, :], writes=[x_])
                        tt("dve", o[:], o[:], ob[:], ALU.add, [o, ob], [o])
                        tt("pool", ob[:], o[:], o[:], ALU.mult, [o], [ob])
                        red("dve", s_[:, 0, :], ob[:].rearrange("p (h v) -> p h v", v=512), [ob], [s_])
                        ts("dve", s_[:, 1, :], s_[:, 0, :], 1.0 / 512, EPS, ALU.mult, ALU.add, [s_], [s_])
                        act(s_[:, 2, :], s_[:, 1, :], AF.Sqrt, [s_], [s_])
                        recip(s_[:, 3, :], s_[:, 2, :], [s_], [s_])
                        o3 = o[:].rearrange("p (h v) -> p h v", v=512)
                        tt("dve", o3, o3, s_[:, 3, :].unsqueeze(2).to_broadcast([128, 8, 512]), ALU.mult, [o, s_], [o])
                        tt("pool", o[:], o[:], GNG[:], ALU.mult, [o, GNG], [o])
                        tt("dve", o[:], o[:], sg[:], ALU.mult, [o, sg], [o])
                        cp("act", ybf1[:], o[:], [o], [ybf1])
                        for g in range(8):
                            pp = ps()
                            for q in range(4):
                                kt = g * 4 + q
                                mm(pp[:, q * 128:(q + 1) * 128], ybf1[:, kt * 128:(kt + 1) * 128], identb[:], True, True, [ybf1, identb], [pp])
                            cp("act" if g % 2 else "dve", yT[:, g * 4:(g + 1) * 4, j * 128:(j + 1) * 128], pp[:].rearrange("p (q t) -> p q t", q=4), [pp], [yT])
                    for cg in range(8):
                        w_ = wp[wi[0] % 2]; wi[0] += 1
                        P.dma("sp", w_[:], Wb_o1[cg], writes=[w_])
                        cs_ = slice(cg * 256, (cg + 1) * 256)
                        for j in range(2):
                            x_ = x_b[j]
                            pp = ps()
                            for kt in range(32):
                                mm(pp[:, 0:256], yT[:, kt, j * 128:(j + 1) * 128], w_[:, kt, :], kt == 0, kt == 31, [yT, w_], [pp])
                            tt("dve", junk[:, cs_], pp[:, 0:256], TG[:, cs_], ALU.mult, [pp, TG], [junk])
                            tt("pool", x_[:, cs_], x_[:, cs_], junk[:, cs_], ALU.add, [x_, junk], [x_])
                    for j, i in enumerate(tiles):
                        x_ = x_b[j]; s_ = st[j]
                        act(junk[:], x_[:], AF.Square, [x_], [junk, s_], accum=s_[:, 0, 0:1])
                        ts("dve", s_[:, 0, 1:2], s_[:, 0, 0:1], 1.0 / D, EPS, ALU.mult, ALU.add, [s_], [s_])
                        act(s_[:, 0, 2:3], s_[:, 0, 1:2], AF.Sqrt, [s_], [s_])
                        recip(s_[:, 0, 3:4], s_[:, 0, 2:3], [s_], [s_])
                        stt("dve", x_[:], x_[:], s_[:, 0, 3:4], FG[:], ALU.mult, ALU.mult, [x_, s_, FG], [x_])
                        P.dma("sp", yout[(i - 2) * 128:(i - 1) * 128, :], x_[:], reads=[x_], is_output=True)

        P.barrier()
        stages = [
            ("ada0", lambda: adaln(0)),
            ("h0", lambda: phase_h(0, xin, HT0, F32)),
            ("p0", phase_p0),
            ("s0f", lambda: phase_s0(0)),
            ("s0b", lambda: phase_s0(1)),
            ("o0", lambda: (issue_cast1(100), phase_o0())),
            ("ada1", lambda: adaln(1)),
            ("h1", lambda: phase_h(1, X1, HT1, BF16)),
            ("p1", phase_p1),
            ("s1f", lambda: phase_s1(0)),
            ("s1b", lambda: phase_s1(1)),
            ("o1", phase_o1),
        ]
        for name, fn in stages:
            fn()
            if STOP_AFTER == name:
                break
        P.emit(E)
    return nc


def rope_tables():
    t = np.arange(NLAT)
    row = (t // 64).astype(np.float32)
    col = (t % 64).astype(np.float32)
    nf = 64
    inv = (10000.0 ** (-np.arange(nf, dtype=np.float32) / nf)).astype(np.float32)
    ang = np.concatenate([row[:, None] * inv, col[:, None] * inv], axis=-1).astype(np.float32)
    return np.ascontiguousarray(np.cos(ang).T.astype(np.float32)), np.ascontiguousarray(np.sin(ang).T.astype(np.float32))


def make_in_maps(x, c, ctx, c_ctx, ada_w, ada_b, norm_g, rk_mix, rk_w_in, rk_w0, rk_w1, rk_w2, rk_a0, rk_a1, rk_a2,
                 rk_k_k, rk_k_a, rk_r_k, rk_ln_g, rk_ln_b, rk_w_out, rt_w_in, rt_decay_logit, rt_gn_g, rt_w_out, final_g):
    f = lambda a: np.ascontiguousarray(np.asarray(a, dtype=np.float32))
    rc, rs_ = rope_tables()
    shared = dict(ada_w=f(ada_w), ada_b=f(ada_b), norm_g=f(norm_g), rk_mix=f(rk_mix)[0], rk_w_in=f(rk_w_in)[0],
                  rk_w0=f(rk_w0)[0], rk_w1=f(rk_w1)[0], rk_w2=f(rk_w2)[0], rk_a0=f(rk_a0)[0], rk_a1=f(rk_a1)[0],
                  rk_a2=f(rk_a2)[0], rk_k_k=f(rk_k_k)[0], rk_k_a=f(rk_k_a)[0], rk_r_k=f(rk_r_k)[0].reshape(-1),
                  rk_ln_g=f(rk_ln_g)[0], rk_ln_b=f(rk_ln_b)[0], rk_w_out=f(rk_w_out)[0], rt_w_in=f(rt_w_in)[0],
                  rt_dl=f(rt_decay_logit)[0].reshape(-1), rt_gn_g=f(rt_gn_g)[0], rt_w_out=f(rt_w_out)[0],
                  final_g=f(final_g), ropec=rc, ropes=rs_)
    maps = []
    for core in range(8):
        b = core % 4
        m = dict(shared)
        m["xin"] = np.ascontiguousarray(np.concatenate([f(ctx)[b], f(x)[b]], axis=0))
        m["cvec"] = np.ascontiguousarray(np.stack([f(c)[b], f(c_ctx)], axis=0))
        maps.append(m)
    return maps


def kernel(**inputs):
    maps = make_in_maps(**inputs)[:NCORES]
    nc = build()
    res = run_bass_kernel_spmd(nc, maps, core_ids=list(range(NCORES)))
    out = np.stack([np.asarray(res.results[b]["yout"], dtype=np.float32) for b in range(4)], axis=0)
    return out
```

```python
import math
from contextlib import ExitStack
import numpy as np
import concourse.bass as bass
import concourse.mybir as mybir
from concourse.bass_utils import run_bass_kernel_spmd

F32 = mybir.dt.float32
BF16 = mybir.dt.bfloat16
ALU = mybir.AluOpType
AF = mybir.ActivationFunctionType
AX = mybir.AxisListType

ENGS = ["pe", "act", "dve", "pool", "sp"]
SIG_LIM = 30000


class Buf:
    __slots__ = ("name", "t", "w", "r", "excl")

    def __init__(self, name, t, excl=False):
        self.name = name
        self.t = t
        self.w = {}
        self.r = {}
        self.excl = excl

    def __getitem__(self, idx):
        return self.t[idx]


class Op:
    __slots__ = ("eng", "fn", "deps", "signal", "sig_no", "dma", "sem_i", "val")

    def __init__(self, eng, fn, dma):
        self.eng = eng
        self.fn = fn
        self.deps = []
        self.signal = False
        self.sig_no = 0
        self.dma = dma
        self.sem_i = 0
        self.val = 0


class Prog:
    def __init__(self, nc):
        self.nc = nc
        self.ops = {e: [] for e in ENGS}
        self.n_dma = {e: 0 for e in ENGS}
        self.n_dma_sems = {"sp": 40, "pool": 4, "act": 8, "pe": 1, "dve": 1}
        self.out_dmas = []
        self.dma_since = {e: [] for e in ENGS}
        self.bar_bufs = None

    def add(self, eng, fn, reads=(), writes=(), dma=False, is_output=False):
        op = Op(eng, fn, dma)
        key = op if dma else eng
        deps = {}
        for b in reads:
            for k, w in b.w.items():
                deps[id(w)] = w
            if b.excl:
                for k, r in b.r.items():
                    if k != eng:
                        deps[id(r)] = r
        for b in writes:
            for k, w in b.w.items():
                if dma or k != eng:
                    deps[id(w)] = w
            for k, r in b.r.items():
                if dma or k != eng:
                    deps[id(r)] = r
        for b in reads:
            b.r[key] = op
        for b in writes:
            b.w = {key: op}
            b.r = {}
        op.deps = list(deps.values())
        for d in op.deps:
            d.signal = True
        if dma:
            ns = self.n_dma_sems[eng]
            i = self.n_dma[eng]
            self.n_dma[eng] += 1
            op.sem_i = i % ns
            op.val = 16 * (i // ns + 1)
            op.signal = True
            self.dma_since[eng].append(op)
            if len(self.dma_since[eng]) > ns:
                self.dma_since[eng] = self.dma_since[eng][-ns:]
            if is_output:
                self.out_dmas.append(op)
        self.ops[eng].append(op)
        return op

    def dma(self, q, out_ap, in_ap, reads=(), writes=(), is_output=False, slow=False):
        if slow:
            return self.add(q, lambda e: e.dma_start(out=out_ap, in_=in_ap, allow_slow_non_contiguous=True), reads, writes,
                            dma=True, is_output=is_output)
        return self.add(q, lambda e: e.dma_start(out=out_ap, in_=in_ap), reads, writes, dma=True,
                        is_output=is_output)

    def barrier(self):
        bb = self.bar_bufs
        firsts = []
        for e in ENGS:
            if e == "sp":
                op = self.add("sp", lambda q: q.dma_start(out=bb["sp"][0:1, 0:4], in_=bb["spsrc"][0:1, 0:4]),
                              writes=[bb["sp"]], dma=True)
                for q in ENGS:
                    for d in self.dma_since[q]:
                        if d is not op:
                            op.deps.append(d)
                            d.signal = True
            elif e == "pe":
                op = self.add("pe", lambda t: t.matmul(bb["pe"][0:1, 0:1], lhsT=bb["pesrc"][0:1, 0:1],
                                                        rhs=bb["pesrc"][0:1, 0:1], start=True, stop=True),
                              writes=[bb["pe"]])
            else:
                b = bb[e]
                if e == "act":
                    op = self.add(e, (lambda b: (lambda g: g.memzero(b[0:1, 0:4])))(b), writes=[b])
                else:
                    op = self.add(e, (lambda b: (lambda g: g.memset(b[0:1, 0:4], 0.0)))(b), writes=[b])
            firsts.append(op)
        for e in ENGS:
            if e == "sp":
                op = self.add("sp", lambda q: q.dma_start(out=bb["sp2"][0:1, 0:4], in_=bb["spsrc"][0:1, 0:4]),
                              writes=[bb["sp2"]], dma=True)
            elif e == "pe":
                op = self.add("pe", lambda t: t.matmul(bb["pe"][0:1, 1:2], lhsT=bb["pesrc"][0:1, 0:1],
                                                        rhs=bb["pesrc"][0:1, 0:1], start=True, stop=True),
                              writes=[])
            else:
                b = bb[e + "2"]
                if e == "act":
                    op = self.add(e, (lambda b: (lambda g: g.memzero(b[0:1, 0:4])))(b), writes=[b])
                else:
                    op = self.add(e, (lambda b: (lambda g: g.memset(b[0:1, 0:4], 0.0)))(b), writes=[b])
            for f in firsts:
                if f.eng != e:
                    op.deps.append(f)
                    f.signal = True
        for q in ENGS:
            self.dma_since[q] = []

    def emit(self, E):
        nc = self.nc
        nsig = {}
        for e in ENGS:
            n = 0
            for op in self.ops[e]:
                if op.signal and not op.dma:
                    n += 1
                    op.sig_no = n
            nsig[e] = n
        esem = {}
        for e in ENGS:
            k = max(1, (nsig[e] + SIG_LIM - 1) // SIG_LIM)
            esem[e] = [E(nc.semaphore(f"s_{e}_{j}")) for j in range(k)]
        dsem = {}
        for e in ENGS:
            if self.n_dma[e] > 0:
                dsem[e] = [E(nc.semaphore(f"d_{e}_{j}")) for j in range(self.n_dma_sems[e])]
        block = E(nc.Block())
        engobj = {"pe": "tensor", "act": "scalar", "dve": "vector", "pool": "gpsimd", "sp": "sync"}

        def dep_wait(op):
            if op.dma:
                return dsem[op.eng][op.sem_i], op.val, ("d", op.eng, op.sem_i)
            ep = (op.sig_no - 1) // SIG_LIM
            return esem[op.eng][ep], op.sig_no - ep * SIG_LIM, ("e", op.eng, ep)

        out_dmas = self.out_dmas

        def make_body(e):
            def body(eng):
                waited = {}
                for op in self.ops[e]:
                    for d in op.deps:
                        sem, val, key = dep_wait(d)
                        if waited.get(key, 0) >= val:
                            continue
                        waited[key] = val
                        eng.wait_ge(sem, val)
                    if op.dma and op.val > 16:
                        key = ("d", e, op.sem_i)
                        if waited.get(key, 0) < op.val - 16:
                            waited[key] = op.val - 16
                            eng.wait_ge(dsem[e][op.sem_i], op.val - 16)
                    inst = op.fn(eng)
                    if op.dma:
                        inst.then_inc(dsem[e][op.sem_i], 16)
                    elif op.signal:
                        ep = (op.sig_no - 1) // SIG_LIM
                        inst.then_inc(esem[e][ep], 1)
                if e == "sp":
                    for d in out_dmas:
                        sem, val, key = dep_wait(d)
                        if waited.get(key, 0) >= val:
                            continue
                        waited[key] = val
                        eng.wait_ge(sem, val)
            return body

        for e in ENGS:
            if self.ops[e] or e == "sp":
                getattr(block, engobj[e])(make_body(e))


D = 2048
KT = 16
NCTX = 256
NLAT = 4096
T = NCTX + NLAT
NTILE = T // 128
EPS = 1e-6
GN_EPS = 64e-5
LORA = 96
C0 = 64
C1 = 128
EXPM05 = math.exp(-0.5)

DEBUG_OUT = set()
NCORES = 4
import os as _os
STOP_AFTER = _os.environ.get('KSTOP') or None


def build():
    nc = bass.Bass("TRN2", target_bir_lowering=False)

    def din(name, shape):
        return nc.dram_tensor(name, list(shape), F32, kind="ExternalInput").ap()

    def scratch(name, shape, dt=F32):
        kind = "ExternalOutput" if name in DEBUG_OUT else "Internal"
        return nc.dram_tensor(name, list(shape), dt, kind=kind).ap()

    xin = din("xin", [T, D])
    cvec = din("cvec", [2, D])
    ada_w = din("ada_w", [2, D, 3 * D])
    ada_b = din("ada_b", [2, 3 * D])
    norm_g = din("norm_g", [2, D])
    rk_mix = din("rk_mix", [6, D])
    rk_w_in = din("rk_w_in", [4, D, D])
    rk_w0 = din("rk_w0", [2, D])
    rk_w1 = din("rk_w1", [2, D, LORA])
    rk_w2 = din("rk_w2", [2, LORA, D])
    rk_a0 = din("rk_a0", [2, D])
    rk_a1 = din("rk_a1", [2, D, LORA])
    rk_a2 = din("rk_a2", [2, LORA, D])
    rk_k_k = din("rk_k_k", [D])
    rk_k_a = din("rk_k_a", [D])
    rk_r_k = din("rk_r_k", [D])
    rk_ln_g = din("rk_ln_g", [D])
    rk_ln_b = din("rk_ln_b", [D])
    rk_w_out = din("rk_w_out", [D, D])
    rt_w_in = din("rt_w_in", [D, 6 * D])
    rt_dl = din("rt_dl", [16])
    rt_gn_g = din("rt_gn_g", [2 * D])
    rt_w_out = din("rt_w_out", [2 * D, D])
    final_g = din("final_g", [D])
    ropec = din("ropec", [128, NLAT])
    ropes = din("ropes", [128, NLAT])
    yout = nc.dram_tensor("yout", [NLAT, D], F32, kind="ExternalOutput").ap()

    Wb_r = scratch("Wb_r", [16, 128, KT, 128], BF16)
    Wb_k = scratch("Wb_k", [16, 128, KT, 128], BF16)
    Wb_v = scratch("Wb_v", [8, 128, KT, 256], BF16)
    Wb_g = scratch("Wb_g", [8, 128, KT, 256], BF16)
    Wb_out0 = scratch("Wb_out0", [D, D], BF16)
    Wb_qk = scratch("Wb_qk", [32, 128, KT, 128], BF16)
    Wb_vg = scratch("Wb_vg", [16, 128, KT, 512], BF16)
    Wb_o1 = scratch("Wb_o1", [8, 128, 32, 256], BF16)
    ADA = scratch("ADA", [2, 2, 3, D])
    HT0 = scratch("HT0", [KT, 128, T])
    HT1 = scratch("HT1", [KT, 128, T], BF16)
    RT = scratch("RT", [KT, 128, T])
    KKT = scratch("KKT", [KT, 128, T])
    KDT = [scratch(f"KDT{d}", [KT, 128, T]) for d in range(2)]
    BT = [scratch(f"BT{d}", [KT, 128, T]) for d in range(2)]
    LW = [scratch(f"LW{d}", [T, D]) for d in range(2)]
    V0 = scratch("V0", [T, D])
    SG0 = scratch("SG0", [T, D])
    BON = scratch("BON", [T, 32])
    O0 = [scratch(f"O0_{d}", [T, D]) for d in range(2)]
    X1 = scratch("X1", [T, D])
    QT = [scratch(f"QT{d}", [KT, 128, T], BF16) for d in range(2)]
    KTT = [scratch(f"KTT{d}", [KT, 128, T], BF16) for d in range(2)]
    V1 = scratch("V1", [T, 2 * D], BF16)
    SG1 = scratch("SG1", [T, 2 * D])
    O1 = [scratch(f"O1_{d}", [T, 2 * D]) for d in range(2)]

    with ExitStack() as es:
        E = es.enter_context
        P = Prog(nc)
        cnt = [0]

        def sb(name, shape, dt=F32, stack=None):
            cnt[0] += 1
            return Buf(name, (stack or E)(nc.sbuf_tensor(f"{name}_{cnt[0]}", list(shape), dt)))

        P.bar_bufs = {k: sb("bar_" + k, [1, 8]) for k in ["sp", "sp2", "spsrc", "act", "act2", "dve", "dve2", "pool", "pool2"]}
        P.bar_bufs["pesrc"] = sb("bar_pesrc", [1, 8])
        PS = [Buf(f"ps{i}", E(nc.psum_tensor(f"ps{i}", [128, 512], F32)), excl=True) for i in range(7)]
        P.bar_bufs["pe"] = Buf("ps_bar", E(nc.psum_tensor("ps_bar", [128, 512], F32)))
        P.add("pool", lambda e: e.memset(P.bar_bufs["pesrc"][:], 0.0), writes=[P.bar_bufs["pesrc"]])
        P.add("pool", lambda e: e.memset(P.bar_bufs["spsrc"][:], 0.0), writes=[P.bar_bufs["spsrc"]])
        psi = [0]

        psn = [7]

        def ps():
            psi[0] = (psi[0] + 1) % psn[0]
            return PS[psi[0]]

        def tt(eng, out, in0, in1, op, R, W):
            P.add(eng, lambda e: e.tensor_tensor(out=out, in0=in0, in1=in1, op=op), reads=R, writes=W)

        def ts(eng, out, in0, s1, s2, op0, op1, R, W):
            if s2 is None:
                P.add(eng, lambda e: e.tensor_scalar(out=out, in0=in0, scalar1=s1, scalar2=None, op0=op0), reads=R, writes=W)
            else:
                P.add(eng, lambda e: e.tensor_scalar(out=out, in0=in0, scalar1=s1, scalar2=s2, op0=op0, op1=op1), reads=R, writes=W)

        def stt(eng, out, in0, s, in1, op0, op1, R, W):
            P.add(eng, lambda e: e.scalar_tensor_tensor(out=out, in0=in0, scalar=s, in1=in1, op0=op0, op1=op1), reads=R, writes=W)

        def act(out, in_, func, R, W, bias=None, scale=None, accum=None):
            kw = {}
            if bias is not None:
                kw["bias"] = bias
            if scale is not None:
                kw["scale"] = scale
            if accum is not None:
                kw["accum_out"] = accum
            P.add("act", lambda e: e.activation(out=out, in_=in_, func=func, **kw), reads=R, writes=W)

        def cp(eng, out, in_, R, W):
            if eng == "act":
                P.add("act", lambda e: e.copy(out=out, in_=in_), reads=R, writes=W)
            else:
                P.add(eng, lambda e: e.tensor_copy(out=out, in_=in_), reads=R, writes=W)

        def mm(out, lhsT, rhs, start, stop, R, W):
            P.add("pe", lambda e: e.matmul(out, lhsT=lhsT, rhs=rhs, start=start, stop=stop), reads=R, writes=W)

        def tr(out, in_, ident, R, W):
            P.add("pe", lambda e: e.transpose(out, in_, ident), reads=R, writes=W)

        def memset(eng, ap, val, W):
            P.add(eng, lambda e: e.memset(ap, val), writes=W)

        def red(eng, out, in_, R, W):
            P.add(eng, lambda e: e.tensor_reduce(out=out, in_=in_, axis=AX.X, op=ALU.add), reads=R, writes=W)

        def recip(out, in_, R, W):
            P.add("dve", lambda e: e.reciprocal(out=out, in_=in_), reads=R, writes=W)

        def ftv(ap1d):
            return ap1d.rearrange("(k p) -> p k", p=128)

        ident = sb("ident", [128, 128])
        identb = sb("identb", [128, 128], BF16)
        P.add("pool", lambda e: e.memset(ident[:], 1.0), writes=[ident])
        P.add("pool", lambda e: e.affine_select(out=ident[:], in_=ident[:], pattern=[[-1, 128]], compare_op=ALU.is_equal,
                                                fill=0.0, base=0, channel_multiplier=1), reads=[ident], writes=[ident])
        cp("dve", identb[:], ident[:], [ident], [identb])
        triU = sb("triU", [128, 128]); triUs = sb("triUs", [128, 128]); triL = sb("triL", [128, 128]); triLs = sb("triLs", [128, 128])

        def mk_tri(tb, cmp_, sg):
            P.add("pool", lambda e: e.memset(tb[:], 1.0), writes=[tb])
            P.add("pool", lambda e: e.affine_select(out=tb[:], in_=tb[:], pattern=[[sg, 128]], compare_op=cmp_,
                                                    fill=0.0, base=0, channel_multiplier=-sg), reads=[tb], writes=[tb])
        mk_tri(triU, ALU.is_ge, 1)
        mk_tri(triUs, ALU.is_gt, 1)
        mk_tri(triL, ALU.is_ge, -1)
        mk_tri(triLs, ALU.is_gt, -1)
        blk = sb("blk", [128, 128])
        P.add("pool", lambda e: e.memset(blk[:], 0.0), writes=[blk])
        P.add("pool", lambda e: e.memset(blk[0:64, 0:64], 1.0), writes=[blk])
        P.add("pool", lambda e: e.memset(blk[64:128, 64:128], 1.0), writes=[blk])
        blkb = sb("blkb", [128, 128], BF16)
        cp("dve", blkb[:], blk[:], [blk], [blkb])
        triUsb = sb("triUsb", [128, 128], BF16)
        cp("dve", triUsb[:], triUs[:], [triUs], [triUsb])
        tribI = []; tribS = []
        for d_, (si, ss) in enumerate(((triU, triUs), (triL, triLs))):
            bi = sb(f"tribI{d_}", [64, 64], BF16); bs = sb(f"tribS{d_}", [64, 64], BF16)
            cp("dve", bi[:], si[0:64, 0:64], [si], [bi]); cp("dve", bs[:], ss[0:64, 0:64], [ss], [bs])
            tribI.append(bi); tribS.append(bs)
        ind2 = sb("ind2", [128, 2])
        P.add("pool", lambda e: e.memset(ind2[:], 0.0), writes=[ind2])
        P.add("pool", lambda e: e.memset(ind2[0:64, 0:1], 1.0), writes=[ind2])
        P.add("pool", lambda e: e.memset(ind2[64:128, 1:2], 1.0), writes=[ind2])
        mS4 = [sb(f"mS4_{d}", [128, 4, 128]) for d in range(2)]
        mI4 = [sb(f"mI4_{d}", [128, 4, 128]) for d in range(2)]
        for d in range(2):
            srcS = triUs if d == 0 else triLs
            srcI = triU if d == 0 else triL
            for r_ in range(4):
                tt("pool", mS4[d][:, r_, :], srcS[:], blk[:], ALU.mult, [srcS, blk], [mS4[d]])
                tt("pool", mI4[d][:, r_, :], srcI[:], blk[:], ALU.mult, [srcI, blk], [mI4[d]])

        def pv(w2d, c0, e):
            return w2d[:, c0:c0 + e].rearrange("(k p) e -> p k e", p=128)
        for t_ in range(16):
            P.dma("pool", Wb_r[t_], pv(rk_w_in[0], t_ * 128, 128))
            P.dma("pool", Wb_k[t_], pv(rk_w_in[1], t_ * 128, 128))
        for t_ in range(8):
            P.dma("pool", Wb_v[t_], pv(rk_w_in[2], t_ * 256, 256))
            P.dma("pool", Wb_g[t_], pv(rk_w_in[3], t_ * 256, 256))
        for r_ in range(4):
            P.dma("pool", Wb_out0[r_ * 512:(r_ + 1) * 512, :], rk_w_out[r_ * 512:(r_ + 1) * 512, :])
        cast1 = []
        for t_ in range(32):
            cast1.append((Wb_qk[t_], pv(rt_w_in, t_ * 128, 128)))
        for t_ in range(16):
            cast1.append((Wb_vg[t_], pv(rt_w_in, 2 * D + t_ * 512, 512)))
        for t_ in range(8):
            cast1.append((Wb_o1[t_], pv(rt_w_out, t_ * 256, 256)))
        cast1_it = iter(cast1)

        def issue_cast1(n=1):
            for _ in range(n):
                nx = next(cast1_it, None)
                if nx is not None:
                    P.dma("pool", nx[0], nx[1])

        def adaln(layer):
            with ExitStack() as ph:
                cs = sb("cs", [128, KT, 2], stack=ph.enter_context)
                cst = sb("cst", [128, 2, KT], stack=ph.enter_context)
                for r_ in range(2):
                    P.dma("sp", cst[:, r_, :], ftv(cvec[r_, :]), writes=[cst], slow=True)
                act(cst[:], cst[:], AF.Silu, [cst], [cst])
                cp("dve", cs[:].rearrange("p k r -> p r k"), cst[:], [cst], [cs])
                wts = [sb(f"adaw{i}", [128, KT, 512], stack=ph.enter_context) for i in range(2)]
                bia = [sb(f"adab{i}", [2, 512], stack=ph.enter_context) for i in range(2)]
                ng = sb("ng", [2, 512], stack=ph.enter_context)
                res = [sb(f"adar{i}", [2, 512], stack=ph.enter_context) for i in range(2)]
                for cg in range(12):
                    w_ = wts[cg % 2]; b_ = bia[cg % 2]; r_ = res[cg % 2]
                    P.dma("sp", w_[:], ada_w[layer, :, cg * 512:(cg + 1) * 512].rearrange("(k p) e -> p k e", p=128), writes=[w_])
                    P.dma("sp", b_[:], ada_b[layer, cg * 512:(cg + 1) * 512].partition_broadcast(2), writes=[b_])
                    pp = ps()
                    for kt in range(KT):
                        mm(pp[0:2, :], cs[:, kt, :], w_[:, kt, :], kt == 0, kt == KT - 1, [cs, w_], [pp])
                    tt("dve", r_[:], pp[0:2, :], b_[:], ALU.add, [pp, b_], [r_])
                    which = cg // 4
                    if which == 1:
                        c4 = cg % 4
                        P.dma("sp", ng[:], norm_g[layer, c4 * 512:(c4 + 1) * 512].partition_broadcast(2), writes=[ng])
                        stt("dve", r_[:], r_[:], 1.0, ng[:], ALU.add, ALU.mult, [r_, ng], [r_])
                    c4 = cg % 4
                    for row in range(2):
                        P.dma("sp", ADA[layer, row, which, c4 * 512:(c4 + 1) * 512].unsqueeze(0), r_[row:row + 1, :], reads=[r_])
            P.barrier()

        def phase_h(layer, src, HTdst, hdt):
            with ExitStack() as ph:
                S = ph.enter_context
                TA = sb("TA", [128, D], stack=S); TB = sb("TB", [128, D], stack=S)
                xs = [sb(f"hx{i}", [128, D], stack=S) for i in range(2)]
                hs = [sb(f"hh{i}", [128, D], stack=S) for i in range(2)]
                hts = [sb(f"hT{i}", [128, KT, 128], hdt, stack=S) for i in range(2)]
                junk = sb("junk", [128, D], stack=S)
                st = [sb(f"hst{i}", [128, 4], stack=S) for i in range(2)]
                for i in range(NTILE):
                    row = 1 if i < 2 else 0
                    if i == 0 or i == 2:
                        P.dma("sp", TA[:], ADA[layer, row, 1, :].partition_broadcast(128), writes=[TA])
                        P.dma("sp", TB[:], ADA[layer, row, 0, :].partition_broadcast(128), writes=[TB])
                    x_ = xs[i % 2]; h_ = hs[i % 2]; hT = hts[i % 2]; s_ = st[i % 2]
                    P.dma("sp", x_[:], src[i * 128:(i + 1) * 128, :], writes=[x_])
                    act(junk[:], x_[:], AF.Square, [x_], [junk, s_], accum=s_[:, 0:1])
                    ts("dve", s_[:, 1:2], s_[:, 0:1], 1.0 / D, EPS, ALU.mult, ALU.add, [s_], [s_])
                    act(s_[:, 2:3], s_[:, 1:2], AF.Sqrt, [s_], [s_])
                    recip(s_[:, 3:4], s_[:, 2:3], [s_], [s_])
                    stt("dve", h_[:], x_[:], s_[:, 3:4], TA[:], ALU.mult, ALU.mult, [x_, s_, TA], [h_])
                    tt("pool", h_[:], h_[:], TB[:], ALU.add, [h_, TB], [h_])
                    for g in range(4):
                        pp = ps()
                        for q in range(4):
                            kt = g * 4 + q
                            tr(pp[:, q * 128:(q + 1) * 128], h_[:, kt * 128:(kt + 1) * 128], ident[:], [h_, ident], [pp])
                        cp("act" if g % 2 == 0 else "dve", hT[:, g * 4:(g + 1) * 4, :],
                           pp[:].rearrange("p (q t) -> p q t", q=4), [pp], [hT])
                    P.dma("sp", HTdst[:, :, i * 128:(i + 1) * 128].rearrange("k p t -> p k t"), hT[:], reads=[hT])
            P.barrier()

        def phase_p0():
            with ExitStack() as ph:
                S = ph.enter_context
                W = 256
                HW = sb("HW", [128, KT, W + 128], stack=S)
                dT = sb("dT", [128, KT, W], stack=S)
                xm = {j: sb(f"xm{j}", [128, KT, W], BF16, stack=S) for j in ("r", "k", "t0")}
                xm["t1"] = xm["t0"]
                mixT = sb("mixT", [128, 6, KT], stack=S)
                for j in range(6):
                    P.dma("sp", mixT[:, j, :], ftv(rk_mix[j, :]), writes=[mixT], slow=True)
                prm = sb("prm", [128, 8, KT], stack=S)
                for i_, src_ in enumerate([rk_k_k, rk_k_a, rk_k_a, rk_r_k, rk_a0[0, :], rk_a0[1, :]]):
                    P.dma("sp", prm[:, i_, :], ftv(src_), writes=[prm], slow=True)
                ts("dve", prm[:, 2, :], prm[:, 2, :], -1.0, 1.0, ALU.mult, ALU.add, [prm], [prm])
                w1b = [sb(f"w1b{d}", [128, KT, LORA], BF16, stack=S) for d in range(2)]
                a1b = [sb(f"a1b{d}", [128, KT, LORA], BF16, stack=S) for d in range(2)]
                w2b = [sb(f"w2b{d}", [LORA, D], BF16, stack=S) for d in range(2)]
                a2b = [sb(f"a2b{d}", [LORA, D], BF16, stack=S) for d in range(2)]
                W0t = [sb(f"W0t{d}", [128, D], stack=S) for d in range(2)]
                for d in range(2):
                    P.dma("pool", w1b[d][:], rk_w1[d].rearrange("(k p) r -> p k r", p=128), writes=[w1b[d]])
                    P.dma("pool", a1b[d][:], rk_a1[d].rearrange("(k p) r -> p k r", p=128), writes=[a1b[d]])
                    P.dma("pool", w2b[d][:], rk_w2[d], writes=[w2b[d]])
                    P.dma("pool", a2b[d][:], rk_a2[d], writes=[a2b[d]])
                    P.dma("sp", W0t[d][:], rk_w0[d, :].partition_broadcast(128), writes=[W0t[d]])
                thT = [sb(f"thT{d}", [LORA, W], BF16, stack=S) for d in range(2)]
                xa1 = [sb(f"xa1{d}", [LORA, W], BF16, stack=S) for d in range(2)]
                big = [sb(f"big{i}", [128, D], stack=S) for i in range(2)]
                wpc = [sb(f"wpc{i}", [128, KT, 256], BF16, stack=S) for i in range(2)]
                wrk = [sb(f"wrk{i}", [128, KT, 128], BF16, stack=S) for i in range(4)]
                fm = {n: [sb(f"fm_{n}{i}", [128, W], stack=S) for i in range(1)] * 2 for n in
                      ("r", "k", "a0", "a1", "kkr", "sq", "rn", "kk", "t1", "kd0", "kd1", "b0", "b1", "ks", "rk")}
                bon = [sb(f"bon{i}", [128, 32], stack=S) for i in range(2)]
                sqb = sb("sqb", [128, W], BF16, stack=S)
                rkb = sb("rkb", [128, W], BF16, stack=S)
                IND = sb("IND", [128, KT, 32], BF16, stack=S)
                memset("pool", IND[:], 0.0, [IND])
                for et_ in range(KT):
                    memset("pool", IND[0:64, et_, 2 * et_:2 * et_ + 1], 1.0, [IND])
                    memset("pool", IND[64:128, et_, 2 * et_ + 1:2 * et_ + 2], 1.0, [IND])
                psb = [PS[5], PS[6]]
                psn[0] = 5
                nwin = T // W
                import os
                nwin_run = int(os.environ.get('P0_NWIN', nwin))
                p0step = int(os.environ.get('P0_STEP', 9))
                p0sub = int(os.environ.get('P0_SUB', 9))
                big_i = [0]; wpc_i = [0]; wrk_i = [0]
                for w in range(nwin_run):
                    t0 = w * W
                    isctx = (w == 0)
                    if p0step < 1:
                        continue
                    lo = max(t0 - 64, 0 if isctx else NCTX)
                    hi = min(t0 + W + 64, NCTX if isctx else T)
                    if w in (0, 1, nwin - 1):
                        memset("pool", HW[:], 0.0, [HW])
                    P.dma("sp", HW[:, :, lo - (t0 - 64):hi - (t0 - 64)], HT0[:, :, lo:hi].rearrange("k p t -> p k t"), writes=[HW])
                    ctr = HW[:, :, 64:64 + W]
                    if isctx:
                        groups = [(0, 8, -1), (8, 16, 1)]
                    else:
                        groups = [(0, 4, -1), (4, 8, 1), (8, 12, -64), (12, 16, 64)]
                    for gi, (k0, k1, s_) in enumerate(groups):
                        tt("dve" if gi % 2 == 0 else "pool", dT[:, k0:k1, :], HW[:, k0:k1, 64 + s_:64 + s_ + W], HW[:, k0:k1, 64:64 + W],
                           ALU.subtract, [HW], [dT])
                    if not isctx:
                        d4 = dT[:].rearrange("p k (r c) -> p k r c", c=64)
                        h4 = ctr.rearrange("p k (r c) -> p k r c", c=64)
                        ts("dve", d4[:, 0:4, :, 0], h4[:, 0:4, :, 0], -1.0, None, ALU.mult, None, [HW, dT], [dT])
                        ts("dve", d4[:, 4:8, :, 63], h4[:, 4:8, :, 63], -1.0, None, ALU.mult, None, [HW, dT], [dT])

                    def mkxm(j, dst, engs=("dve",)):
                        for kt in range(KT):
                            stt(engs[kt % len(engs)], dst[:, kt, :], dT[:, kt, :], mixT[:, j, kt:kt + 1], ctr[:, kt, :],
                                ALU.mult, ALU.add, [dT, mixT, HW], [dst])

                    if p0step < 2:
                        continue
                    mkxm(4, xm["t0"])
                    for d in range(2):
                        pp = ps()
                        for kt in range(KT):
                            mm(pp[0:LORA, 0:W], w1b[d][:, kt, :], xm["t0"][:, kt, :], kt == 0, kt == KT - 1, [w1b[d], xm["t0"]], [pp])
                        act(thT[d][:], pp[0:LORA, 0:W], AF.Tanh, [pp], [thT[d]])
                    for d in range(2):
                        for tt_ in range(2):
                            bg = big[big_i[0] % 2]; big_i[0] += 1
                            for cg in range(4):
                                pp = ps()
                                mm(pp[:], thT[d][:, tt_ * 128:(tt_ + 1) * 128], w2b[d][:, cg * 512:(cg + 1) * 512], True, True, [thT[d], w2b[d]], [pp])
                                tt("dve", bg[:, cg * 512:(cg + 1) * 512], pp[:], W0t[d][:, cg * 512:(cg + 1) * 512], ALU.add, [pp, W0t[d]], [bg])
                            act(bg[:], bg[:], AF.Sigmoid, [bg], [bg])
                            ts("pool", bg[:], bg[:], -EXPM05, None, ALU.mult, None, [bg], [bg])
                            P.dma("sp", LW[d][t0 + tt_ * 128:t0 + (tt_ + 1) * 128, :], bg[:], reads=[bg])
                    if p0step < 3:
                        continue
                    mkxm(5, xm["t1"])
                    for d in range(2):
                        pp = ps()
                        for kt in range(KT):
                            mm(pp[0:LORA, 0:W], a1b[d][:, kt, :], xm["t1"][:, kt, :], kt == 0, kt == KT - 1, [a1b[d], xm["t1"]], [pp])
                        cp("act", xa1[d][:], pp[0:LORA, 0:W], [pp], [xa1[d]])
                    if p0step < 4:
                        continue
                    for j, dst_dram, key in ((2, V0, "t0"), (3, SG0, "t1")):
                        mkxm(j, xm[key])
                        bgs = [big[0], big[1]]
                        for cg in range(8):
                            wp = wpc[wpc_i[0] % 2]; wpc_i[0] += 1
                            P.dma("sp", wp[:], (Wb_v if j == 2 else Wb_g)[cg], writes=[wp])
                            for tt_ in range(2):
                                pp = ps()
                                for kt in range(KT):
                                    mm(pp[:, 0:256], xm[key][:, kt, tt_ * 128:(tt_ + 1) * 128], wp[:, kt, :], kt == 0, kt == KT - 1, [xm[key], wp], [pp])
                                if j == 2:
                                    cp("act", bgs[tt_][:, cg * 256:(cg + 1) * 256], pp[:, 0:256], [pp], [bgs[tt_]])
                                else:
                                    act(bgs[tt_][:, cg * 256:(cg + 1) * 256], pp[:, 0:256], AF.Silu, [pp], [bgs[tt_]])
                        for tt_ in range(2):
                            P.dma("sp", dst_dram[t0 + tt_ * 128:t0 + (tt_ + 1) * 128, :], bgs[tt_][:], reads=[bgs[tt_]])
                    if p0step < 5:
                        continue
                    mkxm(0, xm["r"])
                    mkxm(1, xm["k"])
                    def ld_rk(et_):
                        a_ = wrk[(2 * et_) % 4]; b_ = wrk[(2 * et_ + 1) % 4]
                        P.dma("sp", a_[:], Wb_r[et_], writes=[a_])
                        P.dma("sp", b_[:], Wb_k[et_], writes=[b_])
                    ld_rk(0)
                    for et in range(KT):
                        i2 = et % 2
                        wr = wrk[(2 * et) % 4]; wk = wrk[(2 * et + 1) % 4]
                        if et + 1 < KT:
                            ld_rk(et + 1)
                        p1 = ps(); p2 = ps()
                        psr = p1[:, 0:W]; psk = p1[:, W:2 * W]
                        for kt in range(KT):
                            mm(psr, wr[:, kt, :], xm["r"][:, kt, :], kt == 0, kt == KT - 1, [wr, xm["r"]], [p1])
                        for kt in range(KT):
                            mm(psk, wk[:, kt, :], xm["k"][:, kt, :], kt == 0, kt == KT - 1, [wk, xm["k"]], [p1])
                        for d in range(2):
                            mm(p2[:, d * W:(d + 1) * W], a2b[d][:, et * 128:(et + 1) * 128], xa1[d][:], True, True, [a2b[d], xa1[d]], [p2])
                        f = {n: fm[n][i2] for n in fm}
                        cp("act", f["r"][:], psr, [p1], [f["r"]])
                        cp("act", f["k"][:], psk, [p1], [f["k"]])
                        for d in range(2):
                            act(f[f"a{d}"][:], p2[:, d * W:(d + 1) * W], AF.Sigmoid, [p2, prm], [f[f"a{d}"]], bias=prm[:, 4 + d, et:et + 1])
                        if p0sub < 2:
                            continue
                        ts("dve", f["kkr"][:], psk, prm[:, 0, et:et + 1], None, ALU.mult, None, [p1, prm], [f["kkr"]])
                        act(sqb[:], psk, AF.Square, [p1, prm], [sqb], scale=prm[:, 0, et:et + 1])
                        p3 = ps()
                        mm(p3[:, 0:W], blkb[:], sqb[:], True, True, [blkb, sqb], [p3])
                        act(f["rn"][:], p3[:, 0:W], AF.Sqrt, [p3], [f["rn"]])
                        ts("dve", f["rn"][:], f["rn"][:], 1e-12, None, ALU.max, None, [f["rn"]], [f["rn"]])
                        recip(f["rn"][:], f["rn"][:], [f["rn"]], [f["rn"]])
                        tt("dve", f["kk"][:], f["kkr"][:], f["rn"][:], ALU.mult, [f["kkr"], f["rn"]], [f["kk"]])
                        if p0sub < 3:
                            continue
                        P.dma("sp", RT[et, :, t0:t0 + W], f["r"][:], reads=[f["r"]])
                        P.dma("sp", KKT[et, :, t0:t0 + W], f["kk"][:], reads=[f["kk"]])
                        for d in range(2):
                            ts("dve", f["t1"][:], f[f"a{d}"][:], prm[:, 1, et:et + 1], prm[:, 2, et:et + 1], ALU.mult, ALU.add,
                               [f[f"a{d}"], prm], [f["t1"]])
                            tt("pool", f[f"kd{d}"][:], f["k"][:], f["t1"][:], ALU.mult, [f["k"], f["t1"]], [f[f"kd{d}"]])
                            tt("pool", f[f"b{d}"][:], f["kk"][:], f[f"a{d}"][:], ALU.mult, [f["kk"], f[f"a{d}"]], [f[f"b{d}"]])
                            P.dma("sp", KDT[d][et, :, t0:t0 + W], f[f"kd{d}"][:], reads=[f[f"kd{d}"]])
                            P.dma("sp", BT[d][et, :, t0:t0 + W], f[f"b{d}"][:], reads=[f[f"b{d}"]])
                        if p0sub < 4:
                            continue
                        tt("dve", f["ks"][:], f["kd0"][:], f["kd1"][:], ALU.add, [f["kd0"], f["kd1"]], [f["ks"]])
                        stt("dve", rkb[:], f["r"][:], prm[:, 3, et:et + 1], f["ks"][:], ALU.mult, ALU.mult, [f["r"], prm, f["ks"]], [rkb])
                        if p0step < 6:
                            continue
                        for tt_ in range(2):
                            mm(psb[tt_][:, 0:32], rkb[:, tt_ * 128:(tt_ + 1) * 128], IND[:, et, :], et == 0, et == KT - 1, [rkb, IND], [psb[tt_]])
                    if p0step < 6:
                        continue
                    for tt_ in range(2):
                        cp("act", bon[tt_][:], psb[tt_][:, 0:32], [psb[tt_]], [bon[tt_]])
                        P.dma("sp", BON[t0 + tt_ * 128:t0 + (tt_ + 1) * 128, :], bon[tt_][:], reads=[bon[tt_]])
                psn[0] = 7
            P.barrier()

        def phase_s0(d, NG=4):
            with ExitStack() as ph:
                S = ph.enter_context
                SDT = BF16
                NP = 16
                GP = NP // NG
                NB = GP // 4
                ld = {n: [sb(f"s_{n}{i}", [128, NP, C0], stack=S) for i in range(2)] for n in ("r", "kd", "kk", "b", "v")}
                lwc = [sb(f"s_lw{i}", [C0, D], stack=S) for i in range(2)]
                lwh = sb("s_lwh", [C0, D], BF16, stack=S); lwl = sb("s_lwl", [C0, D], BF16, stack=S)

                def gb(name, shape, dt=F32):
                    return [sb(f"{name}{g}", shape, dt, stack=S) for g in range(NG)]
                vcb = gb("s_vcb", [128, GP, C0], SDT)
                eP = gb("s_eP", [128, GP, C0]); ePx = gb("s_ePx", [128, GP, C0]); eN = gb("s_eN", [128, GP, C0])
                ex = {n: gb(f"s_ex{n}", [128, GP, 128], SDT) for n in ("A", "B", "K", "R")}
                for n in ex:
                    for g in range(NG):
                        memset("pool", ex[n][g][:], 0.0, [ex[n][g]])
                Xs = [gb(f"s_X{i}", [128, GP, 128], SDT) for i in range(2)]
                Ls = [gb(f"s_L{i}", [128, GP, 128], SDT) for i in range(2)]
                Mak = gb("s_Mak", [128, GP, 128], SDT); Mrb = gb("s_Mrb", [128, GP, 128], SDT); Mrk = gb("s_Mrk", [128, GP, 128], SDT)
                BTe = gb("s_BTe", [128, GP, 128], SDT); KTe = gb("s_KTe", [128, GP, 128], SDT)
                ST32 = gb("s_ST32", [128, GP, C0]); STb = gb("s_STb", [128, GP, C0], SDT)
                Y32 = gb("s_Y32", [128, GP, C0]); Yb = gb("s_Yb", [128, GP, C0], SDT)
                oc = [gb(f"s_oc{i}", [128, GP, C0]) for i in range(2)]
                for g in range(NG):
                    memset("pool", ST32[g][:], 0.0, [ST32[g]])
                    memset("pool", STb[g][:], 0.0, [STb[g]])
                nch = T // C0
                order = list(range(nch)) if d == 0 else [3, 2, 1, 0] + list(range(nch - 1, 3, -1))
                tl = C0 - 1 if d == 0 else 0
                GW = GP * C0

                def fview(ap, t0):
                    return ap[:, :, t0:t0 + C0].rearrange("k p t -> p k t")

                def pview(ap, t0, hh, g):
                    return ap[t0:t0 + C0, g * GP * 128:(g + 1) * GP * 128].rearrange("s (pr h v) -> h s pr v", h=2, v=64)[hh]

                def p3(p_):
                    return p_[:, 0:GW].rearrange("p (a t) -> p a t", t=64)

                def p4(p_):
                    return p_[:].rearrange("p (q t) -> p q t", q=4)

                def body(g, L_, t0, b2):
                    gs = slice(g * GP, (g + 1) * GP)
                    pA = ps(); pB = ps()
                    for pp, trib in ((pA, tribI[d]), (pB, tribS[d])):
                        for q in range(GP):
                            pr = g * GP + q
                            o_ = pp[:, q * 64:(q + 1) * 64]
                            mm(o_, lwh[:, pr * 128:(pr + 1) * 128], trib[:], True, False, [lwh, trib], [pp])
                            mm(o_, lwl[:, pr * 128:(pr + 1) * 128], trib[:], False, True, [lwl, trib], [pp])
                    act(eP[g][:], p3(pA), AF.Exp, [pA], [eP[g]])
                    act(eN[g][:], p3(pA), AF.Exp, [pA], [eN[g]], scale=-1.0)
                    act(ePx[g][:], p3(pB), AF.Exp, [pB], [ePx[g]])
                    yield
                    for hh in range(2):
                        psl = slice(hh * 64, (hh + 1) * 64)
                        csl = slice(hh * 64, (hh + 1) * 64)
                        stt("dve", ex["A"][g][psl, :, csl], L_["kk"][psl, gs, :], -1.0, ePx[g][psl, :, :], ALU.mult, ALU.mult, [L_["kk"], ePx[g]], [ex["A"][g]])
                        tt("pool", ex["B"][g][psl, :, csl], L_["b"][psl, gs, :], eN[g][psl, :, :], ALU.mult, [L_["b"], eN[g]], [ex["B"][g]])
                        tt("dve", ex["K"][g][psl, :, csl], L_["kd"][psl, gs, :], eN[g][psl, :, :], ALU.mult, [L_["kd"], eN[g]], [ex["K"][g]])
                        tt("pool", ex["R"][g][psl, :, csl], L_["r"][psl, gs, :], eP[g][psl, :, :], ALU.mult, [L_["r"], eP[g]], [ex["R"][g]])
                    cp("act", vcb[g][:], L_["v"][:, gs, :], [L_["v"]], [vcb[g]])
                    yield
                    X = Xs[0][g]; Lm = Ls[0][g]
                    specs = [(X, "B", "A", mS4[d]), (Lm, "A", "B", mS4[1 - d]), (Mak[g], "K", "A", mS4[d]), (Mrb[g], "B", "R", mI4[d]), (Mrk[g], "K", "R", mI4[d])]
                    for dst, l_, r_, msk in specs:
                        for sbk in range(NB):
                            pp = ps()
                            for q in range(4):
                                pr = sbk * 4 + q
                                mm(pp[:, q * 128:(q + 1) * 128], ex[l_][g][:, pr, :], ex[r_][g][:, pr, :], True, True, [ex[l_][g], ex[r_][g]], [pp])
                            tt("dve", dst[:, sbk * 4:(sbk + 1) * 4, :], p4(pp), msk[:], ALU.mult, [pp, msk], [dst])
                    yield
                    pY = ps()
                    for q in range(GP):
                        o_ = pY[:, q * 64:(q + 1) * 64]
                        mm(o_, ex["A"][g][:, q, :], STb[g][:, q, :], True, False, [ex["A"][g], STb[g]], [pY])
                        mm(o_, Mak[g][:, q, :], vcb[g][:, q, :], False, True, [Mak[g], vcb[g]], [pY])
                    cp("act", Y32[g][:], p3(pY), [pY], [Y32[g]])
                    cp("dve", Yb[g][:], p3(pY), [pY], [Yb[g]])
                    yield
                    cur = 0
                    for lev in range(6):
                        Xc = Xs[cur][g]; Lc = Ls[cur][g]
                        pY = ps()
                        for q in range(GP):
                            mm(pY[:, q * 64:(q + 1) * 64], Xc[:, q, :], Yb[g][:, q, :], True, True, [Xc, Yb[g]], [pY])
                        if lev < 5:
                            Xn = Xs[1 - cur][g]; Ln = Ls[1 - cur][g]
                            pxs = []
                            for sbk in range(NB):
                                pp = ps()
                                for q in range(4):
                                    pr = sbk * 4 + q
                                    mm(pp[:, q * 128:(q + 1) * 128], Lc[:, pr, :], Xc[:, pr, :], True, True, [Lc, Xc], [pp])
                                pxs.append(pp)
                            pls = []
                            if lev < 4:
                                for sbk in range(NB):
                                    pp = ps()
                                    for q in range(4):
                                        pr = sbk * 4 + q
                                        mm(pp[:, q * 128:(q + 1) * 128], Xc[:, pr, :], Lc[:, pr, :], True, True, [Lc, Xc], [pp])
                                    pls.append(pp)
                        tt("dve", Y32[g][:], Y32[g][:], p3(pY), ALU.add, [Y32[g], pY], [Y32[g]])
                        cp("pool", Yb[g][:], Y32[g][:], [Y32[g]], [Yb[g]])
                        if lev < 5:
                            for sbk, pp in enumerate(pxs):
                                cp("act", Xn[:, sbk * 4:(sbk + 1) * 4, :], p4(pp), [pp], [Xn])
                            for sbk, pp in enumerate(pls):
                                cp("act" if sbk % 2 else "dve", Ln[:, sbk * 4:(sbk + 1) * 4, :], p4(pp), [pp], [Ln])
                            cur = 1 - cur
                        yield
                    o_sb = oc[b2][g]
                    pO = ps()
                    for q in range(GP):
                        o_ = pO[:, q * 64:(q + 1) * 64]
                        mm(o_, ex["R"][g][:, q, :], STb[g][:, q, :], True, False, [ex["R"][g], STb[g]], [pO])
                        mm(o_, Mrb[g][:, q, :], Yb[g][:, q, :], False, False, [Mrb[g], Yb[g]], [pO])
                        mm(o_, Mrk[g][:, q, :], vcb[g][:, q, :], False, True, [Mrk[g], vcb[g]], [pO])
                    pts = []
                    for src_, dst in ((ex["B"][g], BTe[g]), (ex["K"][g], KTe[g])):
                        for sbk in range(NB):
                            pp = ps()
                            for q in range(4):
                                pr = sbk * 4 + q
                                mm(pp[:, q * 128:(q + 1) * 128], src_[:, pr, :], identb[:], True, True, [src_, identb], [pp])
                            pts.append((pp, dst, sbk))
                    cp("act", o_sb[:], p3(pO), [pO], [o_sb])
                    for hh in range(2):
                        P.dma("sp", pview(O0[d], t0, hh, g), o_sb[hh * 64:(hh + 1) * 64, :, :], reads=[o_sb])
                    for i_, (pp, dst, sbk) in enumerate(pts):
                        cp("act" if i_ % 2 else "dve", dst[:, sbk * 4:(sbk + 1) * 4, :], p4(pp), [pp], [dst])
                    yield
                    pS = ps()
                    for q in range(GP):
                        o_ = pS[:, q * 64:(q + 1) * 64]
                        mm(o_, BTe[g][:, q, :], Yb[g][:, q, :], True, False, [BTe[g], Yb[g]], [pS])
                        mm(o_, KTe[g][:, q, :], vcb[g][:, q, :], False, True, [KTe[g], vcb[g]], [pS])
                    tt("dve", ST32[g][:], ST32[g][:], p3(pS), ALU.add, [ST32[g], pS], [ST32[g]])
                    tt("pool", ST32[g][:], ST32[g][:], eP[g][:, :, tl:tl + 1].to_broadcast([128, GP, C0]), ALU.mult, [ST32[g], eP[g]], [ST32[g]])
                    cp("act", STb[g][:], ST32[g][:], [ST32[g]], [STb[g]])
                    yield

                def issue_loads(ci):
                    t0 = order[ci] * C0
                    b2 = ci % 2
                    L_ = {n: ld[n][b2] for n in ld}
                    lw_ = lwc[b2]
                    P.dma("sp", L_["r"][:], fview(RT, t0), writes=[L_["r"]])
                    P.dma("sp", L_["kd"][:], fview(KDT[d], t0), writes=[L_["kd"]])
                    P.dma("sp", L_["kk"][:], fview(KKT, t0), writes=[L_["kk"]])
                    P.dma("sp", L_["b"][:], fview(BT[d], t0), writes=[L_["b"]])
                    P.dma("sp", lw_[:], LW[d][t0:t0 + C0, :], writes=[lw_])
                    for hh in range(2):
                        P.dma("sp", L_["v"][hh * 64:(hh + 1) * 64, :, :],
                              V0[t0:t0 + C0, :].rearrange("s (pr h v) -> h s pr v", h=2, v=64)[hh], writes=[L_["v"]])

                issue_loads(0)
                for ci, c in enumerate(order):
                    t0 = c * C0
                    b2 = ci % 2
                    L_ = {n: ld[n][b2] for n in ld}
                    lw_ = lwc[b2]
                    if ci + 1 < len(order):
                        issue_loads(ci + 1)
                    cp("act", lwh[:], lw_[:], [lw_], [lwh])
                    tt("dve", lwl[:], lw_[:], lwh[:], ALU.subtract, [lw_, lwh], [lwl])
                    issue_cast1(1)
                    gens = [body(g, L_, t0, b2) for g in range(NG)]
                    while gens:
                        for gen in list(gens):
                            try:
                                next(gen)
                            except StopIteration:
                                gens.remove(gen)
            P.barrier()

        def phase_o0():
            with ExitStack() as ph:
                S = ph.enter_context
                WoB = sb("WoB", [128, KT, D], BF16, stack=S)
                P.dma("sp", WoB[:], Wb_out0.rearrange("(k p) e -> p k e", p=128), writes=[WoB])
                LNG = sb("LNG", [128, D], stack=S); LNB = sb("LNB", [128, D], stack=S); TG = sb("TG", [128, D], stack=S)
                P.dma("sp", LNG[:], rk_ln_g.partition_broadcast(128), writes=[LNG])
                P.dma("sp", LNB[:], rk_ln_b.partition_broadcast(128), writes=[LNB])
                bufs = {n: [sb(f"o_{n}{i}", [128, D], stack=S) for i in range(1)] * 2 for n in ("of", "ob", "v", "sg", "x")}
                sq = sb("o_sq", [128, D], stack=S)
                ybf = sb("o_ybf", [128, D], BF16, stack=S)
                bo = [sb(f"o_bon{i}", [128, 32], stack=S) for i in range(2)]
                st = [sb(f"o_st{i}", [128, 4, 32], stack=S) for i in range(2)]
                yT = [sb(f"o_yT{i}", [128, KT, 128], BF16, stack=S) for i in range(2)]
                for i in range(NTILE):
                    row = 1 if i < 2 else 0
                    if i == 0 or i == 2:
                        P.dma("sp", TG[:], ADA[0, row, 2, :].partition_broadcast(128), writes=[TG])
                    b2 = i % 2
                    B_ = {n: bufs[n][b2] for n in bufs}
                    rs = slice(i * 128, (i + 1) * 128)
                    P.dma("sp", B_["of"][:], O0[0][rs, :], writes=[B_["of"]])
                    P.dma("sp", B_["ob"][:], O0[1][rs, :], writes=[B_["ob"]])
                    P.dma("sp", B_["v"][:], V0[rs, :], writes=[B_["v"]])
                    P.dma("sp", B_["sg"][:], SG0[rs, :], writes=[B_["sg"]])
                    P.dma("sp", B_["x"][:], xin[rs, :], writes=[B_["x"]])
                    P.dma("sp", bo[b2][:], BON[rs, :], writes=[bo[b2]])
                    o = B_["of"]; s_ = st[b2]
                    o3 = o[:].rearrange("p (h v) -> p h v", v=64)
                    tt("dve", o[:], o[:], B_["ob"][:], ALU.add, [o, B_["ob"]], [o])
                    red("dve", s_[:, 0, :], o3, [o], [s_])
                    ts("dve", s_[:, 0, :], s_[:, 0, :], -1.0 / 64, None, ALU.mult, None, [s_], [s_])
                    tt("dve", o3, o3, s_[:, 0, :].unsqueeze(2).to_broadcast([128, 32, 64]), ALU.add, [o, s_], [o])
                    tt("pool", sq[:], o[:], o[:], ALU.mult, [o], [sq])
                    red("dve", s_[:, 1, :], sq[:].rearrange("p (h v) -> p h v", v=64), [sq], [s_])
                    ts("dve", s_[:, 1, :], s_[:, 1, :], 1.0 / 64, GN_EPS, ALU.mult, ALU.add, [s_], [s_])
                    act(s_[:, 2, :], s_[:, 1, :], AF.Sqrt, [s_], [s_])
                    recip(s_[:, 3, :], s_[:, 2, :], [s_], [s_])
                    tt("dve", o3, o3, s_[:, 3, :].unsqueeze(2).to_broadcast([128, 32, 64]), ALU.mult, [o, s_], [o])
                    tt("pool", o[:], o[:], LNG[:], ALU.mult, [o, LNG], [o])
                    tt("pool", o[:], o[:], LNB[:], ALU.add, [o, LNB], [o])
                    v_ = B_["v"]
                    v3 = v_[:].rearrange("p (h v) -> p h v", v=64)
                    tt("dve", v3, v3, bo[b2][:].unsqueeze(2).to_broadcast([128, 32, 64]), ALU.mult, [v_, bo[b2]], [v_])
                    tt("pool", o[:], o[:], v_[:], ALU.add, [o, v_], [o])
                    tt("dve", o[:], o[:], B_["sg"][:], ALU.mult, [o, B_["sg"]], [o])
                    yt = yT[b2]
                    cp("act", ybf[:], o[:], [o], [ybf])
                    for g in range(4):
                        pp = ps()
                        for q in range(4):
                            kt = g * 4 + q
                            mm(pp[:, q * 128:(q + 1) * 128], ybf[:, kt * 128:(kt + 1) * 128], identb[:], True, True, [ybf, identb], [pp])
                        cp("act", yt[:, g * 4:(g + 1) * 4, :], pp[:].rearrange("p (q t) -> p q t", q=4), [pp], [yt])
                    x_ = B_["x"]
                    for cg in range(4):
                        pp = ps()
                        for kt in range(KT):
                            mm(pp[:], yt[:, kt, :], WoB[:, kt, cg * 512:(cg + 1) * 512], kt == 0, kt == KT - 1, [yt, WoB], [pp])
                        cs_ = slice(cg * 512, (cg + 1) * 512)
                        tt("dve", sq[:, cs_], pp[:], TG[:, cs_], ALU.mult, [pp, TG], [sq])
                        tt("pool", x_[:, cs_], x_[:, cs_], sq[:, cs_], ALU.add, [x_, sq], [x_])
                    P.dma("sp", X1[rs, :], x_[:], reads=[x_])
            P.barrier()

        def phase_p1():
            with ExitStack() as ph:
                S = ph.enter_context
                W = 256
                hT = [sb(f"p1_hT{i}", [128, KT, W], BF16, stack=S) for i in range(2)]
                dl = sb("p1_dl", [128, 16], stack=S)
                P.dma("sp", dl[:], rt_dl.partition_broadcast(128), writes=[dl])
                lg = sb("p1_lg", [128, 6, 16], stack=S)
                act(lg[:, 0, :], dl[:], AF.Exp, [dl], [lg], scale=-1.0)
                act(lg[:, 0, :], lg[:, 0, :], AF.Ln, [lg], [lg], bias=1.0)
                ts("dve", lg[:, 1, :], lg[:, 0, :], 1.0, None, ALU.mult, None, [lg], [lg])
                ts("dve", lg[:, 0, :], lg[:, 1, :], -1.0, None, ALU.mult, None, [lg], [lg])
                ts("dve", lg[:, 2, :], lg[:, 0, :], float(C1), None, ALU.mult, None, [lg], [lg])
                ts("dve", lg[:, 3, :], lg[:, 1, :], math.log(1.0 / 16), None, ALU.add, None, [lg], [lg])
                ts("dve", lg[:, 4, :], lg[:, 1, :], float(C1), math.log(1.0 / 16), ALU.mult, ALU.add, [lg], [lg])
                pos = sb("p1_pos", [128, W], stack=S)
                ones_ = sb("p1_ones", [128, 128], BF16, stack=S)
                memset("pool", ones_[:], 1.0, [ones_])
                pp_ = ps()
                mm(pp_[:, 0:128], ones_[:], triUsb[:], True, True, [ones_, triUsb], [pp_])
                cp("dve", pos[:, 0:128], pp_[:, 0:128], [pp_], [pos])
                cp("dve", pos[:, 128:256], pp_[:, 0:128], [pp_], [pos])
                DQ = [[sb(f"DQ{d}{h}", [128, W], stack=S) for h in range(8)] for d in range(2)]
                DK = [[sb(f"DK{d}{h}", [128, W], stack=S) for h in range(8)] for d in range(2)]
                for h in range(8):
                    c0 = h; c1 = 8 + h
                    act(DQ[0][h][:], pos[:], AF.Exp, [pos, lg], [DQ[0][h]], scale=lg[:, 0, c0:c0 + 1], bias=lg[:, 0, c0:c0 + 1])
                    act(DK[0][h][:], pos[:], AF.Exp, [pos, lg], [DK[0][h]], scale=lg[:, 1, c0:c0 + 1], bias=lg[:, 3, c0:c0 + 1])
                    act(DQ[1][h][:], pos[:], AF.Exp, [pos, lg], [DQ[1][h]], scale=lg[:, 1, c1:c1 + 1], bias=lg[:, 2, c1:c1 + 1])
                    act(DK[1][h][:], pos[:], AF.Exp, [pos, lg], [DK[1][h]], scale=lg[:, 0, c1:c1 + 1], bias=lg[:, 4, c1:c1 + 1])
                cs_t = sb("p1_cos", [128, W], stack=S); sn_t = sb("p1_sin", [128, W], stack=S)
                wrk = [sb(f"p1_wrk{i}", [128, KT, 128], BF16, stack=S) for i in range(4)]
                wpc = [sb(f"p1_wpc{i}", [128, KT, 512], BF16, stack=S) for i in range(2)]
                xx = [[sb(f"p1_x{i}{j}", [128, W], stack=S) for j in range(2)] for i in range(2)]
                tmp = [sb(f"p1_t{i}", [128, W], stack=S) for i in range(4)]
                yy = [sb(f"p1_y{i}", [128, W], stack=S) for i in range(2)]
                ob = [sb(f"p1_ob{i}", [128, W], BF16, stack=S) for i in range(4)]
                vst = [sb(f"p1_vst{i}", [128, 512], BF16, stack=S) for i in range(2)]
                gst = [sb(f"p1_gst{i}", [128, 512], stack=S) for i in range(2)]
                nwin = T // W
                cnt_ = [0]
                for w in range(nwin):
                    t0 = w * W
                    isctx = (w == 0)
                    h_ = hT[w % 2]
                    P.dma("sp", h_[:], HT1[:, :, t0:t0 + W].rearrange("k p t -> p k t"), writes=[h_])
                    if not isctx:
                        P.dma("sp", cs_t[:], ropec[:, t0 - NCTX:t0 - NCTX + W], writes=[cs_t])
                        P.dma("sp", sn_t[:], ropes[:, t0 - NCTX:t0 - NCTX + W], writes=[sn_t])
                    for qk in range(2):
                        dsts = QT if qk == 0 else KTT
                        tabs = DQ if qk == 0 else DK
                        for h in range(8):
                            i2 = cnt_[0] % 2; cnt_[0] += 1
                            for half in range(2):
                                et = h * 2 + half
                                col0 = qk * D + h * 256 + half * 128
                                wr = wrk[(cnt_[0] * 2 + half) % 4]
                                P.dma("sp", wr[:], Wb_qk[qk * 16 + h * 2 + half], writes=[wr])
                                pp = ps()
                                for kt in range(KT):
                                    mm(pp[:, 0:W], wr[:, kt, :], h_[:, kt, :], kt == 0, kt == KT - 1, [wr, h_], [pp])
                                cp("act", xx[i2][half][:], pp[:, 0:W], [pp], [xx[i2][half]])
                            x1 = xx[i2][0]; x2 = xx[i2][1]
                            if isctx:
                                y1, y2 = x1, x2
                            else:
                                y1, y2 = yy[0], yy[1]
                                tt("dve", tmp[0][:], x1[:], cs_t[:], ALU.mult, [x1, cs_t], [tmp[0]])
                                tt("pool", tmp[1][:], x2[:], sn_t[:], ALU.mult, [x2, sn_t], [tmp[1]])
                                tt("dve", y1[:], tmp[0][:], tmp[1][:], ALU.subtract, [tmp[0], tmp[1]], [y1])
                                tt("pool", tmp[2][:], x1[:], sn_t[:], ALU.mult, [x1, sn_t], [tmp[2]])
                                tt("dve", tmp[3][:], x2[:], cs_t[:], ALU.mult, [x2, cs_t], [tmp[3]])
                                tt("pool", y2[:], tmp[2][:], tmp[3][:], ALU.add, [tmp[2], tmp[3]], [y2])
                            for d in range(2):
                                for half, y_ in ((0, y1), (1, y2)):
                                    o_ = ob[d * 2 + half]
                                    tt("dve" if half == 0 else "pool", o_[:], y_[:], tabs[d][h][:], ALU.mult, [y_, tabs[d][h]], [o_])
                                    P.dma("sp", dsts[d][h * 2 + half, :, t0:t0 + W], o_[:], reads=[o_])
                    for vg in range(2):
                        for cg in range(8):
                            wp = wpc[cg % 2]
                            col0 = 2 * D + vg * 2 * D + cg * 512
                            P.dma("sp", wp[:], Wb_vg[vg * 8 + cg], writes=[wp])
                            for tt_ in range(2):
                                pp = ps()
                                for kt in range(KT):
                                    mm(pp[:], h_[:, kt, tt_ * 128:(tt_ + 1) * 128], wp[:, kt, :], kt == 0, kt == KT - 1, [h_, wp], [pp])
                                rs = slice(t0 + tt_ * 128, t0 + (tt_ + 1) * 128)
                                if vg == 0:
                                    cp("act", vst[tt_][:], pp[:], [pp], [vst[tt_]])
                                    P.dma("sp", V1[rs, cg * 512:(cg + 1) * 512], vst[tt_][:], reads=[vst[tt_]])
                                else:
                                    act(gst[tt_][:], pp[:], AF.Silu, [pp], [gst[tt_]])
                                    P.dma("sp", SG1[rs, cg * 512:(cg + 1) * 512], gst[tt_][:], reads=[gst[tt_]])
            P.barrier()

        def phase_s1(d):
            with ExitStack() as ph:
                S = ph.enter_context
                dl = sb("s1_dl", [128, 16], stack=S)
                P.dma("sp", dl[:], rt_dl.partition_broadcast(128), writes=[dl])
                gc = sb("s1_gc", [128, 16], stack=S)
                act(gc[:], dl[:], AF.Exp, [dl], [gc], scale=-1.0)
                act(gc[:], gc[:], AF.Ln, [gc], [gc], bias=1.0)
                act(gc[:], gc[:], AF.Exp, [gc], [gc], scale=-float(C1))
                qt = [sb(f"s1_q{i}", [128, KT, C1], BF16, stack=S) for i in range(2)]
                kt_ = [sb(f"s1_k{i}", [128, KT, C1], BF16, stack=S) for i in range(2)]
                vc = [sb(f"s1_v{i}", [128, 2 * D], BF16, stack=S) for i in range(2)]
                R32 = [sb(f"s1_R32_{h}", [128, 2, 512], stack=S) for h in range(8)]
                Rb = [sb(f"s1_Rb_{h}", [128, 2, 512], BF16, stack=S) for h in range(8)]
                for h_ in range(8):
                    memset("pool", R32[h_][:], 0.0, [R32[h_]]); memset("pool", Rb[h_][:], 0.0, [Rb[h_]])
                Sb = [sb(f"s1_S{i}", [128, 128], BF16, stack=S) for i in range(8)]
                ktok = [sb(f"s1_kt{i}", [128, 256], BF16, stack=S) for i in range(8)]
                ost = [sb(f"s1_o{i}", [128, 512], stack=S) for i in range(8)]
                msk = sb("s1_msk", [128, 128], stack=S)
                cp("dve", msk[:], (triU if d == 0 else triLs)[:], [triU, triLs], [msk])
                nch = T // C1
                order = list(range(nch)) if d == 0 else [1, 0] + list(range(nch - 1, 1, -1))

                def hbody(h, q_, k_, v_, t0):
                    pS = ps()
                    for half in range(2):
                        mm(pS[:, 0:128], k_[:, h * 2 + half, :], q_[:, h * 2 + half, :], half == 0, half == 1, [k_, q_], [pS])
                    tt("dve", Sb[h][:], pS[:, 0:128], msk[:], ALU.mult, [pS, msk], [Sb[h]])
                    pT = ps()
                    for half in range(2):
                        mm(pT[:, half * 128:(half + 1) * 128], k_[:, h * 2 + half, :], identb[:], True, True, [k_, identb], [pT])
                    cp("act", ktok[h][:], pT[:, 0:256], [pT], [ktok[h]])
                    yield
                    pO = ps()
                    mm(pO[:], Sb[h][:], v_[:, h * 512:(h + 1) * 512], True, False, [Sb[h], v_], [pO])
                    for half in range(2):
                        mm(pO[:], q_[:, h * 2 + half, :], Rb[h][:, half, :], False, half == 1, [q_, Rb[h]], [pO])
                    cp("act", ost[h][:], pO[:], [pO], [ost[h]])
                    P.dma("sp", O1[d][t0:t0 + C1, h * 512:(h + 1) * 512], ost[h][:], reads=[ost[h]])
                    yield
                    for half in range(2):
                        pR = ps()
                        mm(pR[:], ktok[h][:, half * 128:(half + 1) * 128], v_[:, h * 512:(h + 1) * 512], True, True, [ktok[h], v_], [pR])
                        tt("dve", R32[h][:, half, :], R32[h][:, half, :], pR[:], ALU.add, [R32[h], pR], [R32[h]])
                        ts("pool", R32[h][:, half, :], R32[h][:, half, :], gc[:, d * 8 + h:d * 8 + h + 1], None, ALU.mult, None, [R32[h], gc], [R32[h]])
                        cp("act", Rb[h][:, half, :], R32[h][:, half, :], [R32[h]], [Rb[h]])
                    yield

                def issue_loads1(ci):
                    t0 = order[ci] * C1
                    b2 = ci % 2
                    P.dma("sp", qt[b2][:], QT[d][:, :, t0:t0 + C1].rearrange("k p t -> p k t"), writes=[qt[b2]])
                    P.dma("sp", kt_[b2][:], KTT[d][:, :, t0:t0 + C1].rearrange("k p t -> p k t"), writes=[kt_[b2]])
                    P.dma("sp", vc[b2][:], V1[t0:t0 + C1, :], writes=[vc[b2]])

                issue_loads1(0)
                for ci, c in enumerate(order):
                    t0 = c * C1
                    b2 = ci % 2
                    q_ = qt[b2]; k_ = kt_[b2]; v_ = vc[b2]
                    if ci + 1 < len(order):
                        issue_loads1(ci + 1)
                    gens = [hbody(h, q_, k_, v_, t0) for h in range(8)]
                    while gens:
                        for gen in list(gens):
                            try:
                                next(gen)
                            except StopIteration:
                                gens.remove(gen)
            P.barrier()

        def phase_o1():
            with ExitStack() as ph:
                S = ph.enter_context
                GNG = sb("o1_gng", [128, 2 * D], stack=S); TG = sb("o1_TG", [128, D], stack=S); FG = sb("o1_FG", [128, D], stack=S)
                P.dma("sp", GNG[:], rt_gn_g.partition_broadcast(128), writes=[GNG])
                P.dma("sp", TG[:], ADA[1, 0, 2, :].partition_broadcast(128), writes=[TG])
                P.dma("sp", FG[:], final_g.partition_broadcast(128), writes=[FG])
                of = [sb(f"o1_of{i}", [128, 2 * D], stack=S) for i in range(1)] * 2
                ob = sb("o1_ob", [128, 2 * D], stack=S)
                sg = sb("o1_sg", [128, 2 * D], stack=S)
                x_b = [sb(f"o1_x{i}", [128, D], stack=S) for i in range(2)]
                st = [sb(f"o1_st{i}", [128, 4, 8], stack=S) for i in range(2)]
                yT = sb("o1_yT", [128, 32, 256], BF16, stack=S)
                wp = [sb(f"o1_wp{i}", [128, 32, 256], BF16, stack=S) for i in range(2)]
                junk = sb("o1_junk", [128, D], stack=S)
                ybf1 = sb("o1_ybf", [128, 2 * D], BF16, stack=S)
                wi = [0]
                for pi in range((NTILE - 2) // 2):
                    tiles = [2 + 2 * pi, 3 + 2 * pi]
                    for j, i in enumerate(tiles):
                        rs = slice(i * 128, (i + 1) * 128)
                        o = of[j]; x_ = x_b[j]; s_ = st[j]
                        P.dma("sp", o[:], O1[0][rs, :], writes=[o])
                        P.dma("sp", ob[:], O1[1][rs, :], writes=[ob])
                        P.dma("sp", sg[:], SG1[rs, :], writes=[sg])
                        P.dma("sp", x_[:], X1[rs, :], writes=[x_])
                        tt("dve", o[:], o[:], ob[:], ALU.add, [o, ob], [o])
                        tt("pool", ob[:], o[:], o[:], ALU.mult, [o], [ob])
                        red("dve", s_[:, 0, :], ob[:].rearrange("p (h v) -> p h v", v=512), [ob], [s_])
                        ts("dve", s_[:, 1, :], s_[:, 0, :], 1.0 / 512, EPS, ALU.mult, ALU.add, [s_], [s_])
                        act(s_[:, 2, :], s_[:, 1, :], AF.Sqrt, [s_], [s_])
                        recip(s_[:, 3, :], s_[:, 2, :], [s_], [s_])
                        o3 = o[:].rearrange("p (h v) -> p h v", v=512)
                        tt("dve", o3, o3, s_[:, 3, :].unsqueeze(2).to_broadcast([128, 8, 512]), ALU.mult, [o, s_], [o])
                        tt("pool", o[:], o[:], GNG[:], ALU.mult, [o, GNG], [o])
                        tt("dve", o[:], o[:], sg[:], ALU.mult, [o, sg], [o])
                        cp("act", ybf1[:], o[:], [o], [ybf1])
                        for g in range(8):
                            pp = ps()
                            for q in range(4):
                                kt = g * 4 + q
                                mm(pp[:, q * 128:(q + 1) * 128], ybf1[:, kt * 128:(kt + 1) * 128], identb[:], True, True, [ybf1, identb], [pp])
                            cp("act" if g % 2 else "dve", yT[:, g * 4:(g + 1) * 4, j * 128:(j + 1) * 128], pp[:].rearrange("p (q t) -> p q t", q=4), [pp], [yT])
                    for cg in range(8):
                        w_ = wp[wi[0] % 2]; wi[0] += 1
                        P.dma("sp", w_[:], Wb_o1[cg], writes=[w_])
                        cs_ = slice(cg * 256, (cg + 1) * 256)
                        for j in range(2):
                            x_ = x_b[j]
                            pp = ps()
                            for kt in range(32):
                                mm(pp[:, 0:256], yT[:, kt, j * 128:(j + 1) * 128], w_[:, kt, :], kt == 0, kt == 31, [yT, w_], [pp])
                            tt("dve", junk[:, cs_], pp[:, 0:256], TG[:, cs_], ALU.mult, [pp, TG], [junk])
                            tt("pool", x_[:, cs_], x_[:, cs_], junk[:, cs_], ALU.add, [x_, junk], [x_])
                    for j, i in enumerate(tiles):
                        x_ = x_b[j]; s_ = st[j]
                        act(junk[:], x_[:], AF.Square, [x_], [junk, s_], accum=s_[:, 0, 0:1])
                        ts("dve", s_[:, 0, 1:2], s_[:, 0, 0:1], 1.0 / D, EPS, ALU.mult, ALU.add, [s_], [s_])
                        act(s_[:, 0, 2:3], s_[:, 0, 1:2], AF.Sqrt, [s_], [s_])
                        recip(s_[:, 0, 3:4], s_[:, 0, 2:3], [s_], [s_])
                        stt("dve", x_[:], x_[:], s_[:, 0, 3:4], FG[:], ALU.mult, ALU.mult, [x_, s_, FG], [x_])
                        P.dma("sp", yout[(i - 2) * 128:(i - 1) * 128, :], x_[:], reads=[x_], is_output=True)

        P.barrier()
        stages = [
            ("ada0", lambda: adaln(0)),
            ("h0", lambda: phase_h(0, xin, HT0, F32)),
            ("p0", phase_p0),
            ("s0f", lambda: phase_s0(0)),
            ("s0b", lambda: phase_s0(1)),
            ("o0", lambda: (issue_cast1(100), phase_o0())),
            ("ada1", lambda: adaln(1)),
            ("h1", lambda: phase_h(1, X1, HT1, BF16)),
            ("p1", phase_p1),
            ("s1f", lambda: phase_s1(0)),
            ("s1b", lambda: phase_s1(1)),
            ("o1", phase_o1),
        ]
        for name, fn in stages:
            fn()
            if STOP_AFTER == name:
                break
        P.emit(E)
    return nc


def rope_tables():
    t = np.arange(NLAT)
    row = (t // 64).astype(np.float32)
    col = (t % 64).astype(np.float32)
    nf = 64
    inv = (10000.0 ** (-np.arange(nf, dtype=np.float32) / nf)).astype(np.float32)
    ang = np.concatenate([row[:, None] * inv, col[:, None] * inv], axis=-1).astype(np.float32)
    return np.ascontiguousarray(np.cos(ang).T.astype(np.float32)), np.ascontiguousarray(np.sin(ang).T.astype(np.float32))


def make_in_maps(x, c, ctx, c_ctx, ada_w, ada_b, norm_g, rk_mix, rk_w_in, rk_w0, rk_w1, rk_w2, rk_a0, rk_a1, rk_a2,
                 rk_k_k, rk_k_a, rk_r_k, rk_ln_g, rk_ln_b, rk_w_out, rt_w_in, rt_decay_logit, rt_gn_g, rt_w_out, final_g):
    f = lambda a: np.ascontiguousarray(np.asarray(a, dtype=np.float32))
    rc, rs_ = rope_tables()
    shared = dict(ada_w=f(ada_w), ada_b=f(ada_b), norm_g=f(norm_g), rk_mix=f(rk_mix)[0], rk_w_in=f(rk_w_in)[0],
                  rk_w0=f(rk_w0)[0], rk_w1=f(rk_w1)[0], rk_w2=f(rk_w2)[0], rk_a0=f(rk_a0)[0], rk_a1=f(rk_a1)[0],
                  rk_a2=f(rk_a2)[0], rk_k_k=f(rk_k_k)[0], rk_k_a=f(rk_k_a)[0], rk_r_k=f(rk_r_k)[0].reshape(-1),
                  rk_ln_g=f(rk_ln_g)[0], rk_ln_b=f(rk_ln_b)[0], rk_w_out=f(rk_w_out)[0], rt_w_in=f(rt_w_in)[0],
                  rt_dl=f(rt_decay_logit)[0].reshape(-1), rt_gn_g=f(rt_gn_g)[0], rt_w_out=f(rt_w_out)[0],
                  final_g=f(final_g), ropec=rc, ropes=rs_)
    maps = []
    for core in range(8):
        b = core % 4
        m = dict(shared)
        m["xin"] = np.ascontiguousarray(np.concatenate([f(ctx)[b], f(x)[b]], axis=0))
        m["cvec"] = np.ascontiguousarray(np.stack([f(c)[b], f(c_ctx)], axis=0))
        maps.append(m)
    return maps


def kernel(**inputs):
    maps = make_in_maps(**inputs)[:NCORES]
    nc = build()
    res = run_bass_kernel_spmd(nc, maps, core_ids=list(range(NCORES)))
    out = np.stack([np.asarray(res.results[b]["yout"], dtype=np.float32) for b in range(4)], axis=0)
    return out
```

```python
import math
from contextlib import ExitStack
import numpy as np
import concourse.bass as bass
import concourse.mybir as mybir
from concourse.bass_utils import run_bass_kernel_spmd

F32 = mybir.dt.float32
BF16 = mybir.dt.bfloat16
ALU = mybir.AluOpType
AF = mybir.ActivationFunctionType
AX = mybir.AxisListType

ENGS = ["pe", "act", "dve", "pool", "sp"]
SIG_LIM = 30000


class Buf:
    __slots__ = ("name", "t", "w", "r", "excl")

    def __init__(self, name, t, excl=False):
        self.name = name
        self.t = t
        self.w = {}
        self.r = {}
        self.excl = excl

    def __getitem__(self, idx):
        return self.t[idx]


class Op:
    __slots__ = ("eng", "fn", "deps", "signal", "sig_no", "dma", "sem_i", "val")

    def __init__(self, eng, fn, dma):
        self.eng = eng
        self.fn = fn
        self.deps = []
        self.signal = False
        self.sig_no = 0
        self.dma = dma
        self.sem_i = 0
        self.val = 0


class Prog:
    def __init__(self, nc):
        self.nc = nc
        self.ops = {e: [] for e in ENGS}
        self.n_dma = {e: 0 for e in ENGS}
        self.n_dma_sems = {"sp": 40, "pool": 4, "act": 8, "pe": 1, "dve": 1}
        self.out_dmas = []
        self.dma_since = {e: [] for e in ENGS}
        self.bar_bufs = None

    def add(self, eng, fn, reads=(), writes=(), dma=False, is_output=False):
        op = Op(eng, fn, dma)
        key = op if dma else eng
        deps = {}
        for b in reads:
            for k, w in b.w.items():
                deps[id(w)] = w
            if b.excl:
                for k, r in b.r.items():
                    if k != eng:
                        deps[id(r)] = r
        for b in writes:
            for k, w in b.w.items():
                if dma or k != eng:
                    deps[id(w)] = w
            for k, r in b.r.items():
                if dma or k != eng:
                    deps[id(r)] = r
        for b in reads:
            b.r[key] = op
        for b in writes:
            b.w = {key: op}
            b.r = {}
        op.deps = list(deps.values())
        for d in op.deps:
            d.signal = True
        if dma:
            ns = self.n_dma_sems[eng]
            i = self.n_dma[eng]
            self.n_dma[eng] += 1
            op.sem_i = i % ns
            op.val = 16 * (i // ns + 1)
            op.signal = True
            self.dma_since[eng].append(op)
            if len(self.dma_since[eng]) > ns:
                self.dma_since[eng] = self.dma_since[eng][-ns:]
            if is_output:
                self.out_dmas.append(op)
        self.ops[eng].append(op)
        return op

    def dma(self, q, out_ap, in_ap, reads=(), writes=(), is_output=False, slow=False):
        if slow:
            return self.add(q, lambda e: e.dma_start(out=out_ap, in_=in_ap, allow_slow_non_contiguous=True), reads, writes,
                            dma=True, is_output=is_output)
        return self.add(q, lambda e: e.dma_start(out=out_ap, in_=in_ap), reads, writes, dma=True,
                        is_output=is_output)

    def barrier(self):
        bb = self.bar_bufs
        firsts = []
        for e in ENGS:
            if e == "sp":
                op = self.add("sp", lambda q: q.dma_start(out=bb["sp"][0:1, 0:4], in_=bb["spsrc"][0:1, 0:4]),
                              writes=[bb["sp"]], dma=True)
                for q in ENGS:
                    for d in self.dma_since[q]:
                        if d is not op:
                            op.deps.append(d)
                            d.signal = True
            elif e == "pe":
                op = self.add("pe", lambda t: t.matmul(bb["pe"][0:1, 0:1], lhsT=bb["pesrc"][0:1, 0:1],
                                                        rhs=bb["pesrc"][0:1, 0:1], start=True, stop=True),
                              writes=[bb["pe"]])
            else:
                b = bb[e]
                if e == "act":
                    op = self.add(e, (lambda b: (lambda g: g.memzero(b[0:1, 0:4])))(b), writes=[b])
                else:
                    op = self.add(e, (lambda b: (lambda g: g.memset(b[0:1, 0:4], 0.0)))(b), writes=[b])
            firsts.append(op)
        for e in ENGS:
            if e == "sp":
                op = self.add("sp", lambda q: q.dma_start(out=bb["sp2"][0:1, 0:4], in_=bb["spsrc"][0:1, 0:4]),
                              writes=[bb["sp2"]], dma=True)
            elif e == "pe":
                op = self.add("pe", lambda t: t.matmul(bb["pe"][0:1, 1:2], lhsT=bb["pesrc"][0:1, 0:1],
                                                        rhs=bb["pesrc"][0:1, 0:1], start=True, stop=True),
                              writes=[])
            else:
                b = bb[e + "2"]
                if e == "act":
                    op = self.add(e, (lambda b: (lambda g: g.memzero(b[0:1, 0:4])))(b), writes=[b])
                else:
                    op = self.add(e, (lambda b: (lambda g: g.memset(b[0:1, 0:4], 0.0)))(b), writes=[b])
            for f in firsts:
                if f.eng != e:
                    op.deps.append(f)
                    f.signal = True
        for q in ENGS:
            self.dma_since[q] = []

    def emit(self, E):
        nc = self.nc
        nsig = {}
        for e in ENGS:
            n = 0
            for op in self.ops[e]:
                if op.signal and not op.dma:
                    n += 1
                    op.sig_no = n
            nsig[e] = n
        esem = {}
        for e in ENGS:
            k = max(1, (nsig[e] + SIG_LIM - 1) // SIG_LIM)
            esem[e] = [E(nc.semaphore(f"s_{e}_{j}")) for j in range(k)]
        dsem = {}
        for e in ENGS:
            if self.n_dma[e] > 0:
                dsem[e] = [E(nc.semaphore(f"d_{e}_{j}")) for j in range(self.n_dma_sems[e])]
        block = E(nc.Block())
        engobj = {"pe": "tensor", "act": "scalar", "dve": "vector", "pool": "gpsimd", "sp": "sync"}

        def dep_wait(op):
            if op.dma:
                return dsem[op.eng][op.sem_i], op.val, ("d", op.eng, op.sem_i)
            ep = (op.sig_no - 1) // SIG_LIM
            return esem[op.eng][ep], op.sig_no - ep * SIG_LIM, ("e", op.eng, ep)

        out_dmas = self.out_dmas

        def make_body(e):
            def body(eng):
                waited = {}
                for op in self.ops[e]:
                    for d in op.deps:
                        sem, val, key = dep_wait(d)
                        if waited.get(key, 0) >= val:
                            continue
                        waited[key] = val
                        eng.wait_ge(sem, val)
                    if op.dma and op.val > 16:
                        key = ("d", e, op.sem_i)
                        if waited.get(key, 0) < op.val - 16:
                            waited[key] = op.val - 16
                            eng.wait_ge(dsem[e][op.sem_i], op.val - 16)
                    inst = op.fn(eng)
                    if op.dma:
                        inst.then_inc(dsem[e][op.sem_i], 16)
                    elif op.signal:
                        ep = (op.sig_no - 1) // SIG_LIM
                        inst.then_inc(esem[e][ep], 1)
                if e == "sp":
                    for d in out_dmas:
                        sem, val, key = dep_wait(d)
                        if waited.get(key, 0) >= val:
                            continue
                        waited[key] = val
                        eng.wait_ge(sem, val)
            return body

        for e in ENGS:
            if self.ops[e] or e == "sp":
                getattr(block, engobj[e])(make_body(e))


D = 2048
KT = 16
NCTX = 256
NLAT = 4096
T = NCTX + NLAT
NTILE = T // 128
EPS = 1e-6
GN_EPS = 64e-5
LORA = 96
C0 = 64
C1 = 128
EXPM05 = math.exp(-0.5)

DEBUG_OUT = set()
NCORES = 4
import os as _os
STOP_AFTER = _os.environ.get('KSTOP') or None


def build():
    nc = bass.Bass("TRN2", target_bir_lowering=False)

    def din(name, shape):
        return nc.dram_tensor(name, list(shape), F32, kind="ExternalInput").ap()

    def scratch(name, shape, dt=F32):
        kind = "ExternalOutput" if name in DEBUG_OUT else "Internal"
        return nc.dram_tensor(name, list(shape), dt, kind=kind).ap()

    xin = din("xin", [T, D])
    cvec = din("cvec", [2, D])
    ada_w = din("ada_w", [2, D, 3 * D])
    ada_b = din("ada_b", [2, 3 * D])
    norm_g = din("norm_g", [2, D])
    rk_mix = din("rk_mix", [6, D])
    rk_w_in = din("rk_w_in", [4, D, D])
    rk_w0 = din("rk_w0", [2, D])
    rk_w1 = din("rk_w1", [2, D, LORA])
    rk_w2 = din("rk_w2", [2, LORA, D])
    rk_a0 = din("rk_a0", [2, D])
    rk_a1 = din("rk_a1", [2, D, LORA])
    rk_a2 = din("rk_a2", [2, LORA, D])
    rk_k_k = din("rk_k_k", [D])
    rk_k_a = din("rk_k_a", [D])
    rk_r_k = din("rk_r_k", [D])
    rk_ln_g = din("rk_ln_g", [D])
    rk_ln_b = din("rk_ln_b", [D])
    rk_w_out = din("rk_w_out", [D, D])
    rt_w_in = din("rt_w_in", [D, 6 * D])
    rt_dl = din("rt_dl", [16])
    rt_gn_g = din("rt_gn_g", [2 * D])
    rt_w_out = din("rt_w_out", [2 * D, D])
    final_g = din("final_g", [D])
    ropec = din("ropec", [128, NLAT])
    ropes = din("ropes", [128, NLAT])
    yout = nc.dram_tensor("yout", [NLAT, D], F32, kind="ExternalOutput").ap()

    Wb_r = scratch("Wb_r", [16, 128, KT, 128], BF16)
    Wb_k = scratch("Wb_k", [16, 128, KT, 128], BF16)
    Wb_v = scratch("Wb_v", [8, 128, KT, 256], BF16)
    Wb_g = scratch("Wb_g", [8, 128, KT, 256], BF16)
    Wb_out0 = scratch("Wb_out0", [D, D], BF16)
    Wb_qk = scratch("Wb_qk", [32, 128, KT, 128], BF16)
    Wb_vg = scratch("Wb_vg", [16, 128, KT, 512], BF16)
    Wb_o1 = scratch("Wb_o1", [8, 128, 32, 256], BF16)
    ADA = scratch("ADA", [2, 2, 3, D])
    HT0 = scratch("HT0", [KT, 128, T])
    HT1 = scratch("HT1", [KT, 128, T], BF16)
    RT = scratch("RT", [KT, 128, T])
    KKT = scratch("KKT", [KT, 128, T])
    KDT = [scratch(f"KDT{d}", [KT, 128, T]) for d in range(2)]
    BT = [scratch(f"BT{d}", [KT, 128, T]) for d in range(2)]
    LW = [scratch(f"LW{d}", [T, D]) for d in range(2)]
    V0 = scratch("V0", [T, D])
    SG0 = scratch("SG0", [T, D])
    BON = scratch("BON", [T, 32])
    O0 = [scratch(f"O0_{d}", [T, D]) for d in range(2)]
    X1 = scratch("X1", [T, D])
    QT = [scratch(f"QT{d}", [KT, 128, T], BF16) for d in range(2)]
    KTT = [scratch(f"KTT{d}", [KT, 128, T], BF16) for d in range(2)]
    V1 = scratch("V1", [T, 2 * D], BF16)
    SG1 = scratch("SG1", [T, 2 * D])
    O1 = [scratch(f"O1_{d}", [T, 2 * D]) for d in range(2)]

    with ExitStack() as es:
        E = es.enter_context
        P = Prog(nc)
        cnt = [0]

        def sb(name, shape, dt=F32, stack=None):
            cnt[0] += 1
            return Buf(name, (stack or E)(nc.sbuf_tensor(f"{name}_{cnt[0]}", list(shape), dt)))

        P.bar_bufs = {k: sb("bar_" + k, [1, 8]) for k in ["sp", "sp2", "spsrc", "act", "act2", "dve", "dve2", "pool", "pool2"]}
        P.bar_bufs["pesrc"] = sb("bar_pesrc", [1, 8])
        PS = [Buf(f"ps{i}", E(nc.psum_tensor(f"ps{i}", [128, 512], F32)), excl=True) for i in range(7)]
        P.bar_bufs["pe"] = Buf("ps_bar", E(nc.psum_tensor("ps_bar", [128, 512], F32)))
        P.add("pool", lambda e: e.memset(P.bar_bufs["pesrc"][:], 0.0), writes=[P.bar_bufs["pesrc"]])
        P.add("pool", lambda e: e.memset(P.bar_bufs["spsrc"][:], 0.0), writes=[P.bar_bufs["spsrc"]])
        psi = [0]

        psn = [7]

        def ps():
            psi[0] = (psi[0] + 1) % psn[0]
            return PS[psi[0]]

        def tt(eng, out, in0, in1, op, R, W):
            P.add(eng, lambda e: e.tensor_tensor(out=out, in0=in0, in1=in1, op=op), reads=R, writes=W)

        def ts(eng, out, in0, s1, s2, op0, op1, R, W):
            if s2 is None:
                P.add(eng, lambda e: e.tensor_scalar(out=out, in0=in0, scalar1=s1, scalar2=None, op0=op0), reads=R, writes=W)
            else:
                P.add(eng, lambda e: e.tensor_scalar(out=out, in0=in0, scalar1=s1, scalar2=s2, op0=op0, op1=op1), reads=R, writes=W)

        def stt(eng, out, in0, s, in1, op0, op1, R, W):
            P.add(eng, lambda e: e.scalar_tensor_tensor(out=out, in0=in0, scalar=s, in1=in1, op0=op0, op1=op1), reads=R, writes=W)

        def act(out, in_, func, R, W, bias=None, scale=None, accum=None):
            kw = {}
            if bias is not None:
                kw["bias"] = bias
            if scale is not None:
                kw["scale"] = scale
            if accum is not None:
                kw["accum_out"] = accum
            P.add("act", lambda e: e.activation(out=out, in_=in_, func=func, **kw), reads=R, writes=W)

        def cp(eng, out, in_, R, W):
            if eng == "act":
                P.add("act", lambda e: e.copy(out=out, in_=in_), reads=R, writes=W)
            else:
                P.add(eng, lambda e: e.tensor_copy(out=out, in_=in_), reads=R, writes=W)

        def mm(out, lhsT, rhs, start, stop, R, W):
            P.add("pe", lambda e: e.matmul(out, lhsT=lhsT, rhs=rhs, start=start, stop=stop), reads=R, writes=W)

        def tr(out, in_, ident, R, W):
            P.add("pe", lambda e: e.transpose(out, in_, ident), reads=R, writes=W)

        def memset(eng, ap, val, W):
            P.add(eng, lambda e: e.memset(ap, val), writes=W)

        def red(eng, out, in_, R, W):
            P.add(eng, lambda e: e.tensor_reduce(out=out, in_=in_, axis=AX.X, op=ALU.add), reads=R, writes=W)

        def recip(out, in_, R, W):
            P.add("dve", lambda e: e.reciprocal(out=out, in_=in_), reads=R, writes=W)

        def ftv(ap1d):
            return ap1d.rearrange("(k p) -> p k", p=128)

        ident = sb("ident", [128, 128])
        identb = sb("identb", [128, 128], BF16)
        P.add("pool", lambda e: e.memset(ident[:], 1.0), writes=[ident])
        P.add("pool", lambda e: e.affine_select(out=ident[:], in_=ident[:], pattern=[[-1, 128]], compare_op=ALU.is_equal,
                                                fill=0.0, base=0, channel_multiplier=1), reads=[ident], writes=[ident])
        cp("dve", identb[:], ident[:], [ident], [identb])
        triU = sb("triU", [128, 128]); triUs = sb("triUs", [128, 128]); triL = sb("triL", [128, 128]); triLs = sb("triLs", [128, 128])

        def mk_tri(tb, cmp_, sg):
            P.add("pool", lambda e: e.memset(tb[:], 1.0), writes=[tb])
            P.add("pool", lambda e: e.affine_select(out=tb[:], in_=tb[:], pattern=[[sg, 128]], compare_op=cmp_,
                                                    fill=0.0, base=0, channel_multiplier=-sg), reads=[tb], writes=[tb])
        mk_tri(triU, ALU.is_ge, 1)
        mk_tri(triUs, ALU.is_gt, 1)
        mk_tri(triL, ALU.is_ge, -1)
        mk_tri(triLs, ALU.is_gt, -1)
        blk = sb("blk", [128, 128])
        P.add("pool", lambda e: e.memset(blk[:], 0.0), writes=[blk])
        P.add("pool", lambda e: e.memset(blk[0:64, 0:64], 1.0), writes=[blk])
        P.add("pool", lambda e: e.memset(blk[64:128, 64:128], 1.0), writes=[blk])
        blkb = sb("blkb", [128, 128], BF16)
        cp("dve", blkb[:], blk[:], [blk], [blkb])
        triUsb = sb("triUsb", [128, 128], BF16)
        cp("dve", triUsb[:], triUs[:], [triUs], [triUsb])
        tribI = []; tribS = []
        for d_, (si, ss) in enumerate(((triU, triUs), (triL, triLs))):
            bi = sb(f"tribI{d_}", [64, 64], BF16); bs = sb(f"tribS{d_}", [64, 64], BF16)
            cp("dve", bi[:], si[0:64, 0:64], [si], [bi]); cp("dve", bs[:], ss[0:64, 0:64], [ss], [bs])
            tribI.append(bi); tribS.append(bs)
        ind2 = sb("ind2", [128, 2])
        P.add("pool", lambda e: e.memset(ind2[:], 0.0), writes=[ind2])
        P.add("pool", lambda e: e.memset(ind2[0:64, 0:1], 1.0), writes=[ind2])
        P.add("pool", lambda e: e.memset(ind2[64:128, 1:2], 1.0), writes=[ind2])
        mS4 = [sb(f"mS4_{d}", [128, 4, 128]) for d in range(2)]
        mI4 = [sb(f"mI4_{d}", [128, 4, 128]) for d in range(2)]
        for d in range(2):
            srcS = triUs if d == 0 else triLs
            srcI = triU if d == 0 else triL
            for r_ in range(4):
                tt("pool", mS4[d][:, r_, :], srcS[:], blk[:], ALU.mult, [srcS, blk], [mS4[d]])
                tt("pool", mI4[d][:, r_, :], srcI[:], blk[:], ALU.mult, [srcI, blk], [mI4[d]])

        def pv(w2d, c0, e):
            return w2d[:, c0:c0 + e].rearrange("(k p) e -> p k e", p=128)
        for t_ in range(16):
            P.dma("pool", Wb_r[t_], pv(rk_w_in[0], t_ * 128, 128))
            P.dma("pool", Wb_k[t_], pv(rk_w_in[1], t_ * 128, 128))
        for t_ in range(8):
            P.dma("pool", Wb_v[t_], pv(rk_w_in[2], t_ * 256, 256))
            P.dma("pool", Wb_g[t_], pv(rk_w_in[3], t_ * 256, 256))
        for r_ in range(4):
            P.dma("pool", Wb_out0[r_ * 512:(r_ + 1) * 512, :], rk_w_out[r_ * 512:(r_ + 1) * 512, :])
        cast1 = []
        for t_ in range(32):
            cast1.append((Wb_qk[t_], pv(rt_w_in, t_ * 128, 128)))
        for t_ in range(16):
            cast1.append((Wb_vg[t_], pv(rt_w_in, 2 * D + t_ * 512, 512)))
        for t_ in range(8):
            cast1.append((Wb_o1[t_], pv(rt_w_out, t_ * 256, 256)))
        cast1_it = iter(cast1)

        def issue_cast1(n=1):
            for _ in range(n):
                nx = next(cast1_it, None)
                if nx is not None:
                    P.dma("pool", nx[0], nx[1])

        def adaln(layer):
            with ExitStack() as ph:
                cs = sb("cs", [128, KT, 2], stack=ph.enter_context)
                cst = sb("cst", [128, 2, KT], stack=ph.enter_context)
                for r_ in range(2):
                    P.dma("sp", cst[:, r_, :], ftv(cvec[r_, :]), writes=[cst], slow=True)
                act(cst[:], cst[:], AF.Silu, [cst], [cst])
                cp("dve", cs[:].rearrange("p k r -> p r k"), cst[:], [cst], [cs])
                wts = [sb(f"adaw{i}", [128, KT, 512], stack=ph.enter_context) for i in range(2)]
                bia = [sb(f"adab{i}", [2, 512], stack=ph.enter_context) for i in range(2)]
                ng = sb("ng", [2, 512], stack=ph.enter_context)
                res = [sb(f"adar{i}", [2, 512], stack=ph.enter_context) for i in range(2)]
                for cg in range(12):
                    w_ = wts[cg % 2]; b_ = bia[cg % 2]; r_ = res[cg % 2]
                    P.dma("sp", w_[:], ada_w[layer, :, cg * 512:(cg + 1) * 512].rearrange("(k p) e -> p k e", p=128), writes=[w_])
                    P.dma("sp", b_[:], ada_b[layer, cg * 512:(cg + 1) * 512].partition_broadcast(2), writes=[b_])
                    pp = ps()
                    for kt in range(KT):
                        mm(pp[0:2, :], cs[:, kt, :], w_[:, kt, :], kt == 0, kt == KT - 1, [cs, w_], [pp])
                    tt("dve", r_[:], pp[0:2, :], b_[:], ALU.add, [pp, b_], [r_])
                    which = cg // 4
                    if which == 1:
                        c4 = cg % 4
                        P.dma("sp", ng[:], norm_g[layer, c4 * 512:(c4 + 1) * 512].partition_broadcast(2), writes=[ng])
                        stt("dve", r_[:], r_[:], 1.0, ng[:], ALU.add, ALU.mult, [r_, ng], [r_])
                    c4 = cg % 4
                    for row in range(2):
                        P.dma("sp", ADA[layer, row, which, c4 * 512:(c4 + 1) * 512].unsqueeze(0), r_[row:row + 1, :], reads=[r_])
            P.barrier()

        def phase_h(layer, src, HTdst, hdt):
            with ExitStack() as ph:
                S = ph.enter_context
                TA = sb("TA", [128, D], stack=S); TB = sb("TB", [128, D], stack=S)
                xs = [sb(f"hx{i}", [128, D], stack=S) for i in range(2)]
                hs = [sb(f"hh{i}", [128, D], stack=S) for i in range(2)]
                hts = [sb(f"hT{i}", [128, KT, 128], hdt, stack=S) for i in range(2)]
                junk = sb("junk", [128, D], stack=S)
                st = [sb(f"hst{i}", [128, 4], stack=S) for i in range(2)]
                for i in range(NTILE):
                    row = 1 if i < 2 else 0
                    if i == 0 or i == 2:
                        P.dma("sp", TA[:], ADA[layer, row, 1, :].partition_broadcast(128), writes=[TA])
                        P.dma("sp", TB[:], ADA[layer, row, 0, :].partition_broadcast(128), writes=[TB])
                    x_ = xs[i % 2]; h_ = hs[i % 2]; hT = hts[i % 2]; s_ = st[i % 2]
                    if i == 0:
                        P.dma("sp", x_[:], src[0:128, :], writes=[x_])
                    if i + 1 < NTILE:
                        P.dma("sp", xs[(i + 1) % 2][:], src[(i + 1) * 128:(i + 2) * 128, :], writes=[xs[(i + 1) % 2]])
                    act(junk[:], x_[:], AF.Square, [x_], [junk, s_], accum=s_[:, 0:1])
                    ts("dve", s_[:, 1:2], s_[:, 0:1], 1.0 / D, EPS, ALU.mult, ALU.add, [s_], [s_])
                    act(s_[:, 2:3], s_[:, 1:2], AF.Sqrt, [s_], [s_])
                    recip(s_[:, 3:4], s_[:, 2:3], [s_], [s_])
                    stt("dve", h_[:], x_[:], s_[:, 3:4], TA[:], ALU.mult, ALU.mult, [x_, s_, TA], [h_])
                    tt("pool", h_[:], h_[:], TB[:], ALU.add, [h_, TB], [h_])
                    for g in range(4):
                        pp = ps()
                        for q in range(4):
                            kt = g * 4 + q
                            tr(pp[:, q * 128:(q + 1) * 128], h_[:, kt * 128:(kt + 1) * 128], ident[:], [h_, ident], [pp])
                        cp("act" if g % 2 == 0 else "dve", hT[:, g * 4:(g + 1) * 4, :],
                           pp[:].rearrange("p (q t) -> p q t", q=4), [pp], [hT])
                    P.dma("sp", HTdst[:, :, i * 128:(i + 1) * 128].rearrange("k p t -> p k t"), hT[:], reads=[hT])
            P.barrier()

        def phase_p0():
            with ExitStack() as ph:
                S = ph.enter_context
                W = 256
                HW = sb("HW", [128, KT, W + 128], stack=S)
                dT = sb("dT", [128, KT, W], stack=S)
                xm = {j: sb(f"xm{j}", [128, KT, W], BF16, stack=S) for j in ("r", "k", "t0")}
                xm["t1"] = xm["t0"]
                mixT = sb("mixT", [128, 6, KT], stack=S)
                for j in range(6):
                    P.dma("sp", mixT[:, j, :], ftv(rk_mix[j, :]), writes=[mixT], slow=True)
                prm = sb("prm", [128, 8, KT], stack=S)
                for i_, src_ in enumerate([rk_k_k, rk_k_a, rk_k_a, rk_r_k, rk_a0[0, :], rk_a0[1, :]]):
                    P.dma("sp", prm[:, i_, :], ftv(src_), writes=[prm], slow=True)
                ts("dve", prm[:, 2, :], prm[:, 2, :], -1.0, 1.0, ALU.mult, ALU.add, [prm], [prm])
                w1b = [sb(f"w1b{d}", [128, KT, LORA], BF16, stack=S) for d in range(2)]
                a1b = [sb(f"a1b{d}", [128, KT, LORA], BF16, stack=S) for d in range(2)]
                w2b = [sb(f"w2b{d}", [LORA, D], BF16, stack=S) for d in range(2)]
                a2b = [sb(f"a2b{d}", [LORA, D], BF16, stack=S) for d in range(2)]
                W0t = [sb(f"W0t{d}", [128, D], stack=S) for d in range(2)]
                for d in range(2):
                    P.dma("pool", w1b[d][:], rk_w1[d].rearrange("(k p) r -> p k r", p=128), writes=[w1b[d]])
                    P.dma("pool", a1b[d][:], rk_a1[d].rearrange("(k p) r -> p k r", p=128), writes=[a1b[d]])
                    P.dma("pool", w2b[d][:], rk_w2[d], writes=[w2b[d]])
                    P.dma("pool", a2b[d][:], rk_a2[d], writes=[a2b[d]])
                    P.dma("sp", W0t[d][:], rk_w0[d, :].partition_broadcast(128), writes=[W0t[d]])
                thT = [sb(f"thT{d}", [LORA, W], BF16, stack=S) for d in range(2)]
                xa1 = [sb(f"xa1{d}", [LORA, W], BF16, stack=S) for d in range(2)]
                big = [sb(f"big{i}", [128, D], stack=S) for i in range(2)]
                wpc = [sb(f"wpc{i}", [128, KT, 256], BF16, stack=S) for i in range(2)]
                wrk = [sb(f"wrk{i}", [128, KT, 128], BF16, stack=S) for i in range(4)]
                fm = {n: [sb(f"fm_{n}{i}", [128, W], stack=S) for i in range(1)] * 2 for n in
                      ("r", "k", "a0", "a1", "kkr", "sq", "rn", "kk", "t1", "kd0", "kd1", "b0", "b1", "ks", "rk")}
                bon = [sb(f"bon{i}", [128, 32], stack=S) for i in range(2)]
                sqb = sb("sqb", [128, W], BF16, stack=S)
                rkb = sb("rkb", [128, W], BF16, stack=S)
                IND = sb("IND", [128, KT, 32], BF16, stack=S)
                memset("pool", IND[:], 0.0, [IND])
                for et_ in range(KT):
                    memset("pool", IND[0:64, et_, 2 * et_:2 * et_ + 1], 1.0, [IND])
                    memset("pool", IND[64:128, et_, 2 * et_ + 1:2 * et_ + 2], 1.0, [IND])
                psb = [PS[5], PS[6]]
                psn[0] = 5
                nwin = T // W
                import os
                nwin_run = int(os.environ.get('P0_NWIN', nwin))
                p0step = int(os.environ.get('P0_STEP', 9))
                p0sub = int(os.environ.get('P0_SUB', 9))
                big_i = [0]; wpc_i = [0]; wrk_i = [0]
                for w in range(nwin_run):
                    t0 = w * W
                    isctx = (w == 0)
                    if p0step < 1:
                        continue
                    def load_window(w_):
                        t0_ = w_ * W
                        c_ = (w_ == 0)
                        lo = max(t0_ - 64, 0 if c_ else NCTX)
                        hi = min(t0_ + W + 64, NCTX if c_ else T)
                        if w_ in (0, 1, nwin - 1):
                            memset("pool", HW[:], 0.0, [HW])
                        P.dma("sp", HW[:, :, lo - (t0_ - 64):hi - (t0_ - 64)], HT0[:, :, lo:hi].rearrange("k p t -> p k t"), writes=[HW])
                    if w == 0:
                        load_window(0)
                    ctr = HW[:, :, 64:64 + W]
                    if isctx:
                        groups = [(0, 8, -1), (8, 16, 1)]
                    else:
                        groups = [(0, 4, -1), (4, 8, 1), (8, 12, -64), (12, 16, 64)]
                    for gi, (k0, k1, s_) in enumerate(groups):
                        tt("dve" if gi % 2 == 0 else "pool", dT[:, k0:k1, :], HW[:, k0:k1, 64 + s_:64 + s_ + W], HW[:, k0:k1, 64:64 + W],
                           ALU.subtract, [HW], [dT])
                    if not isctx:
                        d4 = dT[:].rearrange("p k (r c) -> p k r c", c=64)
                        h4 = ctr.rearrange("p k (r c) -> p k r c", c=64)
                        ts("dve", d4[:, 0:4, :, 0], h4[:, 0:4, :, 0], -1.0, None, ALU.mult, None, [HW, dT], [dT])
                        ts("dve", d4[:, 4:8, :, 63], h4[:, 4:8, :, 63], -1.0, None, ALU.mult, None, [HW, dT], [dT])

                    def mkxm(j, dst, engs=("dve",)):
                        for kt in range(KT):
                            stt(engs[kt % len(engs)], dst[:, kt, :], dT[:, kt, :], mixT[:, j, kt:kt + 1], ctr[:, kt, :],
                                ALU.mult, ALU.add, [dT, mixT, HW], [dst])

                    if p0step < 2:
                        continue
                    mkxm(4, xm["t0"])
                    for d in range(2):
                        pp = ps()
                        for kt in range(KT):
                            mm(pp[0:LORA, 0:W], w1b[d][:, kt, :], xm["t0"][:, kt, :], kt == 0, kt == KT - 1, [w1b[d], xm["t0"]], [pp])
                        act(thT[d][:], pp[0:LORA, 0:W], AF.Tanh, [pp], [thT[d]])
                    for d in range(2):
                        for tt_ in range(2):
                            bg = big[big_i[0] % 2]; big_i[0] += 1
                            for cg in range(4):
                                pp = ps()
                                mm(pp[:], thT[d][:, tt_ * 128:(tt_ + 1) * 128], w2b[d][:, cg * 512:(cg + 1) * 512], True, True, [thT[d], w2b[d]], [pp])
                                tt("dve", bg[:, cg * 512:(cg + 1) * 512], pp[:], W0t[d][:, cg * 512:(cg + 1) * 512], ALU.add, [pp, W0t[d]], [bg])
                            act(bg[:], bg[:], AF.Sigmoid, [bg], [bg])
                            ts("pool", bg[:], bg[:], -EXPM05, None, ALU.mult, None, [bg], [bg])
                            P.dma("sp", LW[d][t0 + tt_ * 128:t0 + (tt_ + 1) * 128, :], bg[:], reads=[bg])
                    if p0step < 3:
                        continue
                    mkxm(5, xm["t1"])
                    for d in range(2):
                        pp = ps()
                        for kt in range(KT):
                            mm(pp[0:LORA, 0:W], a1b[d][:, kt, :], xm["t1"][:, kt, :], kt == 0, kt == KT - 1, [a1b[d], xm["t1"]], [pp])
                        cp("act", xa1[d][:], pp[0:LORA, 0:W], [pp], [xa1[d]])
                    if p0step < 4:
                        continue
                    def ld_vg(idx):
                        P.dma("sp", wpc[idx % 2][:], (Wb_v if idx < 8 else Wb_g)[idx % 8], writes=[wpc[idx % 2]])
                    ld_vg(0)
                    for j, dst_dram, key in ((2, V0, "t0"), (3, SG0, "t1")):
                        mkxm(j, xm[key])
                        bgs = [big[0], big[1]]
                        for cg in range(8):
                            idx_ = (j - 2) * 8 + cg
                            wp = wpc[idx_ % 2]
                            if idx_ + 1 < 16:
                                ld_vg(idx_ + 1)
                            for tt_ in range(2):
                                pp = ps()
                                for kt in range(KT):
                                    mm(pp[:, 0:256], xm[key][:, kt, tt_ * 128:(tt_ + 1) * 128], wp[:, kt, :], kt == 0, kt == KT - 1, [xm[key], wp], [pp])
                                if j == 2:
                                    cp("act", bgs[tt_][:, cg * 256:(cg + 1) * 256], pp[:, 0:256], [pp], [bgs[tt_]])
                                else:
                                    act(bgs[tt_][:, cg * 256:(cg + 1) * 256], pp[:, 0:256], AF.Silu, [pp], [bgs[tt_]])
                        for tt_ in range(2):
                            P.dma("sp", dst_dram[t0 + tt_ * 128:t0 + (tt_ + 1) * 128, :], bgs[tt_][:], reads=[bgs[tt_]])
                    if p0step < 5:
                        continue
                    mkxm(0, xm["r"])
                    mkxm(1, xm["k"])
                    if w + 1 < nwin_run:
                        load_window(w + 1)
                    def ld_rk(et_):
                        a_ = wrk[(2 * et_) % 4]; b_ = wrk[(2 * et_ + 1) % 4]
                        P.dma("sp", a_[:], Wb_r[et_], writes=[a_])
                        P.dma("sp", b_[:], Wb_k[et_], writes=[b_])
                    ld_rk(0)
                    for et in range(KT):
                        i2 = et % 2
                        wr = wrk[(2 * et) % 4]; wk = wrk[(2 * et + 1) % 4]
                        if et + 1 < KT:
                            ld_rk(et + 1)
                        p1 = ps(); p2 = ps()
                        psr = p1[:, 0:W]; psk = p1[:, W:2 * W]
                        for kt in range(KT):
                            mm(psr, wr[:, kt, :], xm["r"][:, kt, :], kt == 0, kt == KT - 1, [wr, xm["r"]], [p1])
                        for kt in range(KT):
                            mm(psk, wk[:, kt, :], xm["k"][:, kt, :], kt == 0, kt == KT - 1, [wk, xm["k"]], [p1])
                        for d in range(2):
                            mm(p2[:, d * W:(d + 1) * W], a2b[d][:, et * 128:(et + 1) * 128], xa1[d][:], True, True, [a2b[d], xa1[d]], [p2])
                        f = {n: fm[n][i2] for n in fm}
                        cp("act", f["r"][:], psr, [p1], [f["r"]])
                        cp("act", f["k"][:], psk, [p1], [f["k"]])
                        for d in range(2):
                            act(f[f"a{d}"][:], p2[:, d * W:(d + 1) * W], AF.Sigmoid, [p2, prm], [f[f"a{d}"]], bias=prm[:, 4 + d, et:et + 1])
                        if p0sub < 2:
                            continue
                        ts("dve", f["kkr"][:], psk, prm[:, 0, et:et + 1], None, ALU.mult, None, [p1, prm], [f["kkr"]])
                        act(sqb[:], psk, AF.Square, [p1, prm], [sqb], scale=prm[:, 0, et:et + 1])
                        p3 = ps()
                        mm(p3[:, 0:W], blkb[:], sqb[:], True, True, [blkb, sqb], [p3])
                        act(f["rn"][:], p3[:, 0:W], AF.Sqrt, [p3], [f["rn"]])
                        ts("dve", f["rn"][:], f["rn"][:], 1e-12, None, ALU.max, None, [f["rn"]], [f["rn"]])
                        recip(f["rn"][:], f["rn"][:], [f["rn"]], [f["rn"]])
                        tt("dve", f["kk"][:], f["kkr"][:], f["rn"][:], ALU.mult, [f["kkr"], f["rn"]], [f["kk"]])
                        if p0sub < 3:
                            continue
                        P.dma("sp", RT[et, :, t0:t0 + W], f["r"][:], reads=[f["r"]])
                        P.dma("sp", KKT[et, :, t0:t0 + W], f["kk"][:], reads=[f["kk"]])
                        for d in range(2):
                            ts("dve", f["t1"][:], f[f"a{d}"][:], prm[:, 1, et:et + 1], prm[:, 2, et:et + 1], ALU.mult, ALU.add,
                               [f[f"a{d}"], prm], [f["t1"]])
                            tt("pool", f[f"kd{d}"][:], f["k"][:], f["t1"][:], ALU.mult, [f["k"], f["t1"]], [f[f"kd{d}"]])
                            tt("pool", f[f"b{d}"][:], f["kk"][:], f[f"a{d}"][:], ALU.mult, [f["kk"], f[f"a{d}"]], [f[f"b{d}"]])
                            P.dma("sp", KDT[d][et, :, t0:t0 + W], f[f"kd{d}"][:], reads=[f[f"kd{d}"]])
                            P.dma("sp", BT[d][et, :, t0:t0 + W], f[f"b{d}"][:], reads=[f[f"b{d}"]])
                        if p0sub < 4:
                            continue
                        tt("dve", f["ks"][:], f["kd0"][:], f["kd1"][:], ALU.add, [f["kd0"], f["kd1"]], [f["ks"]])
                        stt("dve", rkb[:], f["r"][:], prm[:, 3, et:et + 1], f["ks"][:], ALU.mult, ALU.mult, [f["r"], prm, f["ks"]], [rkb])
                        if p0step < 6:
                            continue
                        for tt_ in range(2):
                            mm(psb[tt_][:, 0:32], rkb[:, tt_ * 128:(tt_ + 1) * 128], IND[:, et, :], et == 0, et == KT - 1, [rkb, IND], [psb[tt_]])
                    if p0step < 6:
                        continue
                    for tt_ in range(2):
                        cp("act", bon[tt_][:], psb[tt_][:, 0:32], [psb[tt_]], [bon[tt_]])
                        P.dma("sp", BON[t0 + tt_ * 128:t0 + (tt_ + 1) * 128, :], bon[tt_][:], reads=[bon[tt_]])
                psn[0] = 7
            P.barrier()

        def phase_s0(d, NG=4):
            with ExitStack() as ph:
                S = ph.enter_context
                SDT = BF16
                NP = 16
                GP = NP // NG
                NB = GP // 4
                ld = {n: [sb(f"s_{n}{i}", [128, NP, C0], stack=S) for i in range(2)] for n in ("r", "kd", "kk", "b", "v")}
                lwc = [sb(f"s_lw{i}", [C0, D], stack=S) for i in range(2)]
                lwh = sb("s_lwh", [C0, D], BF16, stack=S); lwl = sb("s_lwl", [C0, D], BF16, stack=S)

                def gb(name, shape, dt=F32):
                    return [sb(f"{name}{g}", shape, dt, stack=S) for g in range(NG)]
                vcb = gb("s_vcb", [128, GP, C0], SDT)
                eP = gb("s_eP", [128, GP, C0]); ePx = gb("s_ePx", [128, GP, C0]); eN = gb("s_eN", [128, GP, C0])
                ex = {n: gb(f"s_ex{n}", [128, GP, 128], SDT) for n in ("A", "B", "K", "R")}
                for n in ex:
                    for g in range(NG):
                        memset("pool", ex[n][g][:], 0.0, [ex[n][g]])
                Xs = [gb(f"s_X{i}", [128, GP, 128], SDT) for i in range(2)]
                Ls = [gb(f"s_L{i}", [128, GP, 128], SDT) for i in range(2)]
                Mak = gb("s_Mak", [128, GP, 128], SDT); Mrb = gb("s_Mrb", [128, GP, 128], SDT); Mrk = gb("s_Mrk", [128, GP, 128], SDT)
                BTe = gb("s_BTe", [128, GP, 128], SDT); KTe = gb("s_KTe", [128, GP, 128], SDT)
                ST32 = gb("s_ST32", [128, GP, C0]); STb = gb("s_STb", [128, GP, C0], SDT)
                Y32 = gb("s_Y32", [128, GP, C0]); Yb = gb("s_Yb", [128, GP, C0], SDT)
                oc = [gb(f"s_oc{i}", [128, GP, C0]) for i in range(2)]
                for g in range(NG):
                    memset("pool", ST32[g][:], 0.0, [ST32[g]])
                    memset("pool", STb[g][:], 0.0, [STb[g]])
                nch = T // C0
                order = list(range(nch)) if d == 0 else [3, 2, 1, 0] + list(range(nch - 1, 3, -1))
                tl = C0 - 1 if d == 0 else 0
                GW = GP * C0

                def fview(ap, t0):
                    return ap[:, :, t0:t0 + C0].rearrange("k p t -> p k t")

                def pview(ap, t0, hh, g):
                    return ap[t0:t0 + C0, g * GP * 128:(g + 1) * GP * 128].rearrange("s (pr h v) -> h s pr v", h=2, v=64)[hh]

                def p3(p_):
                    return p_[:, 0:GW].rearrange("p (a t) -> p a t", t=64)

                def p4(p_):
                    return p_[:].rearrange("p (q t) -> p q t", q=4)

                def body(g, L_, t0, b2):
                    gs = slice(g * GP, (g + 1) * GP)
                    pA = ps(); pB = ps()
                    for pp, trib in ((pA, tribI[d]), (pB, tribS[d])):
                        for q in range(GP):
                            pr = g * GP + q
                            o_ = pp[:, q * 64:(q + 1) * 64]
                            mm(o_, lwh[:, pr * 128:(pr + 1) * 128], trib[:], True, False, [lwh, trib], [pp])
                            mm(o_, lwl[:, pr * 128:(pr + 1) * 128], trib[:], False, True, [lwl, trib], [pp])
                    act(eP[g][:], p3(pA), AF.Exp, [pA], [eP[g]])
                    act(eN[g][:], p3(pA), AF.Exp, [pA], [eN[g]], scale=-1.0)
                    act(ePx[g][:], p3(pB), AF.Exp, [pB], [ePx[g]])
                    yield
                    for hh in range(2):
                        psl = slice(hh * 64, (hh + 1) * 64)
                        csl = slice(hh * 64, (hh + 1) * 64)
                        stt("dve", ex["A"][g][psl, :, csl], L_["kk"][psl, gs, :], -1.0, ePx[g][psl, :, :], ALU.mult, ALU.mult, [L_["kk"], ePx[g]], [ex["A"][g]])
                        tt("pool", ex["B"][g][psl, :, csl], L_["b"][psl, gs, :], eN[g][psl, :, :], ALU.mult, [L_["b"], eN[g]], [ex["B"][g]])
                        tt("dve", ex["K"][g][psl, :, csl], L_["kd"][psl, gs, :], eN[g][psl, :, :], ALU.mult, [L_["kd"], eN[g]], [ex["K"][g]])
                        tt("pool", ex["R"][g][psl, :, csl], L_["r"][psl, gs, :], eP[g][psl, :, :], ALU.mult, [L_["r"], eP[g]], [ex["R"][g]])
                    cp("act", vcb[g][:], L_["v"][:, gs, :], [L_["v"]], [vcb[g]])
                    yield
                    X = Xs[0][g]; Lm = Ls[0][g]
                    specs = [(X, "B", "A", mS4[d]), (Lm, "A", "B", mS4[1 - d]), (Mak[g], "K", "A", mS4[d]), (Mrb[g], "B", "R", mI4[d]), (Mrk[g], "K", "R", mI4[d])]
                    for dst, l_, r_, msk in specs:
                        for sbk in range(NB):
                            pp = ps()
                            for q in range(4):
                                pr = sbk * 4 + q
                                mm(pp[:, q * 128:(q + 1) * 128], ex[l_][g][:, pr, :], ex[r_][g][:, pr, :], True, True, [ex[l_][g], ex[r_][g]], [pp])
                            tt("dve", dst[:, sbk * 4:(sbk + 1) * 4, :], p4(pp), msk[:], ALU.mult, [pp, msk], [dst])
                    yield
                    pY = ps()
                    for q in range(GP):
                        o_ = pY[:, q * 64:(q + 1) * 64]
                        mm(o_, ex["A"][g][:, q, :], STb[g][:, q, :], True, False, [ex["A"][g], STb[g]], [pY])
                        mm(o_, Mak[g][:, q, :], vcb[g][:, q, :], False, True, [Mak[g], vcb[g]], [pY])
                    cp("act", Y32[g][:], p3(pY), [pY], [Y32[g]])
                    cp("dve", Yb[g][:], p3(pY), [pY], [Yb[g]])
                    yield
                    cur = 0
                    for lev in range(6):
                        Xc = Xs[cur][g]; Lc = Ls[cur][g]
                        pY = ps()
                        for q in range(GP):
                            mm(pY[:, q * 64:(q + 1) * 64], Xc[:, q, :], Yb[g][:, q, :], True, True, [Xc, Yb[g]], [pY])
                        if lev < 5:
                            Xn = Xs[1 - cur][g]; Ln = Ls[1 - cur][g]
                            pxs = []
                            for sbk in range(NB):
                                pp = ps()
                                for q in range(4):
                                    pr = sbk * 4 + q
                                    mm(pp[:, q * 128:(q + 1) * 128], Lc[:, pr, :], Xc[:, pr, :], True, True, [Lc, Xc], [pp])
                                pxs.append(pp)
                            pls = []
                            if lev < 4:
                                for sbk in range(NB):
                                    pp = ps()
                                    for q in range(4):
                                        pr = sbk * 4 + q
                                        mm(pp[:, q * 128:(q + 1) * 128], Xc[:, pr, :], Lc[:, pr, :], True, True, [Lc, Xc], [pp])
                                    pls.append(pp)
                        tt("dve", Y32[g][:], Y32[g][:], p3(pY), ALU.add, [Y32[g], pY], [Y32[g]])
                        cp("pool", Yb[g][:], Y32[g][:], [Y32[g]], [Yb[g]])
                        if lev < 5:
                            for sbk, pp in enumerate(pxs):
                                cp("act", Xn[:, sbk * 4:(sbk + 1) * 4, :], p4(pp), [pp], [Xn])
                            for sbk, pp in enumerate(pls):
                                cp("act" if sbk % 2 else "dve", Ln[:, sbk * 4:(sbk + 1) * 4, :], p4(pp), [pp], [Ln])
                            cur = 1 - cur
                        yield
                    o_sb = oc[b2][g]
                    pO = ps()
                    for q in range(GP):
                        o_ = pO[:, q * 64:(q + 1) * 64]
                        mm(o_, ex["R"][g][:, q, :], STb[g][:, q, :], True, False, [ex["R"][g], STb[g]], [pO])
                        mm(o_, Mrb[g][:, q, :], Yb[g][:, q, :], False, False, [Mrb[g], Yb[g]], [pO])
                        mm(o_, Mrk[g][:, q, :], vcb[g][:, q, :], False, True, [Mrk[g], vcb[g]], [pO])
                    pts = []
                    for src_, dst in ((ex["B"][g], BTe[g]), (ex["K"][g], KTe[g])):
                        for sbk in range(NB):
                            pp = ps()
                            for q in range(4):
                                pr = sbk * 4 + q
                                mm(pp[:, q * 128:(q + 1) * 128], src_[:, pr, :], identb[:], True, True, [src_, identb], [pp])
                            pts.append((pp, dst, sbk))
                    cp("act", o_sb[:], p3(pO), [pO], [o_sb])
                    for hh in range(2):
                        P.dma("sp", pview(O0[d], t0, hh, g), o_sb[hh * 64:(hh + 1) * 64, :, :], reads=[o_sb])
                    for i_, (pp, dst, sbk) in enumerate(pts):
                        cp("act" if i_ % 2 else "dve", dst[:, sbk * 4:(sbk + 1) * 4, :], p4(pp), [pp], [dst])
                    yield
                    pS = ps()
                    for q in range(GP):
                        o_ = pS[:, q * 64:(q + 1) * 64]
                        mm(o_, BTe[g][:, q, :], Yb[g][:, q, :], True, False, [BTe[g], Yb[g]], [pS])
                        mm(o_, KTe[g][:, q, :], vcb[g][:, q, :], False, True, [KTe[g], vcb[g]], [pS])
                    tt("dve", ST32[g][:], ST32[g][:], p3(pS), ALU.add, [ST32[g], pS], [ST32[g]])
                    tt("pool", ST32[g][:], ST32[g][:], eP[g][:, :, tl:tl + 1].to_broadcast([128, GP, C0]), ALU.mult, [ST32[g], eP[g]], [ST32[g]])
                    cp("act", STb[g][:], ST32[g][:], [ST32[g]], [STb[g]])
                    yield

                def issue_loads(ci):
                    t0 = order[ci] * C0
                    b2 = ci % 2
                    L_ = {n: ld[n][b2] for n in ld}
                    lw_ = lwc[b2]
                    P.dma("sp", L_["r"][:], fview(RT, t0), writes=[L_["r"]])
                    P.dma("sp", L_["kd"][:], fview(KDT[d], t0), writes=[L_["kd"]])
                    P.dma("sp", L_["kk"][:], fview(KKT, t0), writes=[L_["kk"]])
                    P.dma("sp", L_["b"][:], fview(BT[d], t0), writes=[L_["b"]])
                    P.dma("sp", lw_[:], LW[d][t0:t0 + C0, :], writes=[lw_])
                    for hh in range(2):
                        P.dma("sp", L_["v"][hh * 64:(hh + 1) * 64, :, :],
                              V0[t0:t0 + C0, :].rearrange("s (pr h v) -> h s pr v", h=2, v=64)[hh], writes=[L_["v"]])

                issue_loads(0)
                for ci, c in enumerate(order):
                    t0 = c * C0
                    b2 = ci % 2
                    L_ = {n: ld[n][b2] for n in ld}
                    lw_ = lwc[b2]
                    if ci + 1 < len(order):
                        issue_loads(ci + 1)
                    cp("act", lwh[:], lw_[:], [lw_], [lwh])
                    tt("dve", lwl[:], lw_[:], lwh[:], ALU.subtract, [lw_, lwh], [lwl])
                    issue_cast1(1)
                    gens = [body(g, L_, t0, b2) for g in range(NG)]
                    while gens:
                        for gen in list(gens):
                            try:
                                next(gen)
                            except StopIteration:
                                gens.remove(gen)
            P.barrier()

        def phase_o0():
            with ExitStack() as ph:
                S = ph.enter_context
                WoB = sb("WoB", [128, KT, D], BF16, stack=S)
                P.dma("sp", WoB[:], Wb_out0.rearrange("(k p) e -> p k e", p=128), writes=[WoB])
                LNG = sb("LNG", [128, D], stack=S); LNB = sb("LNB", [128, D], stack=S); TG = sb("TG", [128, D], stack=S)
                P.dma("sp", LNG[:], rk_ln_g.partition_broadcast(128), writes=[LNG])
                P.dma("sp", LNB[:], rk_ln_b.partition_broadcast(128), writes=[LNB])
                bufs = {n: [sb(f"o_{n}{i}", [128, D], stack=S) for i in range(1)] * 2 for n in ("of", "ob", "v", "sg", "x")}
                sq = sb("o_sq", [128, D], stack=S)
                ybf = sb("o_ybf", [128, D], BF16, stack=S)
                bo = [sb(f"o_bon{i}", [128, 32], stack=S) for i in range(2)]
                st = [sb(f"o_st{i}", [128, 4, 32], stack=S) for i in range(2)]
                yT = [sb(f"o_yT{i}", [128, KT, 128], BF16, stack=S) for i in range(2)]
                for i in range(NTILE):
                    row = 1 if i < 2 else 0
                    if i == 0 or i == 2:
                        P.dma("sp", TG[:], ADA[0, row, 2, :].partition_broadcast(128), writes=[TG])
                    b2 = i % 2
                    B_ = {n: bufs[n][b2] for n in bufs}
                    rs = slice(i * 128, (i + 1) * 128)
                    P.dma("sp", B_["of"][:], O0[0][rs, :], writes=[B_["of"]])
                    P.dma("sp", B_["ob"][:], O0[1][rs, :], writes=[B_["ob"]])
                    P.dma("sp", B_["v"][:], V0[rs, :], writes=[B_["v"]])
                    P.dma("sp", B_["sg"][:], SG0[rs, :], writes=[B_["sg"]])
                    P.dma("sp", B_["x"][:], xin[rs, :], writes=[B_["x"]])
                    P.dma("sp", bo[b2][:], BON[rs, :], writes=[bo[b2]])
                    o = B_["of"]; s_ = st[b2]
                    o3 = o[:].rearrange("p (h v) -> p h v", v=64)
                    tt("dve", o[:], o[:], B_["ob"][:], ALU.add, [o, B_["ob"]], [o])
                    red("dve", s_[:, 0, :], o3, [o], [s_])
                    ts("dve", s_[:, 0, :], s_[:, 0, :], -1.0 / 64, None, ALU.mult, None, [s_], [s_])
                    tt("dve", o3, o3, s_[:, 0, :].unsqueeze(2).to_broadcast([128, 32, 64]), ALU.add, [o, s_], [o])
                    tt("pool", sq[:], o[:], o[:], ALU.mult, [o], [sq])
                    red("dve", s_[:, 1, :], sq[:].rearrange("p (h v) -> p h v", v=64), [sq], [s_])
                    ts("dve", s_[:, 1, :], s_[:, 1, :], 1.0 / 64, GN_EPS, ALU.mult, ALU.add, [s_], [s_])
                    act(s_[:, 2, :], s_[:, 1, :], AF.Sqrt, [s_], [s_])
                    recip(s_[:, 3, :], s_[:, 2, :], [s_], [s_])
                    tt("dve", o3, o3, s_[:, 3, :].unsqueeze(2).to_broadcast([128, 32, 64]), ALU.mult, [o, s_], [o])
                    tt("pool", o[:], o[:], LNG[:], ALU.mult, [o, LNG], [o])
                    tt("pool", o[:], o[:], LNB[:], ALU.add, [o, LNB], [o])
                    v_ = B_["v"]
                    v3 = v_[:].rearrange("p (h v) -> p h v", v=64)
                    tt("dve", v3, v3, bo[b2][:].unsqueeze(2).to_broadcast([128, 32, 64]), ALU.mult, [v_, bo[b2]], [v_])
                    tt("pool", o[:], o[:], v_[:], ALU.add, [o, v_], [o])
                    tt("dve", o[:], o[:], B_["sg"][:], ALU.mult, [o, B_["sg"]], [o])
                    yt = yT[b2]
                    cp("act", ybf[:], o[:], [o], [ybf])
                    for g in range(4):
                        pp = ps()
                        for q in range(4):
                            kt = g * 4 + q
                            mm(pp[:, q * 128:(q + 1) * 128], ybf[:, kt * 128:(kt + 1) * 128], identb[:], True, True, [ybf, identb], [pp])
                        cp("act", yt[:, g * 4:(g + 1) * 4, :], pp[:].rearrange("p (q t) -> p q t", q=4), [pp], [yt])
                    x_ = B_["x"]
                    for cg in range(4):
                        pp = ps()
                        for kt in range(KT):
                            mm(pp[:], yt[:, kt, :], WoB[:, kt, cg * 512:(cg + 1) * 512], kt == 0, kt == KT - 1, [yt, WoB], [pp])
                        cs_ = slice(cg * 512, (cg + 1) * 512)
                        tt("dve", sq[:, cs_], pp[:], TG[:, cs_], ALU.mult, [pp, TG], [sq])
                        tt("pool", x_[:, cs_], x_[:, cs_], sq[:, cs_], ALU.add, [x_, sq], [x_])
                    P.dma("sp", X1[rs, :], x_[:], reads=[x_])
            P.barrier()

        def phase_p1():
            with ExitStack() as ph:
                S = ph.enter_context
                W = 256
                hT = [sb(f"p1_hT{i}", [128, KT, W], BF16, stack=S) for i in range(2)]
                dl = sb("p1_dl", [128, 16], stack=S)
                P.dma("sp", dl[:], rt_dl.partition_broadcast(128), writes=[dl])
                lg = sb("p1_lg", [128, 6, 16], stack=S)
                act(lg[:, 0, :], dl[:], AF.Exp, [dl], [lg], scale=-1.0)
                act(lg[:, 0, :], lg[:, 0, :], AF.Ln, [lg], [lg], bias=1.0)
                ts("dve", lg[:, 1, :], lg[:, 0, :], 1.0, None, ALU.mult, None, [lg], [lg])
                ts("dve", lg[:, 0, :], lg[:, 1, :], -1.0, None, ALU.mult, None, [lg], [lg])
                ts("dve", lg[:, 2, :], lg[:, 0, :], float(C1), None, ALU.mult, None, [lg], [lg])
                ts("dve", lg[:, 3, :], lg[:, 1, :], math.log(1.0 / 16), None, ALU.add, None, [lg], [lg])
                ts("dve", lg[:, 4, :], lg[:, 1, :], float(C1), math.log(1.0 / 16), ALU.mult, ALU.add, [lg], [lg])
                pos = sb("p1_pos", [128, W], stack=S)
                ones_ = sb("p1_ones", [128, 128], BF16, stack=S)
                memset("pool", ones_[:], 1.0, [ones_])
                pp_ = ps()
                mm(pp_[:, 0:128], ones_[:], triUsb[:], True, True, [ones_, triUsb], [pp_])
                cp("dve", pos[:, 0:128], pp_[:, 0:128], [pp_], [pos])
                cp("dve", pos[:, 128:256], pp_[:, 0:128], [pp_], [pos])
                DQ = [[sb(f"DQ{d}{h}", [128, W], stack=S) for h in range(8)] for d in range(2)]
                DK = [[sb(f"DK{d}{h}", [128, W], stack=S) for h in range(8)] for d in range(2)]
                for h in range(8):
                    c0 = h; c1 = 8 + h
                    act(DQ[0][h][:], pos[:], AF.Exp, [pos, lg], [DQ[0][h]], scale=lg[:, 0, c0:c0 + 1], bias=lg[:, 0, c0:c0 + 1])
                    act(DK[0][h][:], pos[:], AF.Exp, [pos, lg], [DK[0][h]], scale=lg[:, 1, c0:c0 + 1], bias=lg[:, 3, c0:c0 + 1])
                    act(DQ[1][h][:], pos[:], AF.Exp, [pos, lg], [DQ[1][h]], scale=lg[:, 1, c1:c1 + 1], bias=lg[:, 2, c1:c1 + 1])
                    act(DK[1][h][:], pos[:], AF.Exp, [pos, lg], [DK[1][h]], scale=lg[:, 0, c1:c1 + 1], bias=lg[:, 4, c1:c1 + 1])
                cs_t = sb("p1_cos", [128, W], stack=S); sn_t = sb("p1_sin", [128, W], stack=S)
                wrk = [sb(f"p1_wrk{i}", [128, KT, 128], BF16, stack=S) for i in range(4)]
                wpc = [sb(f"p1_wpc{i}", [128, KT, 512], BF16, stack=S) for i in range(2)]
                xx = [[sb(f"p1_x{i}{j}", [128, W], stack=S) for j in range(2)] for i in range(2)]
                tmp = [sb(f"p1_t{i}", [128, W], stack=S) for i in range(4)]
                yy = [sb(f"p1_y{i}", [128, W], stack=S) for i in range(2)]
                ob = [sb(f"p1_ob{i}", [128, W], BF16, stack=S) for i in range(4)]
                vst = [sb(f"p1_vst{i}", [128, 512], BF16, stack=S) for i in range(2)]
                gst = [sb(f"p1_gst{i}", [128, 512], stack=S) for i in range(2)]
                nwin = T // W
                cnt_ = [0]
                def ld_h(w_):
                    P.dma("sp", hT[w_ % 2][:], HT1[:, :, w_ * W:(w_ + 1) * W].rearrange("k p t -> p k t"), writes=[hT[w_ % 2]])

                def ld_qk(pi):
                    for half in range(2):
                        P.dma("sp", wrk[(pi * 2 + half) % 4][:], Wb_qk[pi * 2 + half], writes=[wrk[(pi * 2 + half) % 4]])

                def ld_vg1(idx):
                    P.dma("sp", wpc[idx % 2][:], Wb_vg[idx], writes=[wpc[idx % 2]])

                ld_h(0)
                for w in range(nwin):
                    t0 = w * W
                    isctx = (w == 0)
                    h_ = hT[w % 2]
                    if not isctx:
                        P.dma("sp", cs_t[:], ropec[:, t0 - NCTX:t0 - NCTX + W], writes=[cs_t])
                        P.dma("sp", sn_t[:], ropes[:, t0 - NCTX:t0 - NCTX + W], writes=[sn_t])
                    ld_qk(0)
                    for qk in range(2):
                        dsts = QT if qk == 0 else KTT
                        tabs = DQ if qk == 0 else DK
                        for h in range(8):
                            i2 = cnt_[0] % 2; cnt_[0] += 1
                            pi_ = qk * 8 + h
                            if pi_ + 1 < 16:
                                ld_qk(pi_ + 1)
                            else:
                                ld_vg1(0)
                                if w + 1 < nwin:
                                    ld_h(w + 1)
                            for half in range(2):
                                et = h * 2 + half
                                wr = wrk[(pi_ * 2 + half) % 4]
                                pp = ps()
                                for kt in range(KT):
                                    mm(pp[:, 0:W], wr[:, kt, :], h_[:, kt, :], kt == 0, kt == KT - 1, [wr, h_], [pp])
                                cp("act", xx[i2][half][:], pp[:, 0:W], [pp], [xx[i2][half]])
                            x1 = xx[i2][0]; x2 = xx[i2][1]
                            if isctx:
                                y1, y2 = x1, x2
                            else:
                                y1, y2 = yy[0], yy[1]
                                tt("dve", tmp[0][:], x1[:], cs_t[:], ALU.mult, [x1, cs_t], [tmp[0]])
                                tt("pool", tmp[1][:], x2[:], sn_t[:], ALU.mult, [x2, sn_t], [tmp[1]])
                                tt("dve", y1[:], tmp[0][:], tmp[1][:], ALU.subtract, [tmp[0], tmp[1]], [y1])
                                tt("pool", tmp[2][:], x1[:], sn_t[:], ALU.mult, [x1, sn_t], [tmp[2]])
                                tt("dve", tmp[3][:], x2[:], cs_t[:], ALU.mult, [x2, cs_t], [tmp[3]])
                                tt("pool", y2[:], tmp[2][:], tmp[3][:], ALU.add, [tmp[2], tmp[3]], [y2])
                            for d in range(2):
                                for half, y_ in ((0, y1), (1, y2)):
                                    o_ = ob[d * 2 + half]
                                    tt("dve" if half == 0 else "pool", o_[:], y_[:], tabs[d][h][:], ALU.mult, [y_, tabs[d][h]], [o_])
                                    P.dma("sp", dsts[d][h * 2 + half, :, t0:t0 + W], o_[:], reads=[o_])
                    for vg in range(2):
                        for cg in range(8):
                            idx_ = vg * 8 + cg
                            wp = wpc[idx_ % 2]
                            if idx_ + 1 < 16:
                                ld_vg1(idx_ + 1)
                            for tt_ in range(2):
                                pp = ps()
                                for kt in range(KT):
                                    mm(pp[:], h_[:, kt, tt_ * 128:(tt_ + 1) * 128], wp[:, kt, :], kt == 0, kt == KT - 1, [h_, wp], [pp])
                                rs = slice(t0 + tt_ * 128, t0 + (tt_ + 1) * 128)
                                if vg == 0:
                                    cp("act", vst[tt_][:], pp[:], [pp], [vst[tt_]])
                                    P.dma("sp", V1[rs, cg * 512:(cg + 1) * 512], vst[tt_][:], reads=[vst[tt_]])
                                else:
                                    act(gst[tt_][:], pp[:], AF.Silu, [pp], [gst[tt_]])
                                    P.dma("sp", SG1[rs, cg * 512:(cg + 1) * 512], gst[tt_][:], reads=[gst[tt_]])
            P.barrier()

        def phase_s1(d):
            with ExitStack() as ph:
                S = ph.enter_context
                dl = sb("s1_dl", [128, 16], stack=S)
                P.dma("sp", dl[:], rt_dl.partition_broadcast(128), writes=[dl])
                gc = sb("s1_gc", [128, 16], stack=S)
                act(gc[:], dl[:], AF.Exp, [dl], [gc], scale=-1.0)
                act(gc[:], gc[:], AF.Ln, [gc], [gc], bias=1.0)
                act(gc[:], gc[:], AF.Exp, [gc], [gc], scale=-float(C1))
                qt = [sb(f"s1_q{i}", [128, KT, C1], BF16, stack=S) for i in range(2)]
                kt_ = [sb(f"s1_k{i}", [128, KT, C1], BF16, stack=S) for i in range(2)]
                vc = [sb(f"s1_v{i}", [128, 2 * D], BF16, stack=S) for i in range(2)]
                R32 = [sb(f"s1_R32_{h}", [128, 2, 512], stack=S) for h in range(8)]
                Rb = [sb(f"s1_Rb_{h}", [128, 2, 512], BF16, stack=S) for h in range(8)]
                for h_ in range(8):
                    memset("pool", R32[h_][:], 0.0, [R32[h_]]); memset("pool", Rb[h_][:], 0.0, [Rb[h_]])
                Sb = [sb(f"s1_S{i}", [128, 128], BF16, stack=S) for i in range(8)]
                ktok = [sb(f"s1_kt{i}", [128, 256], BF16, stack=S) for i in range(8)]
                ost = [sb(f"s1_o{i}", [128, 512], stack=S) for i in range(8)]
                msk = sb("s1_msk", [128, 128], stack=S)
                cp("dve", msk[:], (triU if d == 0 else triLs)[:], [triU, triLs], [msk])
                nch = T // C1
                order = list(range(nch)) if d == 0 else [1, 0] + list(range(nch - 1, 1, -1))

                def hbody(h, q_, k_, v_, t0):
                    pS = ps()
                    for half in range(2):
                        mm(pS[:, 0:128], k_[:, h * 2 + half, :], q_[:, h * 2 + half, :], half == 0, half == 1, [k_, q_], [pS])
                    tt("dve", Sb[h][:], pS[:, 0:128], msk[:], ALU.mult, [pS, msk], [Sb[h]])
                    pT = ps()
                    for half in range(2):
                        mm(pT[:, half * 128:(half + 1) * 128], k_[:, h * 2 + half, :], identb[:], True, True, [k_, identb], [pT])
                    cp("act", ktok[h][:], pT[:, 0:256], [pT], [ktok[h]])
                    yield
                    pO = ps()
                    mm(pO[:], Sb[h][:], v_[:, h * 512:(h + 1) * 512], True, False, [Sb[h], v_], [pO])
                    for half in range(2):
                        mm(pO[:], q_[:, h * 2 + half, :], Rb[h][:, half, :], False, half == 1, [q_, Rb[h]], [pO])
                    cp("act", ost[h][:], pO[:], [pO], [ost[h]])
                    P.dma("sp", O1[d][t0:t0 + C1, h * 512:(h + 1) * 512], ost[h][:], reads=[ost[h]])
                    yield
                    for half in range(2):
                        pR = ps()
                        mm(pR[:], ktok[h][:, half * 128:(half + 1) * 128], v_[:, h * 512:(h + 1) * 512], True, True, [ktok[h], v_], [pR])
                        tt("dve", R32[h][:, half, :], R32[h][:, half, :], pR[:], ALU.add, [R32[h], pR], [R32[h]])
                        ts("pool", R32[h][:, half, :], R32[h][:, half, :], gc[:, d * 8 + h:d * 8 + h + 1], None, ALU.mult, None, [R32[h], gc], [R32[h]])
                        cp("act", Rb[h][:, half, :], R32[h][:, half, :], [R32[h]], [Rb[h]])
                    yield

                def issue_loads1(ci):
                    t0 = order[ci] * C1
                    b2 = ci % 2
                    P.dma("sp", qt[b2][:], QT[d][:, :, t0:t0 + C1].rearrange("k p t -> p k t"), writes=[qt[b2]])
                    P.dma("sp", kt_[b2][:], KTT[d][:, :, t0:t0 + C1].rearrange("k p t -> p k t"), writes=[kt_[b2]])
                    P.dma("sp", vc[b2][:], V1[t0:t0 + C1, :], writes=[vc[b2]])

                issue_loads1(0)
                for ci, c in enumerate(order):
                    t0 = c * C1
                    b2 = ci % 2
                    q_ = qt[b2]; k_ = kt_[b2]; v_ = vc[b2]
                    if ci + 1 < len(order):
                        issue_loads1(ci + 1)
                    gens = [hbody(h, q_, k_, v_, t0) for h in range(8)]
                    while gens:
                        for gen in list(gens):
                            try:
                                next(gen)
                            except StopIteration:
                                gens.remove(gen)
            P.barrier()

        def phase_o1():
            with ExitStack() as ph:
                S = ph.enter_context
                GNG = sb("o1_gng", [128, 2 * D], stack=S); TG = sb("o1_TG", [128, D], stack=S); FG = sb("o1_FG", [128, D], stack=S)
                P.dma("sp", GNG[:], rt_gn_g.partition_broadcast(128), writes=[GNG])
                P.dma("sp", TG[:], ADA[1, 0, 2, :].partition_broadcast(128), writes=[TG])
                P.dma("sp", FG[:], final_g.partition_broadcast(128), writes=[FG])
                of = [sb(f"o1_of{i}", [128, 2 * D], stack=S) for i in range(1)] * 2
                ob = sb("o1_ob", [128, 2 * D], stack=S)
                sg = sb("o1_sg", [128, 2 * D], stack=S)
                x_b = [sb(f"o1_x{i}", [128, D], stack=S) for i in range(2)]
                st = [sb(f"o1_st{i}", [128, 4, 8], stack=S) for i in range(2)]
                yT = sb("o1_yT", [128, 32, 256], BF16, stack=S)
                wp = [sb(f"o1_wp{i}", [128, 32, 256], BF16, stack=S) for i in range(2)]
                junk = sb("o1_junk", [128, D], stack=S)
                ybf1 = sb("o1_ybf", [128, 2 * D], BF16, stack=S)
                wi = [0]
                for pi in range((NTILE - 2) // 2):
                    tiles = [2 + 2 * pi, 3 + 2 * pi]
                    for j, i in enumerate(tiles):
                        rs = slice(i * 128, (i + 1) * 128)
                        o = of[j]; x_ = x_b[j]; s_ = st[j]
                        P.dma("sp", o[:], O1[0][rs, :], writes=[o])
                        P.dma("sp", ob[:], O1[1][rs, :], writes=[ob])
                        P.dma("sp", sg[:], SG1[rs, :], writes=[sg])
                        P.dma("sp", x_[:], X1[rs, :], writes=[x_])
                        tt("dve", o[:], o[:], ob[:], ALU.add, [o, ob], [o])
                        tt("pool", ob[:], o[:], o[:], ALU.mult, [o], [ob])
                        red("dve", s_[:, 0, :], ob[:].rearrange("p (h v) -> p h v", v=512), [ob], [s_])
                        ts("dve", s_[:, 1, :], s_[:, 0, :], 1.0 / 512, EPS, ALU.mult, ALU.add, [s_], [s_])
                        act(s_[:, 2, :], s_[:, 1, :], AF.Sqrt, [s_], [s_])
                        recip(s_[:, 3, :], s_[:, 2, :], [s_], [s_])
                        o3 = o[:].rearrange("p (h v) -> p h v", v=512)
                        tt("dve", o3, o3, s_[:, 3, :].unsqueeze(2).to_broadcast([128, 8, 512]), ALU.mult, [o, s_], [o])
                        tt("pool", o[:], o[:], GNG[:], ALU.mult, [o, GNG], [o])
                        tt("dve", o[:], o[:], sg[:], ALU.mult, [o, sg], [o])
                        cp("act", ybf1[:], o[:], [o], [ybf1])
                        for g in range(8):
                            pp = ps()
                            for q in range(4):
                                kt = g * 4 + q
                                mm(pp[:, q * 128:(q + 1) * 128], ybf1[:, kt * 128:(kt + 1) * 128], identb[:], True, True, [ybf1, identb], [pp])
                            cp("act" if g % 2 else "dve", yT[:, g * 4:(g + 1) * 4, j * 128:(j + 1) * 128], pp[:].rearrange("p (q t) -> p q t", q=4), [pp], [yT])
                    for cg in range(8):
                        w_ = wp[wi[0] % 2]; wi[0] += 1
                        P.dma("sp", w_[:], Wb_o1[cg], writes=[w_])
                        cs_ = slice(cg * 256, (cg + 1) * 256)
                        for j in range(2):
                            x_ = x_b[j]
                            pp = ps()
                            for kt in range(32):
                                mm(pp[:, 0:256], yT[:, kt, j * 128:(j + 1) * 128], w_[:, kt, :], kt == 0, kt == 31, [yT, w_], [pp])
                            tt("dve", junk[:, cs_], pp[:, 0:256], TG[:, cs_], ALU.mult, [pp, TG], [junk])
                            tt("pool", x_[:, cs_], x_[:, cs_], junk[:, cs_], ALU.add, [x_, junk], [x_])
                    for j, i in enumerate(tiles):
                        x_ = x_b[j]; s_ = st[j]
                        act(junk[:], x_[:], AF.Square, [x_], [junk, s_], accum=s_[:, 0, 0:1])
                        ts("dve", s_[:, 0, 1:2], s_[:, 0, 0:1], 1.0 / D, EPS, ALU.mult, ALU.add, [s_], [s_])
                        act(s_[:, 0, 2:3], s_[:, 0, 1:2], AF.Sqrt, [s_], [s_])
                        recip(s_[:, 0, 3:4], s_[:, 0, 2:3], [s_], [s_])
                        stt("dve", x_[:], x_[:], s_[:, 0, 3:4], FG[:], ALU.mult, ALU.mult, [x_, s_, FG], [x_])
                        P.dma("sp", yout[(i - 2) * 128:(i - 1) * 128, :], x_[:], reads=[x_], is_output=True)

        P.barrier()
        stages = [
            ("ada0", lambda: adaln(0)),
            ("h0", lambda: phase_h(0, xin, HT0, F32)),
            ("p0", phase_p0),
            ("s0f", lambda: phase_s0(0)),
            ("s0b", lambda: phase_s0(1)),
            ("o0", lambda: (issue_cast1(100), phase_o0())),
            ("ada1", lambda: adaln(1)),
            ("h1", lambda: phase_h(1, X1, HT1, BF16)),
            ("p1", phase_p1),
            ("s1f", lambda: phase_s1(0)),
            ("s1b", lambda: phase_s1(1)),
            ("o1", phase_o1),
        ]
        for name, fn in stages:
            fn()
            if STOP_AFTER == name:
                break
        P.emit(E)
    return nc


def rope_tables():
    t = np.arange(NLAT)
    row = (t // 64).astype(np.float32)
    col = (t % 64).astype(np.float32)
    nf = 64
    inv = (10000.0 ** (-np.arange(nf, dtype=np.float32) / nf)).astype(np.float32)
    ang = np.concatenate([row[:, None] * inv, col[:, None] * inv], axis=-1).astype(np.float32)
    return np.ascontiguousarray(np.cos(ang).T.astype(np.float32)), np.ascontiguousarray(np.sin(ang).T.astype(np.float32))


def make_in_maps(x, c, ctx, c_ctx, ada_w, ada_b, norm_g, rk_mix, rk_w_in, rk_w0, rk_w1, rk_w2, rk_a0, rk_a1, rk_a2,
                 rk_k_k, rk_k_a, rk_r_k, rk_ln_g, rk_ln_b, rk_w_out, rt_w_in, rt_decay_logit, rt_gn_g, rt_w_out, final_g):
    f = lambda a: np.ascontiguousarray(np.asarray(a, dtype=np.float32))
    rc, rs_ = rope_tables()
    shared = dict(ada_w=f(ada_w), ada_b=f(ada_b), norm_g=f(norm_g), rk_mix=f(rk_mix)[0], rk_w_in=f(rk_w_in)[0],
                  rk_w0=f(rk_w0)[0], rk_w1=f(rk_w1)[0], rk_w2=f(rk_w2)[0], rk_a0=f(rk_a0)[0], rk_a1=f(rk_a1)[0],
                  rk_a2=f(rk_a2)[0], rk_k_k=f(rk_k_k)[0], rk_k_a=f(rk_k_a)[0], rk_r_k=f(rk_r_k)[0].reshape(-1),
                  rk_ln_g=f(rk_ln_g)[0], rk_ln_b=f(rk_ln_b)[0], rk_w_out=f(rk_w_out)[0], rt_w_in=f(rt_w_in)[0],
                  rt_dl=f(rt_decay_logit)[0].reshape(-1), rt_gn_g=f(rt_gn_g)[0], rt_w_out=f(rt_w_out)[0],
                  final_g=f(final_g), ropec=rc, ropes=rs_)
    maps = []
    for core in range(8):
        b = core % 4
        m = dict(shared)
        m["xin"] = np.ascontiguousarray(np.concatenate([f(ctx)[b], f(x)[b]], axis=0))
        m["cvec"] = np.ascontiguousarray(np.stack([f(c)[b], f(c_ctx)], axis=0))
        maps.append(m)
    return maps


def kernel(**inputs):
    maps = make_in_maps(**inputs)[:NCORES]
    nc = build()
    res = run_bass_kernel_spmd(nc, maps, core_ids=list(range(NCORES)))
    out = np.stack([np.asarray(res.results[b]["yout"], dtype=np.float32) for b in range(4)], axis=0)
    return out
```

```python
import math
from contextlib import ExitStack
import numpy as np
import concourse.bass as bass
import concourse.mybir as mybir
from concourse.bass_utils import run_bass_kernel_spmd

F32 = mybir.dt.float32
BF16 = mybir.dt.bfloat16
ALU = mybir.AluOpType
AF = mybir.ActivationFunctionType
AX = mybir.AxisListType

ENGS = ["pe", "act", "dve", "pool", "sp"]
SIG_LIM = 30000


class Buf:
    __slots__ = ("name", "t", "w", "r", "excl")

    def __init__(self, name, t, excl=False):
        self.name = name
        self.t = t
        self.w = {}
        self.r = {}
        self.excl = excl

    def __getitem__(self, idx):
        return self.t[idx]


class Op:
    __slots__ = ("eng", "fn", "deps", "signal", "sig_no", "dma", "sem_i", "val")

    def __init__(self, eng, fn, dma):
        self.eng = eng
        self.fn = fn
        self.deps = []
        self.signal = False
        self.sig_no = 0
        self.dma = dma
        self.sem_i = 0
        self.val = 0


class Prog:
    def __init__(self, nc):
        self.nc = nc
        self.ops = {e: [] for e in ENGS}
        self.n_dma = {e: 0 for e in ENGS}
        self.n_dma_sems = {"sp": 40, "pool": 4, "act": 8, "pe": 1, "dve": 1}
        self.out_dmas = []
        self.dma_since = {e: [] for e in ENGS}
        self.bar_bufs = None

    def add(self, eng, fn, reads=(), writes=(), dma=False, is_output=False):
        op = Op(eng, fn, dma)
        key = op if dma else eng
        deps = {}
        for b in reads:
            for k, w in b.w.items():
                deps[id(w)] = w
            if b.excl:
                for k, r in b.r.items():
                    if k != eng:
                        deps[id(r)] = r
        for b in writes:
            for k, w in b.w.items():
                if dma or k != eng:
                    deps[id(w)] = w
            for k, r in b.r.items():
                if dma or k != eng:
                    deps[id(r)] = r
        for b in reads:
            b.r[key] = op
        for b in writes:
            b.w = {key: op}
            b.r = {}
        op.deps = list(deps.values())
        for d in op.deps:
            d.signal = True
        if dma:
            ns = self.n_dma_sems[eng]
            i = self.n_dma[eng]
            self.n_dma[eng] += 1
            op.sem_i = i % ns
            op.val = 16 * (i // ns + 1)
            op.signal = True
            self.dma_since[eng].append(op)
            if len(self.dma_since[eng]) > ns:
                self.dma_since[eng] = self.dma_since[eng][-ns:]
            if is_output:
                self.out_dmas.append(op)
        self.ops[eng].append(op)
        return op

    def dma(self, q, out_ap, in_ap, reads=(), writes=(), is_output=False, slow=False):
        if slow:
            return self.add(q, lambda e: e.dma_start(out=out_ap, in_=in_ap, allow_slow_non_contiguous=True), reads, writes,
                            dma=True, is_output=is_output)
        return self.add(q, lambda e: e.dma_start(out=out_ap, in_=in_ap), reads, writes, dma=True,
                        is_output=is_output)

    def barrier(self):
        bb = self.bar_bufs
        firsts = []
        for e in ENGS:
            if e == "sp":
                op = self.add("sp", lambda q: q.dma_start(out=bb["sp"][0:1, 0:4], in_=bb["spsrc"][0:1, 0:4]),
                              reads=[bb["spsrc"]], writes=[bb["sp"]], dma=True)
                for q in ENGS:
                    for d in self.dma_since[q]:
                        if d is not op:
                            op.deps.append(d)
                            d.signal = True
            elif e == "pe":
                op = self.add("pe", lambda t: t.matmul(bb["pe"][0:1, 0:1], lhsT=bb["pesrc"][0:1, 0:1],
                                                        rhs=bb["pesrc"][0:1, 0:1], start=True, stop=True),
                              reads=[bb["pesrc"]], writes=[bb["pe"]])
            else:
                b = bb[e]
                if e == "act":
                    op = self.add(e, (lambda b: (lambda g: g.memzero(b[0:1, 0:4])))(b), writes=[b])
                else:
                    op = self.add(e, (lambda b: (lambda g: g.memset(b[0:1, 0:4], 0.0)))(b), writes=[b])
            firsts.append(op)
        for e in ENGS:
            if e == "sp":
                op = self.add("sp", lambda q: q.dma_start(out=bb["sp2"][0:1, 0:4], in_=bb["spsrc"][0:1, 0:4]),
                              reads=[bb["spsrc"]], writes=[bb["sp2"]], dma=True)
            elif e == "pe":
                op = self.add("pe", lambda t: t.matmul(bb["pe"][0:1, 1:2], lhsT=bb["pesrc"][0:1, 0:1],
                                                        rhs=bb["pesrc"][0:1, 0:1], start=True, stop=True),
                              reads=[bb["pesrc"]], writes=[bb["pe"]])
            else:
                b = bb[e + "2"]
                if e == "act":
                    op = self.add(e, (lambda b: (lambda g: g.memzero(b[0:1, 0:4])))(b), writes=[b])
                else:
                    op = self.add(e, (lambda b: (lambda g: g.memset(b[0:1, 0:4], 0.0)))(b), writes=[b])
            for f in firsts:
                if f.eng != e:
                    op.deps.append(f)
                    f.signal = True
        for q in ENGS:
            self.dma_since[q] = []

    def emit(self, E):
        nc = self.nc
        nsig = {}
        for e in ENGS:
            n = 0
            for op in self.ops[e]:
                if op.signal and not op.dma:
                    n += 1
                    op.sig_no = n
            nsig[e] = n
        esem = {}
        for e in ENGS:
            k = max(1, (nsig[e] + SIG_LIM - 1) // SIG_LIM)
            esem[e] = [E(nc.semaphore(f"s_{e}_{j}")) for j in range(k)]
        dsem = {}
        for e in ENGS:
            if self.n_dma[e] > 0:
                dsem[e] = [E(nc.semaphore(f"d_{e}_{j}")) for j in range(self.n_dma_sems[e])]
        block = E(nc.Block())
        engobj = {"pe": "tensor", "act": "scalar", "dve": "vector", "pool": "gpsimd", "sp": "sync"}

        def dep_wait(op):
            if op.dma:
                return dsem[op.eng][op.sem_i], op.val, ("d", op.eng, op.sem_i)
            ep = (op.sig_no - 1) // SIG_LIM
            return esem[op.eng][ep], op.sig_no - ep * SIG_LIM, ("e", op.eng, ep)

        out_dmas = self.out_dmas

        def make_body(e):
            def body(eng):
                waited = {}
                for op in self.ops[e]:
                    for d in op.deps:
                        sem, val, key = dep_wait(d)
                        if waited.get(key, 0) >= val:
                            continue
                        waited[key] = val
                        eng.wait_ge(sem, val)
                    if op.dma and op.val > 16:
                        key = ("d", e, op.sem_i)
                        if waited.get(key, 0) < op.val - 16:
                            waited[key] = op.val - 16
                            eng.wait_ge(dsem[e][op.sem_i], op.val - 16)
                    inst = op.fn(eng)
                    if op.dma:
                        inst.then_inc(dsem[e][op.sem_i], 16)
                    elif op.signal:
                        ep = (op.sig_no - 1) // SIG_LIM
                        inst.then_inc(esem[e][ep], 1)
                if e == "sp":
                    for d in out_dmas:
                        sem, val, key = dep_wait(d)
                        if waited.get(key, 0) >= val:
                            continue
                        waited[key] = val
                        eng.wait_ge(sem, val)
            return body

        for e in ENGS:
            if self.ops[e] or e == "sp":
                getattr(block, engobj[e])(make_body(e))


D = 2048
KT = 16
NCTX = 256
NLAT = 4096
T = NCTX + NLAT
NTILE = T // 128
EPS = 1e-6
GN_EPS = 64e-5
LORA = 96
C0 = 64
C1 = 128
EXPM05 = math.exp(-0.5)

DEBUG_OUT = set()
NCORES = 4
import os as _os
STOP_AFTER = _os.environ.get('KSTOP') or None


def build():
    nc = bass.Bass("TRN2", target_bir_lowering=False)

    def din(name, shape):
        return nc.dram_tensor(name, list(shape), F32, kind="ExternalInput").ap()

    def scratch(name, shape, dt=F32):
        kind = "ExternalOutput" if name in DEBUG_OUT else "Internal"
        return nc.dram_tensor(name, list(shape), dt, kind=kind).ap()

    xin = din("xin", [T, D])
    cvec = din("cvec", [2, D])
    ada_w = din("ada_w", [2, D, 3 * D])
    ada_b = din("ada_b", [2, 3 * D])
    norm_g = din("norm_g", [2, D])
    rk_mix = din("rk_mix", [6, D])
    rk_w_in = din("rk_w_in", [4, D, D])
    rk_w0 = din("rk_w0", [2, D])
    rk_w1 = din("rk_w1", [2, D, LORA])
    rk_w2 = din("rk_w2", [2, LORA, D])
    rk_a0 = din("rk_a0", [2, D])
    rk_a1 = din("rk_a1", [2, D, LORA])
    rk_a2 = din("rk_a2", [2, LORA, D])
    rk_k_k = din("rk_k_k", [D])
    rk_k_a = din("rk_k_a", [D])
    rk_r_k = din("rk_r_k", [D])
    rk_ln_g = din("rk_ln_g", [D])
    rk_ln_b = din("rk_ln_b", [D])
    rk_w_out = din("rk_w_out", [D, D])
    rt_w_in = din("rt_w_in", [D, 6 * D])
    rt_dl = din("rt_dl", [16])
    rt_gn_g = din("rt_gn_g", [2 * D])
    rt_w_out = din("rt_w_out", [2 * D, D])
    final_g = din("final_g", [D])
    ropec = din("ropec", [128, NLAT])
    ropes = din("ropes", [128, NLAT])
    yout = nc.dram_tensor("yout", [NLAT, D], F32, kind="ExternalOutput").ap()

    Wb_r = scratch("Wb_r", [16, 128, KT, 128], BF16)
    Wb_k = scratch("Wb_k", [16, 128, KT, 128], BF16)
    Wb_v = scratch("Wb_v", [8, 128, KT, 256], BF16)
    Wb_g = scratch("Wb_g", [8, 128, KT, 256], BF16)
    Wb_out0 = scratch("Wb_out0", [D, D], BF16)
    Wb_qk = scratch("Wb_qk", [32, 128, KT, 128], BF16)
    Wb_vg = scratch("Wb_vg", [16, 128, KT, 512], BF16)
    Wb_o1 = scratch("Wb_o1", [8, 128, 32, 256], BF16)
    ADA = scratch("ADA", [2, 2, 3, D])
    HT0 = scratch("HT0", [KT, 128, T])
    HT1 = scratch("HT1", [KT, 128, T], BF16)
    RT = scratch("RT", [KT, 128, T])
    KKT = scratch("KKT", [KT, 128, T])
    KDT = [scratch(f"KDT{d}", [KT, 128, T]) for d in range(2)]
    BT = [scratch(f"BT{d}", [KT, 128, T]) for d in range(2)]
    LW = [scratch(f"LW{d}", [T, D]) for d in range(2)]
    V0 = scratch("V0", [T, D])
    SG0 = scratch("SG0", [T, D])
    BON = scratch("BON", [T, 32])
    O0 = [scratch(f"O0_{d}", [T, D]) for d in range(2)]
    X1 = scratch("X1", [T, D])
    QT = [scratch(f"QT{d}", [KT, 128, T], BF16) for d in range(2)]
    KTT = [scratch(f"KTT{d}", [KT, 128, T], BF16) for d in range(2)]
    V1 = scratch("V1", [T, 2 * D], BF16)
    SG1 = scratch("SG1", [T, 2 * D])
    O1 = [scratch(f"O1_{d}", [T, 2 * D]) for d in range(2)]

    with ExitStack() as es:
        E = es.enter_context
        P = Prog(nc)
        cnt = [0]

        def sb(name, shape, dt=F32, stack=None):
            cnt[0] += 1
            return Buf(name, (stack or E)(nc.sbuf_tensor(f"{name}_{cnt[0]}", list(shape), dt)))

        P.bar_bufs = {k: sb("bar_" + k, [1, 8]) for k in ["sp", "sp2", "spsrc", "act", "act2", "dve", "dve2", "pool", "pool2"]}
        P.bar_bufs["pesrc"] = sb("bar_pesrc", [1, 8])
        PS = [Buf(f"ps{i}", E(nc.psum_tensor(f"ps{i}", [128, 512], F32)), excl=True) for i in range(7)]
        P.bar_bufs["pe"] = Buf("ps_bar", E(nc.psum_tensor("ps_bar", [128, 512], F32)))
        P.add("pool", lambda e: e.memset(P.bar_bufs["pesrc"][:], 0.0), writes=[P.bar_bufs["pesrc"]])
        P.add("pool", lambda e: e.memset(P.bar_bufs["spsrc"][:], 0.0), writes=[P.bar_bufs["spsrc"]])
        psi = [0]

        psn = [7]

        def ps():
            psi[0] = (psi[0] + 1) % psn[0]
            return PS[psi[0]]

        def tt(eng, out, in0, in1, op, R, W):
            P.add(eng, lambda e: e.tensor_tensor(out=out, in0=in0, in1=in1, op=op), reads=R, writes=W)

        def ts(eng, out, in0, s1, s2, op0, op1, R, W):
            if s2 is None:
                P.add(eng, lambda e: e.tensor_scalar(out=out, in0=in0, scalar1=s1, scalar2=None, op0=op0), reads=R, writes=W)
            else:
                P.add(eng, lambda e: e.tensor_scalar(out=out, in0=in0, scalar1=s1, scalar2=s2, op0=op0, op1=op1), reads=R, writes=W)

        def stt(eng, out, in0, s, in1, op0, op1, R, W):
            P.add(eng, lambda e: e.scalar_tensor_tensor(out=out, in0=in0, scalar=s, in1=in1, op0=op0, op1=op1), reads=R, writes=W)

        def act(out, in_, func, R, W, bias=None, scale=None, accum=None):
            kw = {}
            if bias is not None:
                kw["bias"] = bias
            if scale is not None:
                kw["scale"] = scale
            if accum is not None:
                kw["accum_out"] = accum
            P.add("act", lambda e: e.activation(out=out, in_=in_, func=func, **kw), reads=R, writes=W)

        def cp(eng, out, in_, R, W):
            if eng == "act":
                P.add("act", lambda e: e.copy(out=out, in_=in_), reads=R, writes=W)
            else:
                P.add(eng, lambda e: e.tensor_copy(out=out, in_=in_), reads=R, writes=W)

        def mm(out, lhsT, rhs, start, stop, R, W):
            P.add("pe", lambda e: e.matmul(out, lhsT=lhsT, rhs=rhs, start=start, stop=stop), reads=R, writes=W)

        def tr(out, in_, ident, R, W):
            P.add("pe", lambda e: e.transpose(out, in_, ident), reads=R, writes=W)

        def memset(eng, ap, val, W):
            P.add(eng, lambda e: e.memset(ap, val), writes=W)

        def red(eng, out, in_, R, W):
            P.add(eng, lambda e: e.tensor_reduce(out=out, in_=in_, axis=AX.X, op=ALU.add), reads=R, writes=W)

        def recip(out, in_, R, W):
            P.add("dve", lambda e: e.reciprocal(out=out, in_=in_), reads=R, writes=W)

        def ftv(ap1d):
            return ap1d.rearrange("(k p) -> p k", p=128)

        ident = sb("ident", [128, 128])
        identb = sb("identb", [128, 128], BF16)
        P.add("pool", lambda e: e.memset(ident[:], 1.0), writes=[ident])
        P.add("pool", lambda e: e.affine_select(out=ident[:], in_=ident[:], pattern=[[-1, 128]], compare_op=ALU.is_equal,
                                                fill=0.0, base=0, channel_multiplier=1), reads=[ident], writes=[ident])
        cp("dve", identb[:], ident[:], [ident], [identb])
        triU = sb("triU", [128, 128]); triUs = sb("triUs", [128, 128]); triL = sb("triL", [128, 128]); triLs = sb("triLs", [128, 128])

        def mk_tri(tb, cmp_, sg):
            P.add("pool", lambda e: e.memset(tb[:], 1.0), writes=[tb])
            P.add("pool", lambda e: e.affine_select(out=tb[:], in_=tb[:], pattern=[[sg, 128]], compare_op=cmp_,
                                                    fill=0.0, base=0, channel_multiplier=-sg), reads=[tb], writes=[tb])
        mk_tri(triU, ALU.is_ge, 1)
        mk_tri(triUs, ALU.is_gt, 1)
        mk_tri(triL, ALU.is_ge, -1)
        mk_tri(triLs, ALU.is_gt, -1)
        blk = sb("blk", [128, 128])
        P.add("pool", lambda e: e.memset(blk[:], 0.0), writes=[blk])
        P.add("pool", lambda e: e.memset(blk[0:64, 0:64], 1.0), writes=[blk])
        P.add("pool", lambda e: e.memset(blk[64:128, 64:128], 1.0), writes=[blk])
        blkb = sb("blkb", [128, 128], BF16)
        cp("dve", blkb[:], blk[:], [blk], [blkb])
        triUsb = sb("triUsb", [128, 128], BF16)
        cp("dve", triUsb[:], triUs[:], [triUs], [triUsb])
        tribI = []; tribS = []
        for d_, (si, ss) in enumerate(((triU, triUs), (triL, triLs))):
            bi = sb(f"tribI{d_}", [64, 64], BF16); bs = sb(f"tribS{d_}", [64, 64], BF16)
            cp("dve", bi[:], si[0:64, 0:64], [si], [bi]); cp("dve", bs[:], ss[0:64, 0:64], [ss], [bs])
            tribI.append(bi); tribS.append(bs)
        ind2 = sb("ind2", [128, 2])
        P.add("pool", lambda e: e.memset(ind2[:], 0.0), writes=[ind2])
        P.add("pool", lambda e: e.memset(ind2[0:64, 0:1], 1.0), writes=[ind2])
        P.add("pool", lambda e: e.memset(ind2[64:128, 1:2], 1.0), writes=[ind2])
        mS4 = [sb(f"mS4_{d}", [128, 4, 128]) for d in range(2)]
        mI4 = [sb(f"mI4_{d}", [128, 4, 128]) for d in range(2)]
        for d in range(2):
            srcS = triUs if d == 0 else triLs
            srcI = triU if d == 0 else triL
            for r_ in range(4):
                tt("pool", mS4[d][:, r_, :], srcS[:], blk[:], ALU.mult, [srcS, blk], [mS4[d]])
                tt("pool", mI4[d][:, r_, :], srcI[:], blk[:], ALU.mult, [srcI, blk], [mI4[d]])

        def pv(w2d, c0, e):
            return w2d[:, c0:c0 + e].rearrange("(k p) e -> p k e", p=128)
        for t_ in range(16):
            P.dma("pool", Wb_r[t_], pv(rk_w_in[0], t_ * 128, 128))
            P.dma("pool", Wb_k[t_], pv(rk_w_in[1], t_ * 128, 128))
        for t_ in range(8):
            P.dma("pool", Wb_v[t_], pv(rk_w_in[2], t_ * 256, 256))
            P.dma("pool", Wb_g[t_], pv(rk_w_in[3], t_ * 256, 256))
        for r_ in range(4):
            P.dma("pool", Wb_out0[r_ * 512:(r_ + 1) * 512, :], rk_w_out[r_ * 512:(r_ + 1) * 512, :])
        cast1 = []
        for t_ in range(32):
            cast1.append((Wb_qk[t_], pv(rt_w_in, t_ * 128, 128)))
        for t_ in range(16):
            cast1.append((Wb_vg[t_], pv(rt_w_in, 2 * D + t_ * 512, 512)))
        for t_ in range(8):
            cast1.append((Wb_o1[t_], pv(rt_w_out, t_ * 256, 256)))
        cast1_it = iter(cast1)

        def issue_cast1(n=1):
            for _ in range(n):
                nx = next(cast1_it, None)
                if nx is not None:
                    P.dma("pool", nx[0], nx[1])

        def adaln(layer):
            with ExitStack() as ph:
                cs = sb("cs", [128, KT, 2], stack=ph.enter_context)
                cst = sb("cst", [128, 2, KT], stack=ph.enter_context)
                for r_ in range(2):
                    P.dma("sp", cst[:, r_, :], ftv(cvec[r_, :]), writes=[cst], slow=True)
                act(cst[:], cst[:], AF.Silu, [cst], [cst])
                cp("dve", cs[:].rearrange("p k r -> p r k"), cst[:], [cst], [cs])
                wts = [sb(f"adaw{i}", [128, KT, 512], stack=ph.enter_context) for i in range(2)]
                bia = [sb(f"adab{i}", [2, 512], stack=ph.enter_context) for i in range(2)]
                ng = sb("ng", [2, 512], stack=ph.enter_context)
                res = [sb(f"adar{i}", [2, 512], stack=ph.enter_context) for i in range(2)]
                for cg in range(12):
                    w_ = wts[cg % 2]; b_ = bia[cg % 2]; r_ = res[cg % 2]
                    P.dma("sp", w_[:], ada_w[layer, :, cg * 512:(cg + 1) * 512].rearrange("(k p) e -> p k e", p=128), writes=[w_])
                    P.dma("sp", b_[:], ada_b[layer, cg * 512:(cg + 1) * 512].partition_broadcast(2), writes=[b_])
                    pp = ps()
                    for kt in range(KT):
                        mm(pp[0:2, :], cs[:, kt, :], w_[:, kt, :], kt == 0, kt == KT - 1, [cs, w_], [pp])
                    tt("dve", r_[:], pp[0:2, :], b_[:], ALU.add, [pp, b_], [r_])
                    which = cg // 4
                    if which == 1:
                        c4 = cg % 4
                        P.dma("sp", ng[:], norm_g[layer, c4 * 512:(c4 + 1) * 512].partition_broadcast(2), writes=[ng])
                        stt("dve", r_[:], r_[:], 1.0, ng[:], ALU.add, ALU.mult, [r_, ng], [r_])
                    c4 = cg % 4
                    for row in range(2):
                        P.dma("sp", ADA[layer, row, which, c4 * 512:(c4 + 1) * 512].unsqueeze(0), r_[row:row + 1, :], reads=[r_])
            P.barrier()

        def phase_h(layer, src, HTdst, hdt):
            with ExitStack() as ph:
                S = ph.enter_context
                TA = sb("TA", [128, D], stack=S); TB = sb("TB", [128, D], stack=S)
                xs = [sb(f"hx{i}", [128, D], stack=S) for i in range(2)]
                hs = [sb(f"hh{i}", [128, D], stack=S) for i in range(2)]
                hts = [sb(f"hT{i}", [128, KT, 128], hdt, stack=S) for i in range(2)]
                junk = sb("junk", [128, D], stack=S)
                st = [sb(f"hst{i}", [128, 4], stack=S) for i in range(2)]
                for i in range(NTILE):
                    row = 1 if i < 2 else 0
                    if i == 0 or i == 2:
                        P.dma("sp", TA[:], ADA[layer, row, 1, :].partition_broadcast(128), writes=[TA])
                        P.dma("sp", TB[:], ADA[layer, row, 0, :].partition_broadcast(128), writes=[TB])
                    x_ = xs[i % 2]; h_ = hs[i % 2]; hT = hts[i % 2]; s_ = st[i % 2]
                    if i == 0:
                        P.dma("sp", x_[:], src[0:128, :], writes=[x_])
                    if i + 1 < NTILE:
                        P.dma("sp", xs[(i + 1) % 2][:], src[(i + 1) * 128:(i + 2) * 128, :], writes=[xs[(i + 1) % 2]])
                    act(junk[:], x_[:], AF.Square, [x_], [junk, s_], accum=s_[:, 0:1])
                    ts("dve", s_[:, 1:2], s_[:, 0:1], 1.0 / D, EPS, ALU.mult, ALU.add, [s_], [s_])
                    act(s_[:, 2:3], s_[:, 1:2], AF.Sqrt, [s_], [s_])
                    recip(s_[:, 3:4], s_[:, 2:3], [s_], [s_])
                    stt("dve", h_[:], x_[:], s_[:, 3:4], TA[:], ALU.mult, ALU.mult, [x_, s_, TA], [h_])
                    tt("pool", h_[:], h_[:], TB[:], ALU.add, [h_, TB], [h_])
                    for g in range(4):
                        pp = ps()
                        for q in range(4):
                            kt = g * 4 + q
                            tr(pp[:, q * 128:(q + 1) * 128], h_[:, kt * 128:(kt + 1) * 128], ident[:], [h_, ident], [pp])
                        cp("act" if g % 2 == 0 else "dve", hT[:, g * 4:(g + 1) * 4, :],
                           pp[:].rearrange("p (q t) -> p q t", q=4), [pp], [hT])
                    P.dma("sp", HTdst[:, :, i * 128:(i + 1) * 128].rearrange("k p t -> p k t"), hT[:], reads=[hT])
            P.barrier()

        def phase_p0():
            with ExitStack() as ph:
                S = ph.enter_context
                W = 256
                HW = sb("HW", [128, KT, W + 128], stack=S)
                dT = sb("dT", [128, KT, W], stack=S)
                xm = {j: sb(f"xm{j}", [128, KT, W], BF16, stack=S) for j in ("r", "k", "t0")}
                xm["t1"] = xm["t0"]
                mixT = sb("mixT", [128, 6, KT], stack=S)
                for j in range(6):
                    P.dma("sp", mixT[:, j, :], ftv(rk_mix[j, :]), writes=[mixT], slow=True)
                prm = sb("prm", [128, 8, KT], stack=S)
                for i_, src_ in enumerate([rk_k_k, rk_k_a, rk_k_a, rk_r_k, rk_a0[0, :], rk_a0[1, :]]):
                    P.dma("sp", prm[:, i_, :], ftv(src_), writes=[prm], slow=True)
                ts("dve", prm[:, 2, :], prm[:, 2, :], -1.0, 1.0, ALU.mult, ALU.add, [prm], [prm])
                w1b = [sb(f"w1b{d}", [128, KT, LORA], BF16, stack=S) for d in range(2)]
                a1b = [sb(f"a1b{d}", [128, KT, LORA], BF16, stack=S) for d in range(2)]
                w2b = [sb(f"w2b{d}", [LORA, D], BF16, stack=S) for d in range(2)]
                a2b = [sb(f"a2b{d}", [LORA, D], BF16, stack=S) for d in range(2)]
                W0t = [sb(f"W0t{d}", [128, D], stack=S) for d in range(2)]
                for d in range(2):
                    P.dma("pool", w1b[d][:], rk_w1[d].rearrange("(k p) r -> p k r", p=128), writes=[w1b[d]])
                    P.dma("pool", a1b[d][:], rk_a1[d].rearrange("(k p) r -> p k r", p=128), writes=[a1b[d]])
                    P.dma("pool", w2b[d][:], rk_w2[d], writes=[w2b[d]])
                    P.dma("pool", a2b[d][:], rk_a2[d], writes=[a2b[d]])
                    P.dma("sp", W0t[d][:], rk_w0[d, :].partition_broadcast(128), writes=[W0t[d]])
                thT = [sb(f"thT{d}", [LORA, W], BF16, stack=S) for d in range(2)]
                xa1 = [sb(f"xa1{d}", [LORA, W], BF16, stack=S) for d in range(2)]
                big = [sb(f"big{i}", [128, D], stack=S) for i in range(2)]
                wpc = [sb(f"wpc{i}", [128, KT, 256], BF16, stack=S) for i in range(2)]
                wrk = [sb(f"wrk{i}", [128, KT, 128], BF16, stack=S) for i in range(4)]
                fm = {n: [sb(f"fm_{n}{i}", [128, W], stack=S) for i in range(1)] * 2 for n in
                      ("r", "k", "a0", "a1", "kkr", "sq", "rn", "kk", "t1", "kd0", "kd1", "b0", "b1", "ks", "rk")}
                bon = [sb(f"bon{i}", [128, 32], stack=S) for i in range(2)]
                sqb = sb("sqb", [128, W], BF16, stack=S)
                rkb = sb("rkb", [128, W], BF16, stack=S)
                IND = sb("IND", [128, KT, 32], BF16, stack=S)
                memset("pool", IND[:], 0.0, [IND])
                for et_ in range(KT):
                    memset("pool", IND[0:64, et_, 2 * et_:2 * et_ + 1], 1.0, [IND])
                    memset("pool", IND[64:128, et_, 2 * et_ + 1:2 * et_ + 2], 1.0, [IND])
                psb = [PS[5], PS[6]]
                psn[0] = 5
                nwin = T // W
                import os
                nwin_run = int(os.environ.get('P0_NWIN', nwin))
                p0step = int(os.environ.get('P0_STEP', 9))
                p0sub = int(os.environ.get('P0_SUB', 9))
                big_i = [0]; wpc_i = [0]; wrk_i = [0]
                for w in range(nwin_run):
                    t0 = w * W
                    isctx = (w == 0)
                    if p0step < 1:
                        continue
                    def load_window(w_):
                        t0_ = w_ * W
                        c_ = (w_ == 0)
                        lo = max(t0_ - 64, 0 if c_ else NCTX)
                        hi = min(t0_ + W + 64, NCTX if c_ else T)
                        if w_ in (0, 1, nwin - 1):
                            memset("pool", HW[:], 0.0, [HW])
                        P.dma("sp", HW[:, :, lo - (t0_ - 64):hi - (t0_ - 64)], HT0[:, :, lo:hi].rearrange("k p t -> p k t"), writes=[HW])
                    if w == 0:
                        load_window(0)
                    ctr = HW[:, :, 64:64 + W]
                    if isctx:
                        groups = [(0, 8, -1), (8, 16, 1)]
                    else:
                        groups = [(0, 4, -1), (4, 8, 1), (8, 12, -64), (12, 16, 64)]
                    for gi, (k0, k1, s_) in enumerate(groups):
                        tt("dve" if gi % 2 == 0 else "pool", dT[:, k0:k1, :], HW[:, k0:k1, 64 + s_:64 + s_ + W], HW[:, k0:k1, 64:64 + W],
                           ALU.subtract, [HW], [dT])
                    if not isctx:
                        d4 = dT[:].rearrange("p k (r c) -> p k r c", c=64)
                        h4 = ctr.rearrange("p k (r c) -> p k r c", c=64)
                        ts("dve", d4[:, 0:4, :, 0], h4[:, 0:4, :, 0], -1.0, None, ALU.mult, None, [HW, dT], [dT])
                        ts("dve", d4[:, 4:8, :, 63], h4[:, 4:8, :, 63], -1.0, None, ALU.mult, None, [HW, dT], [dT])

                    def mkxm(j, dst, engs=("dve",)):
                        for kt in range(KT):
                            stt(engs[kt % len(engs)], dst[:, kt, :], dT[:, kt, :], mixT[:, j, kt:kt + 1], ctr[:, kt, :],
                                ALU.mult, ALU.add, [dT, mixT, HW], [dst])

                    if p0step < 2:
                        continue
                    mkxm(4, xm["t0"])
                    for d in range(2):
                        pp = ps()
                        for kt in range(KT):
                            mm(pp[0:LORA, 0:W], w1b[d][:, kt, :], xm["t0"][:, kt, :], kt == 0, kt == KT - 1, [w1b[d], xm["t0"]], [pp])
                        act(thT[d][:], pp[0:LORA, 0:W], AF.Tanh, [pp], [thT[d]])
                    for d in range(2):
                        for tt_ in range(2):
                            bg = big[big_i[0] % 2]; big_i[0] += 1
                            for cg in range(4):
                                pp = ps()
                                mm(pp[:], thT[d][:, tt_ * 128:(tt_ + 1) * 128], w2b[d][:, cg * 512:(cg + 1) * 512], True, True, [thT[d], w2b[d]], [pp])
                                tt("dve", bg[:, cg * 512:(cg + 1) * 512], pp[:], W0t[d][:, cg * 512:(cg + 1) * 512], ALU.add, [pp, W0t[d]], [bg])
                            act(bg[:], bg[:], AF.Sigmoid, [bg], [bg])
                            ts("pool", bg[:], bg[:], -EXPM05, None, ALU.mult, None, [bg], [bg])
                            P.dma("sp", LW[d][t0 + tt_ * 128:t0 + (tt_ + 1) * 128, :], bg[:], reads=[bg])
                    if p0step < 3:
                        continue
                    mkxm(5, xm["t1"])
                    for d in range(2):
                        pp = ps()
                        for kt in range(KT):
                            mm(pp[0:LORA, 0:W], a1b[d][:, kt, :], xm["t1"][:, kt, :], kt == 0, kt == KT - 1, [a1b[d], xm["t1"]], [pp])
                        cp("act", xa1[d][:], pp[0:LORA, 0:W], [pp], [xa1[d]])
                    if p0step < 4:
                        continue
                    def ld_vg(idx):
                        P.dma("sp", wpc[idx % 2][:], (Wb_v if idx < 8 else Wb_g)[idx % 8], writes=[wpc[idx % 2]])
                    ld_vg(0)
                    for j, dst_dram, key in ((2, V0, "t0"), (3, SG0, "t1")):
                        mkxm(j, xm[key])
                        bgs = [big[0], big[1]]
                        for cg in range(8):
                            idx_ = (j - 2) * 8 + cg
                            wp = wpc[idx_ % 2]
                            if idx_ + 1 < 16:
                                ld_vg(idx_ + 1)
                            for tt_ in range(2):
                                pp = ps()
                                for kt in range(KT):
                                    mm(pp[:, 0:256], xm[key][:, kt, tt_ * 128:(tt_ + 1) * 128], wp[:, kt, :], kt == 0, kt == KT - 1, [xm[key], wp], [pp])
                                if j == 2:
                                    cp("act", bgs[tt_][:, cg * 256:(cg + 1) * 256], pp[:, 0:256], [pp], [bgs[tt_]])
                                else:
                                    act(bgs[tt_][:, cg * 256:(cg + 1) * 256], pp[:, 0:256], AF.Silu, [pp], [bgs[tt_]])
                        for tt_ in range(2):
                            P.dma("sp", dst_dram[t0 + tt_ * 128:t0 + (tt_ + 1) * 128, :], bgs[tt_][:], reads=[bgs[tt_]])
                    if p0step < 5:
                        continue
                    mkxm(0, xm["r"])
                    mkxm(1, xm["k"])
                    if w + 1 < nwin_run:
                        load_window(w + 1)
                    def ld_rk(et_):
                        a_ = wrk[(2 * et_) % 4]; b_ = wrk[(2 * et_ + 1) % 4]
                        P.dma("sp", a_[:], Wb_r[et_], writes=[a_])
                        P.dma("sp", b_[:], Wb_k[et_], writes=[b_])
                    ld_rk(0)
                    for et in range(KT):
                        i2 = et % 2
                        wr = wrk[(2 * et) % 4]; wk = wrk[(2 * et + 1) % 4]
                        if et + 1 < KT:
                            ld_rk(et + 1)
                        p1 = ps(); p2 = ps()
                        psr = p1[:, 0:W]; psk = p1[:, W:2 * W]
                        for kt in range(KT):
                            mm(psr, wr[:, kt, :], xm["r"][:, kt, :], kt == 0, kt == KT - 1, [wr, xm["r"]], [p1])
                        for kt in range(KT):
                            mm(psk, wk[:, kt, :], xm["k"][:, kt, :], kt == 0, kt == KT - 1, [wk, xm["k"]], [p1])
                        for d in range(2):
                            mm(p2[:, d * W:(d + 1) * W], a2b[d][:, et * 128:(et + 1) * 128], xa1[d][:], True, True, [a2b[d], xa1[d]], [p2])
                        f = {n: fm[n][i2] for n in fm}
                        cp("act", f["r"][:], psr, [p1], [f["r"]])
                        cp("act", f["k"][:], psk, [p1], [f["k"]])
                        for d in range(2):
                            act(f[f"a{d}"][:], p2[:, d * W:(d + 1) * W], AF.Sigmoid, [p2, prm], [f[f"a{d}"]], bias=prm[:, 4 + d, et:et + 1])
                        if p0sub < 2:
                            continue
                        ts("dve", f["kkr"][:], psk, prm[:, 0, et:et + 1], None, ALU.mult, None, [p1, prm], [f["kkr"]])
                        act(sqb[:], psk, AF.Square, [p1, prm], [sqb], scale=prm[:, 0, et:et + 1])
                        p3 = ps()
                        mm(p3[:, 0:W], blkb[:], sqb[:], True, True, [blkb, sqb], [p3])
                        act(f["rn"][:], p3[:, 0:W], AF.Sqrt, [p3], [f["rn"]])
                        ts("dve", f["rn"][:], f["rn"][:], 1e-12, None, ALU.max, None, [f["rn"]], [f["rn"]])
                        recip(f["rn"][:], f["rn"][:], [f["rn"]], [f["rn"]])
                        tt("dve", f["kk"][:], f["kkr"][:], f["rn"][:], ALU.mult, [f["kkr"], f["rn"]], [f["kk"]])
                        if p0sub < 3:
                            continue
                        P.dma("sp", RT[et, :, t0:t0 + W], f["r"][:], reads=[f["r"]])
                        P.dma("sp", KKT[et, :, t0:t0 + W], f["kk"][:], reads=[f["kk"]])
                        for d in range(2):
                            ts("dve", f["t1"][:], f[f"a{d}"][:], prm[:, 1, et:et + 1], prm[:, 2, et:et + 1], ALU.mult, ALU.add,
                               [f[f"a{d}"], prm], [f["t1"]])
                            tt("pool", f[f"kd{d}"][:], f["k"][:], f["t1"][:], ALU.mult, [f["k"], f["t1"]], [f[f"kd{d}"]])
                            tt("pool", f[f"b{d}"][:], f["kk"][:], f[f"a{d}"][:], ALU.mult, [f["kk"], f[f"a{d}"]], [f[f"b{d}"]])
                            P.dma("sp", KDT[d][et, :, t0:t0 + W], f[f"kd{d}"][:], reads=[f[f"kd{d}"]])
                            P.dma("sp", BT[d][et, :, t0:t0 + W], f[f"b{d}"][:], reads=[f[f"b{d}"]])
                        if p0sub < 4:
                            continue
                        tt("dve", f["ks"][:], f["kd0"][:], f["kd1"][:], ALU.add, [f["kd0"], f["kd1"]], [f["ks"]])
                        stt("dve", rkb[:], f["r"][:], prm[:, 3, et:et + 1], f["ks"][:], ALU.mult, ALU.mult, [f["r"], prm, f["ks"]], [rkb])
                        if p0step < 6:
                            continue
                        for tt_ in range(2):
                            mm(psb[tt_][:, 0:32], rkb[:, tt_ * 128:(tt_ + 1) * 128], IND[:, et, :], et == 0, et == KT - 1, [rkb, IND], [psb[tt_]])
                    if p0step < 6:
                        continue
                    for tt_ in range(2):
                        cp("act", bon[tt_][:], psb[tt_][:, 0:32], [psb[tt_]], [bon[tt_]])
                        P.dma("sp", BON[t0 + tt_ * 128:t0 + (tt_ + 1) * 128, :], bon[tt_][:], reads=[bon[tt_]])
                psn[0] = 7
            P.barrier()

        def phase_s0(d, NG=4):
            with ExitStack() as ph:
                S = ph.enter_context
                SDT = BF16
                NP = 16
                GP = NP // NG
                NB = GP // 4
                ld = {n: [sb(f"s_{n}{i}", [128, NP, C0], stack=S) for i in range(2)] for n in ("r", "kd", "kk", "b", "v")}
                lwc = [sb(f"s_lw{i}", [C0, D], stack=S) for i in range(2)]
                lwh = sb("s_lwh", [C0, D], BF16, stack=S); lwl = sb("s_lwl", [C0, D], BF16, stack=S)

                def gb(name, shape, dt=F32):
                    return [sb(f"{name}{g}", shape, dt, stack=S) for g in range(NG)]
                vcb = gb("s_vcb", [128, GP, C0], SDT)
                eP = gb("s_eP", [128, GP, C0]); ePx = gb("s_ePx", [128, GP, C0]); eN = gb("s_eN", [128, GP, C0])
                ex = {n: gb(f"s_ex{n}", [128, GP, 128], SDT) for n in ("A", "B", "K", "R")}
                for n in ex:
                    for g in range(NG):
                        memset("pool", ex[n][g][:], 0.0, [ex[n][g]])
                Xs = [gb(f"s_X{i}", [128, GP, 128], SDT) for i in range(2)]
                Ls = [gb(f"s_L{i}", [128, GP, 128], SDT) for i in range(2)]
                Mak = gb("s_Mak", [128, GP, 128], SDT); Mrb = gb("s_Mrb", [128, GP, 128], SDT); Mrk = gb("s_Mrk", [128, GP, 128], SDT)
                BTe = gb("s_BTe", [128, GP, 128], SDT); KTe = gb("s_KTe", [128, GP, 128], SDT)
                ST32 = gb("s_ST32", [128, GP, C0]); STb = gb("s_STb", [128, GP, C0], SDT)
                Y32 = gb("s_Y32", [128, GP, C0]); Yb = gb("s_Yb", [128, GP, C0], SDT)
                oc = [gb(f"s_oc{i}", [128, GP, C0]) for i in range(2)]
                for g in range(NG):
                    memset("pool", ST32[g][:], 0.0, [ST32[g]])
                    memset("pool", STb[g][:], 0.0, [STb[g]])
                nch = T // C0
                order = list(range(nch)) if d == 0 else [3, 2, 1, 0] + list(range(nch - 1, 3, -1))
                tl = C0 - 1 if d == 0 else 0
                GW = GP * C0

                def fview(ap, t0):
                    return ap[:, :, t0:t0 + C0].rearrange("k p t -> p k t")

                def pview(ap, t0, hh, g):
                    return ap[t0:t0 + C0, g * GP * 128:(g + 1) * GP * 128].rearrange("s (pr h v) -> h s pr v", h=2, v=64)[hh]

                def p3(p_):
                    return p_[:, 0:GW].rearrange("p (a t) -> p a t", t=64)

                def p4(p_):
                    return p_[:].rearrange("p (q t) -> p q t", q=4)

                def body(g, L_, t0, b2):
                    gs = slice(g * GP, (g + 1) * GP)
                    pA = ps(); pB = ps()
                    for pp, trib in ((pA, tribI[d]), (pB, tribS[d])):
                        for q in range(GP):
                            pr = g * GP + q
                            o_ = pp[:, q * 64:(q + 1) * 64]
                            mm(o_, lwh[:, pr * 128:(pr + 1) * 128], trib[:], True, False, [lwh, trib], [pp])
                            mm(o_, lwl[:, pr * 128:(pr + 1) * 128], trib[:], False, True, [lwl, trib], [pp])
                    act(eP[g][:], p3(pA), AF.Exp, [pA], [eP[g]])
                    act(eN[g][:], p3(pA), AF.Exp, [pA], [eN[g]], scale=-1.0)
                    act(ePx[g][:], p3(pB), AF.Exp, [pB], [ePx[g]])
                    yield
                    for hh in range(2):
                        psl = slice(hh * 64, (hh + 1) * 64)
                        csl = slice(hh * 64, (hh + 1) * 64)
                        stt("dve", ex["A"][g][psl, :, csl], L_["kk"][psl, gs, :], -1.0, ePx[g][psl, :, :], ALU.mult, ALU.mult, [L_["kk"], ePx[g]], [ex["A"][g]])
                        tt("pool", ex["B"][g][psl, :, csl], L_["b"][psl, gs, :], eN[g][psl, :, :], ALU.mult, [L_["b"], eN[g]], [ex["B"][g]])
                        tt("dve", ex["K"][g][psl, :, csl], L_["kd"][psl, gs, :], eN[g][psl, :, :], ALU.mult, [L_["kd"], eN[g]], [ex["K"][g]])
                        tt("pool", ex["R"][g][psl, :, csl], L_["r"][psl, gs, :], eP[g][psl, :, :], ALU.mult, [L_["r"], eP[g]], [ex["R"][g]])
                    cp("act", vcb[g][:], L_["v"][:, gs, :], [L_["v"]], [vcb[g]])
                    yield
                    X = Xs[0][g]; Lm = Ls[0][g]
                    specs = [(X, "B", "A", mS4[d]), (Lm, "A", "B", mS4[1 - d]), (Mak[g], "K", "A", mS4[d]), (Mrb[g], "B", "R", mI4[d]), (Mrk[g], "K", "R", mI4[d])]
                    for dst, l_, r_, msk in specs:
                        for sbk in range(NB):
                            pp = ps()
                            for q in range(4):
                                pr = sbk * 4 + q
                                mm(pp[:, q * 128:(q + 1) * 128], ex[l_][g][:, pr, :], ex[r_][g][:, pr, :], True, True, [ex[l_][g], ex[r_][g]], [pp])
                            tt("dve", dst[:, sbk * 4:(sbk + 1) * 4, :], p4(pp), msk[:], ALU.mult, [pp, msk], [dst])
                    yield
                    pY = ps()
                    for q in range(GP):
                        o_ = pY[:, q * 64:(q + 1) * 64]
                        mm(o_, ex["A"][g][:, q, :], STb[g][:, q, :], True, False, [ex["A"][g], STb[g]], [pY])
                        mm(o_, Mak[g][:, q, :], vcb[g][:, q, :], False, True, [Mak[g], vcb[g]], [pY])
                    cp("act", Y32[g][:], p3(pY), [pY], [Y32[g]])
                    cp("dve", Yb[g][:], p3(pY), [pY], [Yb[g]])
                    yield
                    cur = 0
                    for lev in range(6):
                        Xc = Xs[cur][g]; Lc = Ls[cur][g]
                        pY = ps()
                        for q in range(GP):
                            mm(pY[:, q * 64:(q + 1) * 64], Xc[:, q, :], Yb[g][:, q, :], True, True, [Xc, Yb[g]], [pY])
                        if lev < 5:
                            Xn = Xs[1 - cur][g]; Ln = Ls[1 - cur][g]
                            pxs = []
                            for sbk in range(NB):
                                pp = ps()
                                for q in range(4):
                                    pr = sbk * 4 + q
                                    mm(pp[:, q * 128:(q + 1) * 128], Lc[:, pr, :], Xc[:, pr, :], True, True, [Lc, Xc], [pp])
                                pxs.append(pp)
                            pls = []
                            if lev < 4:
                                for sbk in range(NB):
                                    pp = ps()
                                    for q in range(4):
                                        pr = sbk * 4 + q
                                        mm(pp[:, q * 128:(q + 1) * 128], Xc[:, pr, :], Lc[:, pr, :], True, True, [Lc, Xc], [pp])
                                    pls.append(pp)
                        tt("dve", Y32[g][:], Y32[g][:], p3(pY), ALU.add, [Y32[g], pY], [Y32[g]])
                        cp("pool", Yb[g][:], Y32[g][:], [Y32[g]], [Yb[g]])
                        if lev < 5:
                            for sbk, pp in enumerate(pxs):
                                cp("act", Xn[:, sbk * 4:(sbk + 1) * 4, :], p4(pp), [pp], [Xn])
                            for sbk, pp in enumerate(pls):
                                cp("act" if sbk % 2 else "dve", Ln[:, sbk * 4:(sbk + 1) * 4, :], p4(pp), [pp], [Ln])
                            cur = 1 - cur
                        yield
                    o_sb = oc[b2][g]
                    pO = ps()
                    for q in range(GP):
                        o_ = pO[:, q * 64:(q + 1) * 64]
                        mm(o_, ex["R"][g][:, q, :], STb[g][:, q, :], True, False, [ex["R"][g], STb[g]], [pO])
                        mm(o_, Mrb[g][:, q, :], Yb[g][:, q, :], False, False, [Mrb[g], Yb[g]], [pO])
                        mm(o_, Mrk[g][:, q, :], vcb[g][:, q, :], False, True, [Mrk[g], vcb[g]], [pO])
                    pts = []
                    for src_, dst in ((ex["B"][g], BTe[g]), (ex["K"][g], KTe[g])):
                        for sbk in range(NB):
                            pp = ps()
                            for q in range(4):
                                pr = sbk * 4 + q
                                mm(pp[:, q * 128:(q + 1) * 128], src_[:, pr, :], identb[:], True, True, [src_, identb], [pp])
                            pts.append((pp, dst, sbk))
                    cp("act", o_sb[:], p3(pO), [pO], [o_sb])
                    for hh in range(2):
                        P.dma("sp", pview(O0[d], t0, hh, g), o_sb[hh * 64:(hh + 1) * 64, :, :], reads=[o_sb])
                    for i_, (pp, dst, sbk) in enumerate(pts):
                        cp("act" if i_ % 2 else "dve", dst[:, sbk * 4:(sbk + 1) * 4, :], p4(pp), [pp], [dst])
                    yield
                    pS = ps()
                    for q in range(GP):
                        o_ = pS[:, q * 64:(q + 1) * 64]
                        mm(o_, BTe[g][:, q, :], Yb[g][:, q, :], True, False, [BTe[g], Yb[g]], [pS])
                        mm(o_, KTe[g][:, q, :], vcb[g][:, q, :], False, True, [KTe[g], vcb[g]], [pS])
                    tt("dve", ST32[g][:], ST32[g][:], p3(pS), ALU.add, [ST32[g], pS], [ST32[g]])
                    tt("pool", ST32[g][:], ST32[g][:], eP[g][:, :, tl:tl + 1].to_broadcast([128, GP, C0]), ALU.mult, [ST32[g], eP[g]], [ST32[g]])
                    cp("act", STb[g][:], ST32[g][:], [ST32[g]], [STb[g]])
                    yield

                def issue_loads(ci):
                    t0 = order[ci] * C0
                    b2 = ci % 2
                    L_ = {n: ld[n][b2] for n in ld}
                    lw_ = lwc[b2]
                    P.dma("sp", L_["r"][:], fview(RT, t0), writes=[L_["r"]])
                    P.dma("sp", L_["kd"][:], fview(KDT[d], t0), writes=[L_["kd"]])
                    P.dma("sp", L_["kk"][:], fview(KKT, t0), writes=[L_["kk"]])
                    P.dma("sp", L_["b"][:], fview(BT[d], t0), writes=[L_["b"]])
                    P.dma("sp", lw_[:], LW[d][t0:t0 + C0, :], writes=[lw_])
                    for hh in range(2):
                        P.dma("sp", L_["v"][hh * 64:(hh + 1) * 64, :, :],
                              V0[t0:t0 + C0, :].rearrange("s (pr h v) -> h s pr v", h=2, v=64)[hh], writes=[L_["v"]])

                issue_loads(0)
                for ci, c in enumerate(order):
                    t0 = c * C0
                    b2 = ci % 2
                    L_ = {n: ld[n][b2] for n in ld}
                    lw_ = lwc[b2]
                    if ci + 1 < len(order):
                        issue_loads(ci + 1)
                    cp("act", lwh[:], lw_[:], [lw_], [lwh])
                    tt("dve", lwl[:], lw_[:], lwh[:], ALU.subtract, [lw_, lwh], [lwl])
                    issue_cast1(1)
                    gens = [body(g, L_, t0, b2) for g in range(NG)]
                    while gens:
                        for gen in list(gens):
                            try:
                                next(gen)
                            except StopIteration:
                                gens.remove(gen)
            P.barrier()

        def phase_o0():
            with ExitStack() as ph:
                S = ph.enter_context
                WoB = sb("WoB", [128, KT, D], BF16, stack=S)
                P.dma("sp", WoB[:], Wb_out0.rearrange("(k p) e -> p k e", p=128), writes=[WoB])
                LNG = sb("LNG", [128, D], stack=S); LNB = sb("LNB", [128, D], stack=S); TG = sb("TG", [128, D], stack=S)
                P.dma("sp", LNG[:], rk_ln_g.partition_broadcast(128), writes=[LNG])
                P.dma("sp", LNB[:], rk_ln_b.partition_broadcast(128), writes=[LNB])
                bufs = {n: [sb(f"o_{n}{i}", [128, D], stack=S) for i in range(1)] * 2 for n in ("of", "ob", "v", "sg", "x")}
                sq = sb("o_sq", [128, D], stack=S)
                ybf = sb("o_ybf", [128, D], BF16, stack=S)
                bo = [sb(f"o_bon{i}", [128, 32], stack=S) for i in range(2)]
                st = [sb(f"o_st{i}", [128, 4, 32], stack=S) for i in range(2)]
                yT = [sb(f"o_yT{i}", [128, KT, 128], BF16, stack=S) for i in range(2)]
                for i in range(NTILE):
                    row = 1 if i < 2 else 0
                    if i == 0 or i == 2:
                        P.dma("sp", TG[:], ADA[0, row, 2, :].partition_broadcast(128), writes=[TG])
                    b2 = i % 2
                    B_ = {n: bufs[n][b2] for n in bufs}
                    rs = slice(i * 128, (i + 1) * 128)

                    def ld_o0(i_):
                        rs_ = slice(i_ * 128, (i_ + 1) * 128)
                        P.dma("sp", bufs["of"][0][:], O0[0][rs_, :], writes=[bufs["of"][0]])
                        P.dma("sp", bufs["ob"][0][:], O0[1][rs_, :], writes=[bufs["ob"][0]])
                        P.dma("sp", bufs["v"][0][:], V0[rs_, :], writes=[bufs["v"][0]])
                        P.dma("sp", bufs["sg"][0][:], SG0[rs_, :], writes=[bufs["sg"][0]])
                        P.dma("sp", bo[i_ % 2][:], BON[rs_, :], writes=[bo[i_ % 2]])
                    if i == 0:
                        ld_o0(0)
                    P.dma("sp", B_["x"][:], xin[rs, :], writes=[B_["x"]])
                    o = B_["of"]; s_ = st[b2]
                    o3 = o[:].rearrange("p (h v) -> p h v", v=64)
                    tt("dve", o[:], o[:], B_["ob"][:], ALU.add, [o, B_["ob"]], [o])
                    red("dve", s_[:, 0, :], o3, [o], [s_])
                    ts("dve", s_[:, 0, :], s_[:, 0, :], -1.0 / 64, None, ALU.mult, None, [s_], [s_])
                    tt("dve", o3, o3, s_[:, 0, :].unsqueeze(2).to_broadcast([128, 32, 64]), ALU.add, [o, s_], [o])
                    tt("pool", sq[:], o[:], o[:], ALU.mult, [o], [sq])
                    red("dve", s_[:, 1, :], sq[:].rearrange("p (h v) -> p h v", v=64), [sq], [s_])
                    ts("dve", s_[:, 1, :], s_[:, 1, :], 1.0 / 64, GN_EPS, ALU.mult, ALU.add, [s_], [s_])
                    act(s_[:, 2, :], s_[:, 1, :], AF.Sqrt, [s_], [s_])
                    recip(s_[:, 3, :], s_[:, 2, :], [s_], [s_])
                    tt("dve", o3, o3, s_[:, 3, :].unsqueeze(2).to_broadcast([128, 32, 64]), ALU.mult, [o, s_], [o])
                    tt("pool", o[:], o[:], LNG[:], ALU.mult, [o, LNG], [o])
                    tt("pool", o[:], o[:], LNB[:], ALU.add, [o, LNB], [o])
                    v_ = B_["v"]
                    v3 = v_[:].rearrange("p (h v) -> p h v", v=64)
                    tt("dve", v3, v3, bo[b2][:].unsqueeze(2).to_broadcast([128, 32, 64]), ALU.mult, [v_, bo[b2]], [v_])
                    tt("pool", o[:], o[:], v_[:], ALU.add, [o, v_], [o])
                    tt("dve", o[:], o[:], B_["sg"][:], ALU.mult, [o, B_["sg"]], [o])
                    yt = yT[b2]
                    cp("act", ybf[:], o[:], [o], [ybf])
                    if i + 1 < NTILE:
                        ld_o0(i + 1)
                    for g in range(4):
                        pp = ps()
                        for q in range(4):
                            kt = g * 4 + q
                            mm(pp[:, q * 128:(q + 1) * 128], ybf[:, kt * 128:(kt + 1) * 128], identb[:], True, True, [ybf, identb], [pp])
                        cp("act", yt[:, g * 4:(g + 1) * 4, :], pp[:].rearrange("p (q t) -> p q t", q=4), [pp], [yt])
                    x_ = B_["x"]
                    for cg in range(4):
                        pp = ps()
                        for kt in range(KT):
                            mm(pp[:], yt[:, kt, :], WoB[:, kt, cg * 512:(cg + 1) * 512], kt == 0, kt == KT - 1, [yt, WoB], [pp])
                        cs_ = slice(cg * 512, (cg + 1) * 512)
                        tt("dve", sq[:, cs_], pp[:], TG[:, cs_], ALU.mult, [pp, TG], [sq])
                        tt("pool", x_[:, cs_], x_[:, cs_], sq[:, cs_], ALU.add, [x_, sq], [x_])
                    P.dma("sp", X1[rs, :], x_[:], reads=[x_])
            P.barrier()

        def phase_p1():
            with ExitStack() as ph:
                S = ph.enter_context
                W = 256
                hT = [sb(f"p1_hT{i}", [128, KT, W], BF16, stack=S) for i in range(2)]
                dl = sb("p1_dl", [128, 16], stack=S)
                P.dma("sp", dl[:], rt_dl.partition_broadcast(128), writes=[dl])
                lg = sb("p1_lg", [128, 6, 16], stack=S)
                act(lg[:, 0, :], dl[:], AF.Exp, [dl], [lg], scale=-1.0)
                act(lg[:, 0, :], lg[:, 0, :], AF.Ln, [lg], [lg], bias=1.0)
                ts("dve", lg[:, 1, :], lg[:, 0, :], 1.0, None, ALU.mult, None, [lg], [lg])
                ts("dve", lg[:, 0, :], lg[:, 1, :], -1.0, None, ALU.mult, None, [lg], [lg])
                ts("dve", lg[:, 2, :], lg[:, 0, :], float(C1), None, ALU.mult, None, [lg], [lg])
                ts("dve", lg[:, 3, :], lg[:, 1, :], math.log(1.0 / 16), None, ALU.add, None, [lg], [lg])
                ts("dve", lg[:, 4, :], lg[:, 1, :], float(C1), math.log(1.0 / 16), ALU.mult, ALU.add, [lg], [lg])
                pos = sb("p1_pos", [128, W], stack=S)
                ones_ = sb("p1_ones", [128, 128], BF16, stack=S)
                memset("pool", ones_[:], 1.0, [ones_])
                pp_ = ps()
                mm(pp_[:, 0:128], ones_[:], triUsb[:], True, True, [ones_, triUsb], [pp_])
                cp("dve", pos[:, 0:128], pp_[:, 0:128], [pp_], [pos])
                cp("dve", pos[:, 128:256], pp_[:, 0:128], [pp_], [pos])
                DQ = [[sb(f"DQ{d}{h}", [128, W], stack=S) for h in range(8)] for d in range(2)]
                DK = [[sb(f"DK{d}{h}", [128, W], stack=S) for h in range(8)] for d in range(2)]
                for h in range(8):
                    c0 = h; c1 = 8 + h
                    act(DQ[0][h][:], pos[:], AF.Exp, [pos, lg], [DQ[0][h]], scale=lg[:, 0, c0:c0 + 1], bias=lg[:, 0, c0:c0 + 1])
                    act(DK[0][h][:], pos[:], AF.Exp, [pos, lg], [DK[0][h]], scale=lg[:, 1, c0:c0 + 1], bias=lg[:, 3, c0:c0 + 1])
                    act(DQ[1][h][:], pos[:], AF.Exp, [pos, lg], [DQ[1][h]], scale=lg[:, 1, c1:c1 + 1], bias=lg[:, 2, c1:c1 + 1])
                    act(DK[1][h][:], pos[:], AF.Exp, [pos, lg], [DK[1][h]], scale=lg[:, 0, c1:c1 + 1], bias=lg[:, 4, c1:c1 + 1])
                cs_t = sb("p1_cos", [128, W], stack=S); sn_t = sb("p1_sin", [128, W], stack=S)
                wrk = [sb(f"p1_wrk{i}", [128, KT, 128], BF16, stack=S) for i in range(4)]
                wpc = [sb(f"p1_wpc{i}", [128, KT, 512], BF16, stack=S) for i in range(2)]
                xx = [[sb(f"p1_x{i}{j}", [128, W], stack=S) for j in range(2)] for i in range(2)]
                tmp = [sb(f"p1_t{i}", [128, W], stack=S) for i in range(4)]
                yy = [sb(f"p1_y{i}", [128, W], stack=S) for i in range(2)]
                ob = [sb(f"p1_ob{i}", [128, W], BF16, stack=S) for i in range(4)]
                vst = [sb(f"p1_vst{i}", [128, 512], BF16, stack=S) for i in range(2)]
                gst = [sb(f"p1_gst{i}", [128, 512], stack=S) for i in range(2)]
                nwin = T // W
                cnt_ = [0]
                def ld_h(w_):
                    P.dma("sp", hT[w_ % 2][:], HT1[:, :, w_ * W:(w_ + 1) * W].rearrange("k p t -> p k t"), writes=[hT[w_ % 2]])

                def ld_qk(pi):
                    for half in range(2):
                        P.dma("sp", wrk[(pi * 2 + half) % 4][:], Wb_qk[pi * 2 + half], writes=[wrk[(pi * 2 + half) % 4]])

                def ld_vg1(idx):
                    P.dma("sp", wpc[idx % 2][:], Wb_vg[idx], writes=[wpc[idx % 2]])

                ld_h(0)
                for w in range(nwin):
                    t0 = w * W
                    isctx = (w == 0)
                    h_ = hT[w % 2]
                    if not isctx:
                        P.dma("sp", cs_t[:], ropec[:, t0 - NCTX:t0 - NCTX + W], writes=[cs_t])
                        P.dma("sp", sn_t[:], ropes[:, t0 - NCTX:t0 - NCTX + W], writes=[sn_t])
                    ld_qk(0)
                    for qk in range(2):
                        dsts = QT if qk == 0 else KTT
                        tabs = DQ if qk == 0 else DK
                        for h in range(8):
                            i2 = cnt_[0] % 2; cnt_[0] += 1
                            pi_ = qk * 8 + h
                            if pi_ + 1 < 16:
                                ld_qk(pi_ + 1)
                            else:
                                ld_vg1(0)
                                if w + 1 < nwin:
                                    ld_h(w + 1)
                            for half in range(2):
                                et = h * 2 + half
                                wr = wrk[(pi_ * 2 + half) % 4]
                                pp = ps()
                                for kt in range(KT):
                                    mm(pp[:, 0:W], wr[:, kt, :], h_[:, kt, :], kt == 0, kt == KT - 1, [wr, h_], [pp])
                                cp("act", xx[i2][half][:], pp[:, 0:W], [pp], [xx[i2][half]])
                            x1 = xx[i2][0]; x2 = xx[i2][1]
                            if isctx:
                                y1, y2 = x1, x2
                            else:
                                y1, y2 = yy[0], yy[1]
                                tt("dve", tmp[0][:], x1[:], cs_t[:], ALU.mult, [x1, cs_t], [tmp[0]])
                                tt("pool", tmp[1][:], x2[:], sn_t[:], ALU.mult, [x2, sn_t], [tmp[1]])
                                tt("dve", y1[:], tmp[0][:], tmp[1][:], ALU.subtract, [tmp[0], tmp[1]], [y1])
                                tt("pool", tmp[2][:], x1[:], sn_t[:], ALU.mult, [x1, sn_t], [tmp[2]])
                                tt("dve", tmp[3][:], x2[:], cs_t[:], ALU.mult, [x2, cs_t], [tmp[3]])
                                tt("pool", y2[:], tmp[2][:], tmp[3][:], ALU.add, [tmp[2], tmp[3]], [y2])
                            for d in range(2):
                                for half, y_ in ((0, y1), (1, y2)):
                                    o_ = ob[d * 2 + half]
                                    tt("dve" if half == 0 else "pool", o_[:], y_[:], tabs[d][h][:], ALU.mult, [y_, tabs[d][h]], [o_])
                                    P.dma("sp", dsts[d][h * 2 + half, :, t0:t0 + W], o_[:], reads=[o_])
                    for vg in range(2):
                        for cg in range(8):
                            idx_ = vg * 8 + cg
                            wp = wpc[idx_ % 2]
                            if idx_ + 1 < 16:
                                ld_vg1(idx_ + 1)
                            for tt_ in range(2):
                                pp = ps()
                                for kt in range(KT):
                                    mm(pp[:], h_[:, kt, tt_ * 128:(tt_ + 1) * 128], wp[:, kt, :], kt == 0, kt == KT - 1, [h_, wp], [pp])
                                rs = slice(t0 + tt_ * 128, t0 + (tt_ + 1) * 128)
                                if vg == 0:
                                    cp("act", vst[tt_][:], pp[:], [pp], [vst[tt_]])
                                    P.dma("sp", V1[rs, cg * 512:(cg + 1) * 512], vst[tt_][:], reads=[vst[tt_]])
                                else:
                                    act(gst[tt_][:], pp[:], AF.Silu, [pp], [gst[tt_]])
                                    P.dma("sp", SG1[rs, cg * 512:(cg + 1) * 512], gst[tt_][:], reads=[gst[tt_]])
            P.barrier()

        def phase_s1(d):
            with ExitStack() as ph:
                S = ph.enter_context
                dl = sb("s1_dl", [128, 16], stack=S)
                P.dma("sp", dl[:], rt_dl.partition_broadcast(128), writes=[dl])
                gc = sb("s1_gc", [128, 16], stack=S)
                act(gc[:], dl[:], AF.Exp, [dl], [gc], scale=-1.0)
                act(gc[:], gc[:], AF.Ln, [gc], [gc], bias=1.0)
                act(gc[:], gc[:], AF.Exp, [gc], [gc], scale=-float(C1))
                qt = [sb(f"s1_q{i}", [128, KT, C1], BF16, stack=S) for i in range(2)]
                kt_ = [sb(f"s1_k{i}", [128, KT, C1], BF16, stack=S) for i in range(2)]
                vc = [sb(f"s1_v{i}", [128, 2 * D], BF16, stack=S) for i in range(2)]
                R32 = [sb(f"s1_R32_{h}", [128, 2, 512], stack=S) for h in range(8)]
                Rb = [sb(f"s1_Rb_{h}", [128, 2, 512], BF16, stack=S) for h in range(8)]
                for h_ in range(8):
                    memset("pool", R32[h_][:], 0.0, [R32[h_]]); memset("pool", Rb[h_][:], 0.0, [Rb[h_]])
                Sb = [sb(f"s1_S{i}", [128, 128], BF16, stack=S) for i in range(8)]
                ktok = [sb(f"s1_kt{i}", [128, 256], BF16, stack=S) for i in range(8)]
                ost = [sb(f"s1_o{i}", [128, 512], stack=S) for i in range(8)]
                msk = sb("s1_msk", [128, 128], stack=S)
                cp("dve", msk[:], (triU if d == 0 else triLs)[:], [triU, triLs], [msk])
                nch = T // C1
                order = list(range(nch)) if d == 0 else [1, 0] + list(range(nch - 1, 1, -1))

                def hbody(h, q_, k_, v_, t0):
                    pS = ps()
                    for half in range(2):
                        mm(pS[:, 0:128], k_[:, h * 2 + half, :], q_[:, h * 2 + half, :], half == 0, half == 1, [k_, q_], [pS])
                    tt("dve", Sb[h][:], pS[:, 0:128], msk[:], ALU.mult, [pS, msk], [Sb[h]])
                    pT = ps()
                    for half in range(2):
                        mm(pT[:, half * 128:(half + 1) * 128], k_[:, h * 2 + half, :], identb[:], True, True, [k_, identb], [pT])
                    cp("act", ktok[h][:], pT[:, 0:256], [pT], [ktok[h]])
                    yield
                    pO = ps()
                    mm(pO[:], Sb[h][:], v_[:, h * 512:(h + 1) * 512], True, False, [Sb[h], v_], [pO])
                    for half in range(2):
                        mm(pO[:], q_[:, h * 2 + half, :], Rb[h][:, half, :], False, half == 1, [q_, Rb[h]], [pO])
                    cp("act", ost[h][:], pO[:], [pO], [ost[h]])
                    P.dma("sp", O1[d][t0:t0 + C1, h * 512:(h + 1) * 512], ost[h][:], reads=[ost[h]])
                    yield
                    for half in range(2):
                        pR = ps()
                        mm(pR[:], ktok[h][:, half * 128:(half + 1) * 128], v_[:, h * 512:(h + 1) * 512], True, True, [ktok[h], v_], [pR])
                        tt("dve", R32[h][:, half, :], R32[h][:, half, :], pR[:], ALU.add, [R32[h], pR], [R32[h]])
                        ts("pool", R32[h][:, half, :], R32[h][:, half, :], gc[:, d * 8 + h:d * 8 + h + 1], None, ALU.mult, None, [R32[h], gc], [R32[h]])
                        cp("act", Rb[h][:, half, :], R32[h][:, half, :], [R32[h]], [Rb[h]])
                    yield

                def issue_loads1(ci):
                    t0 = order[ci] * C1
                    b2 = ci % 2
                    P.dma("sp", qt[b2][:], QT[d][:, :, t0:t0 + C1].rearrange("k p t -> p k t"), writes=[qt[b2]])
                    P.dma("sp", kt_[b2][:], KTT[d][:, :, t0:t0 + C1].rearrange("k p t -> p k t"), writes=[kt_[b2]])
                    P.dma("sp", vc[b2][:], V1[t0:t0 + C1, :], writes=[vc[b2]])

                issue_loads1(0)
                for ci, c in enumerate(order):
                    t0 = c * C1
                    b2 = ci % 2
                    q_ = qt[b2]; k_ = kt_[b2]; v_ = vc[b2]
                    if ci + 1 < len(order):
                        issue_loads1(ci + 1)
                    gens = [hbody(h, q_, k_, v_, t0) for h in range(8)]
                    while gens:
                        for gen in list(gens):
                            try:
                                next(gen)
                            except StopIteration:
                                gens.remove(gen)
            P.barrier()

        def phase_o1():
            with ExitStack() as ph:
                S = ph.enter_context
                GNG = sb("o1_gng", [128, 2 * D], stack=S); TG = sb("o1_TG", [128, D], stack=S); FG = sb("o1_FG", [128, D], stack=S)
                P.dma("sp", GNG[:], rt_gn_g.partition_broadcast(128), writes=[GNG])
                P.dma("sp", TG[:], ADA[1, 0, 2, :].partition_broadcast(128), writes=[TG])
                P.dma("sp", FG[:], final_g.partition_broadcast(128), writes=[FG])
                of = [sb(f"o1_of{i}", [128, 2 * D], stack=S) for i in range(1)] * 2
                ob = sb("o1_ob", [128, 2 * D], stack=S)
                sg = sb("o1_sg", [128, 2 * D], stack=S)
                x_b = [sb(f"o1_x{i}", [128, D], stack=S) for i in range(2)]
                st = [sb(f"o1_st{i}", [128, 4, 8], stack=S) for i in range(2)]
                yT = sb("o1_yT", [128, 32, 256], BF16, stack=S)
                wp = [sb(f"o1_wp{i}", [128, 32, 256], BF16, stack=S) for i in range(2)]
                junk = sb("o1_junk", [128, D], stack=S)
                ybf1 = sb("o1_ybf", [128, 2 * D], BF16, stack=S)
                wi = [0]
                for pi in range((NTILE - 2) // 2):
                    tiles = [2 + 2 * pi, 3 + 2 * pi]
                    for j, i in enumerate(tiles):
                        rs = slice(i * 128, (i + 1) * 128)
                        o = of[j]; x_ = x_b[j]; s_ = st[j]
                        P.dma("sp", o[:], O1[0][rs, :], writes=[o])
                        P.dma("sp", ob[:], O1[1][rs, :], writes=[ob])
                        P.dma("sp", sg[:], SG1[rs, :], writes=[sg])
                        P.dma("sp", x_[:], X1[rs, :], writes=[x_])
                        tt("dve", o[:], o[:], ob[:], ALU.add, [o, ob], [o])
                        tt("pool", ob[:], o[:], o[:], ALU.mult, [o], [ob])
                        red("dve", s_[:, 0, :], ob[:].rearrange("p (h v) -> p h v", v=512), [ob], [s_])
                        ts("dve", s_[:, 1, :], s_[:, 0, :], 1.0 / 512, EPS, ALU.mult, ALU.add, [s_], [s_])
                        act(s_[:, 2, :], s_[:, 1, :], AF.Sqrt, [s_], [s_])
                        recip(s_[:, 3, :], s_[:, 2, :], [s_], [s_])
                        o3 = o[:].rearrange("p (h v) -> p h v", v=512)
                        tt("dve", o3, o3, s_[:, 3, :].unsqueeze(2).to_broadcast([128, 8, 512]), ALU.mult, [o, s_], [o])
                        tt("pool", o[:], o[:], GNG[:], ALU.mult, [o, GNG], [o])
                        tt("dve", o[:], o[:], sg[:], ALU.mult, [o, sg], [o])
                        cp("act", ybf1[:], o[:], [o], [ybf1])
                        for g in range(8):
                            pp = ps()
                            for q in range(4):
                                kt = g * 4 + q
                                mm(pp[:, q * 128:(q + 1) * 128], ybf1[:, kt * 128:(kt + 1) * 128], identb[:], True, True, [ybf1, identb], [pp])
                            cp("act" if g % 2 else "dve", yT[:, g * 4:(g + 1) * 4, j * 128:(j + 1) * 128], pp[:].rearrange("p (q t) -> p q t", q=4), [pp], [yT])
                    for cg in range(8):
                        w_ = wp[wi[0] % 2]; wi[0] += 1
                        P.dma("sp", w_[:], Wb_o1[cg], writes=[w_])
                        cs_ = slice(cg * 256, (cg + 1) * 256)
                        for j in range(2):
                            x_ = x_b[j]
                            pp = ps()
                            for kt in range(32):
                                mm(pp[:, 0:256], yT[:, kt, j * 128:(j + 1) * 128], w_[:, kt, :], kt == 0, kt == 31, [yT, w_], [pp])
                            tt("dve", junk[:, cs_], pp[:, 0:256], TG[:, cs_], ALU.mult, [pp, TG], [junk])
                            tt("pool", x_[:, cs_], x_[:, cs_], junk[:, cs_], ALU.add, [x_, junk], [x_])
                    for j, i in enumerate(tiles):
                        x_ = x_b[j]; s_ = st[j]
                        act(junk[:], x_[:], AF.Square, [x_], [junk, s_], accum=s_[:, 0, 0:1])
                        ts("dve", s_[:, 0, 1:2], s_[:, 0, 0:1], 1.0 / D, EPS, ALU.mult, ALU.add, [s_], [s_])
                        act(s_[:, 0, 2:3], s_[:, 0, 1:2], AF.Sqrt, [s_], [s_])
                        recip(s_[:, 0, 3:4], s_[:, 0, 2:3], [s_], [s_])
                        stt("dve", x_[:], x_[:], s_[:, 0, 3:4], FG[:], ALU.mult, ALU.mult, [x_, s_, FG], [x_])
                        P.dma("sp", yout[(i - 2) * 128:(i - 1) * 128, :], x_[:], reads=[x_], is_output=True)

        P.barrier()
        stages = [
            ("ada0", lambda: adaln(0)),
            ("h0", lambda: phase_h(0, xin, HT0, F32)),
            ("p0", phase_p0),
            ("s0f", lambda: phase_s0(0)),
            ("s0b", lambda: phase_s0(1)),
            ("o0", lambda: (issue_cast1(100), phase_o0())),
            ("ada1", lambda: adaln(1)),
            ("h1", lambda: phase_h(1, X1, HT1, BF16)),
            ("p1", phase_p1),
            ("s1f", lambda: phase_s1(0)),
            ("s1b", lambda: phase_s1(1)),
            ("o1", phase_o1),
        ]
        for name, fn in stages:
            fn()
            if STOP_AFTER == name:
                break
        P.emit(E)
    return nc


def rope_tables():
    t = np.arange(NLAT)
    row = (t // 64).astype(np.float32)
    col = (t % 64).astype(np.float32)
    nf = 64
    inv = (10000.0 ** (-np.arange(nf, dtype=np.float32) / nf)).astype(np.float32)
    ang = np.concatenate([row[:, None] * inv, col[:, None] * inv], axis=-1).astype(np.float32)
    return np.ascontiguousarray(np.cos(ang).T.astype(np.float32)), np.ascontiguousarray(np.sin(ang).T.astype(np.float32))


def make_in_maps(x, c, ctx, c_ctx, ada_w, ada_b, norm_g, rk_mix, rk_w_in, rk_w0, rk_w1, rk_w2, rk_a0, rk_a1, rk_a2,
                 rk_k_k, rk_k_a, rk_r_k, rk_ln_g, rk_ln_b, rk_w_out, rt_w_in, rt_decay_logit, rt_gn_g, rt_w_out, final_g):
    f = lambda a: np.ascontiguousarray(np.asarray(a, dtype=np.float32))
    rc, rs_ = rope_tables()
    shared = dict(ada_w=f(ada_w), ada_b=f(ada_b), norm_g=f(norm_g), rk_mix=f(rk_mix)[0], rk_w_in=f(rk_w_in)[0],
                  rk_w0=f(rk_w0)[0], rk_w1=f(rk_w1)[0], rk_w2=f(rk_w2)[0], rk_a0=f(rk_a0)[0], rk_a1=f(rk_a1)[0],
                  rk_a2=f(rk_a2)[0], rk_k_k=f(rk_k_k)[0], rk_k_a=f(rk_k_a)[0], rk_r_k=f(rk_r_k)[0].reshape(-1),
                  rk_ln_g=f(rk_ln_g)[0], rk_ln_b=f(rk_ln_b)[0], rk_w_out=f(rk_w_out)[0], rt_w_in=f(rt_w_in)[0],
                  rt_dl=f(rt_decay_logit)[0].reshape(-1), rt_gn_g=f(rt_gn_g)[0], rt_w_out=f(rt_w_out)[0],
                  final_g=f(final_g), ropec=rc, ropes=rs_)
    maps = []
    for core in range(8):
        b = core % 4
        m = dict(shared)
        m["xin"] = np.ascontiguousarray(np.concatenate([f(ctx)[b], f(x)[b]], axis=0))
        m["cvec"] = np.ascontiguousarray(np.stack([f(c)[b], f(c_ctx)], axis=0))
        maps.append(m)
    return maps


def kernel(**inputs):
    maps = make_in_maps(**inputs)[:NCORES]
    nc = build()
    res = run_bass_kernel_spmd(nc, maps, core_ids=list(range(NCORES)))
    out = np.stack([np.asarray(res.results[b]["yout"], dtype=np.float32) for b in range(4)], axis=0)
    return out
```
